# Optimizing a Trainium2 kernel written in Bass

```python
import jax, jax.numpy as jnp
from jax import lax
import numpy as np

D_MODEL = 2048
BATCH = 1
SEQ = 8192
DEPTH = 4
DEC_BATCH = 4
DEC_SEQ = 4096
PAST_LEN = 128

N_MIXERS = 2
N_ATT_LAYERS = (DEPTH + 1) // 2
N_RWKV_LAYERS = DEPTH // 2
N_VRES_LAYERS = max(N_RWKV_LAYERS - 1, 0)

ATT_HEAD_DIM = 128
ATT_HEADS = D_MODEL // ATT_HEAD_DIM
ATT_GROUPS = ((128, 1), (512, 4), (2048, 16))
N_ATT_GROUPS = len(ATT_GROUPS)
ATT_WIDTH = ATT_HEADS * ATT_HEAD_DIM
ATT_QKV_COLS = N_ATT_GROUPS * 3 * ATT_WIDTH
ATT_IN_COLS = ATT_QKV_COLS + ATT_WIDTH
ROPE_THETA = 10000.0

RWKV_HEAD_DIM = 64
RWKV_HEADS = D_MODEL // RWKV_HEAD_DIM
DECAY_LORA = 96
ICLR_LORA = 96
VRES_LORA = 64
N_SHIFT_TARGETS = 6
N_DIRS = 2

RMS_EPS = 1e-6
GN_EPS = 64e-5
NEG_INF = -1e30

kernel_name = "hybrid_dilated_attn_rwkv7_bidir_encoder"


def rmsnorm(x, g):
    xf = x.astype(jnp.float32)
    y = xf * lax.rsqrt(jnp.mean(xf * xf, axis=-1, keepdims=True) + RMS_EPS)
    return (y * g.astype(jnp.float32)).astype(x.dtype)


def apply_rope(t):
    S = t.shape[1]
    half = ATT_HEAD_DIM // 2
    inv_freq = 1.0 / (ROPE_THETA ** (jnp.arange(half, dtype=jnp.float32) * 2.0 / ATT_HEAD_DIM))
    ang = jnp.arange(S, dtype=jnp.float32)[:, None] * inv_freq[None, :]
    cos = jnp.cos(ang)[:, None, None, :]
    sin = jnp.sin(ang)[:, None, None, :]
    tf = t.astype(jnp.float32)
    t1, t2 = tf[..., :half], tf[..., half:]
    return jnp.concatenate([t1 * cos - t2 * sin, t2 * cos + t1 * sin], axis=-1).astype(t.dtype)


def banded_attention(q, k, v, half):
    N, L, H, dh = q.shape
    blk = half
    nb = -(-L // blk)
    Lp = nb * blk
    qb = jnp.pad(q, ((0, 0), (0, Lp - L), (0, 0), (0, 0))).reshape(N, nb, blk, H, dh)
    pad_kv = ((0, 0), (blk, Lp - L + blk), (0, 0), (0, 0))

    def bands(t):
        tb = jnp.pad(t, pad_kv).reshape(N, nb + 2, blk, H, dh)
        return jnp.concatenate([tb[:, :-2], tb[:, 1:-1], tb[:, 2:]], axis=2)

    kb, vb = bands(k), bands(v)
    scores = jnp.einsum('nbqhd,nbkhd->nbhqk', qb, kb,
                        preferred_element_type=jnp.float32) * (dh ** -0.5)
    i = jnp.arange(blk)[:, None]
    j = jnp.arange(3 * blk)[None, :]
    rel_ok = (j - i >= 0) & (j - i <= 2 * blk)
    key_pos = jnp.arange(nb)[:, None] * blk - blk + jnp.arange(3 * blk)[None, :]
    key_ok = (key_pos >= 0) & (key_pos < L)
    mask = rel_ok[None, :, :] & key_ok[:, None, :]
    scores = jnp.where(mask[None, :, None], scores, NEG_INF)
    lse = jax.nn.logsumexp(scores, axis=-1)
    p = jnp.exp(scores - lse[..., None])
    o = jnp.einsum('nbhqk,nbkhd->nbqhd', p, vb.astype(jnp.float32))
    o = o.reshape(N, Lp, H, dh)[:, :L]
    lse = jnp.transpose(lse, (0, 1, 3, 2)).reshape(N, Lp, H)[:, :L]
    return o, lse


def dilated_group_attention(q, k, v, window, dilation):
    B, S, H, dh = q.shape
    L = S // dilation
    half = (window // 2) // dilation

    def to_sub(t):
        return jnp.transpose(t.reshape(B, L, dilation, H, dh), (0, 2, 1, 3, 4)).reshape(B * dilation, L, H, dh)

    o, lse = banded_attention(to_sub(q), to_sub(k), to_sub(v), half)
    o = jnp.transpose(o.reshape(B, dilation, L, H, dh), (0, 2, 1, 3, 4)).reshape(B, S, H, dh)
    lse = jnp.transpose(lse.reshape(B, dilation, L, H), (0, 2, 1, 3)).reshape(B, S, H)
    return o, lse


def attention_mixer(h, w_in, w_out):
    B, S, _ = h.shape
    proj = h @ w_in
    qkv = proj[..., :ATT_QKV_COLS].reshape(B, S, N_ATT_GROUPS, 3, ATT_HEADS, ATT_HEAD_DIM)
    z = proj[..., ATT_QKV_COLS:]
    q = apply_rope(qkv[:, :, :, 0])
    k = apply_rope(qkv[:, :, :, 1])
    v = qkv[:, :, :, 2]
    outs, lses = [], []
    for g, (window, dilation) in enumerate(ATT_GROUPS):
        o, l = dilated_group_attention(q[:, :, g], k[:, :, g], v[:, :, g], window, dilation)
        outs.append(o)
        lses.append(l)
    wts = jax.nn.softmax(jnp.stack(lses, axis=0), axis=0)
    o = jnp.sum(wts[..., None] * jnp.stack(outs, axis=0), axis=0)
    y = o.reshape(B, S, ATT_WIDTH).astype(h.dtype) * jax.nn.silu(z)
    return y @ w_out


def heads(t):
    return t.reshape(t.shape[:-1] + (RWKV_HEADS, RWKV_HEAD_DIM))


def rwkv_step(state, inp):
    r, w, k, v, kk, kka = inp
    sa = jnp.einsum('bhvk,bhk->bhv', state, kk)
    state = state * w[:, :, None, :] - sa[..., None] * kka[:, :, None, :] + v[..., None] * k[:, :, None, :]
    y = jnp.einsum('bhvk,bhk->bhv', state, r)
    return state, y


def rwkv_scan(r, w, k, v, kk, kka, reverse):
    B, S, H, N = r.shape
    tm = lambda t: jnp.swapaxes(t, 0, 1)
    state0 = jnp.zeros((B, H, N, N), jnp.float32)
    _, y = lax.scan(rwkv_step, state0, (tm(r), tm(w), tm(k), tm(v), tm(kk), tm(kka)), reverse=reverse)
    return jnp.swapaxes(y, 0, 1)


def rwkv_mixer(h, v_first, vres, mu_prev, mu_next, w_in, w0, w1, w2, a0, a1, a2,
               k_k, k_a, r_k, gn_g, gn_b, w_out):
    B, S, D = h.shape
    f32 = jnp.float32
    x_prev = jnp.pad(h[:, :-1], ((0, 0), (1, 0), (0, 0)))
    x_next = jnp.pad(h[:, 1:], ((0, 0), (0, 1), (0, 0)))
    xs = (h[None] + (x_prev - h)[None] * mu_prev[:, None, None, :]
          + (x_next - h)[None] * mu_next[:, None, None, :])
    rkvz = jnp.einsum('gbsd,dge->gbse', jnp.stack([xs[0], xs[2], xs[3], xs[5]]),
                      w_in.reshape(D, 4, D))
    r, k, v, z = rkvz[0], rkvz[1], rkvz[2], rkvz[3]
    xw, xv, xa = xs[1], xs[3], xs[4]
    wlog = w0[:, None, None, :] + jnp.einsum(
        'cbsr,crd->cbsd', jnp.tanh(jnp.einsum('bsd,cdr->cbsr', xw, w1)), w2)
    decay = jnp.exp(-jnp.exp(-jax.nn.softplus(-wlog.astype(f32)) - 0.5))
    a = jax.nn.sigmoid((a0[:, None, None, :] + jnp.einsum(
        'cbsr,crd->cbsd', jnp.einsum('bsd,cdr->cbsr', xa, a1), a2)).astype(f32))
    if vres is None:
        v_first = v
    else:
        v0, v1, v2 = vres
        v = v + (v_first - v) * jax.nn.sigmoid(v0 + (xv @ v1) @ v2)
    rf, kf, vf = r.astype(f32), k.astype(f32), v.astype(f32)
    kk = heads(kf * k_k)
    kk = kk / jnp.maximum(jnp.sqrt(jnp.sum(kk * kk, axis=-1, keepdims=True)), 1e-12)
    k_dir = kf[None] * (1.0 + (a - 1.0) * k_a)
    rh, vh = heads(rf), heads(vf)
    kdh, ah = heads(k_dir), heads(a)
    y_f = rwkv_scan(rh, heads(decay[0]), kdh[0], vh, kk, kk * ah[0], reverse=False)
    y_b = rwkv_scan(rh, heads(decay[1]), kdh[1], vh, kk, kk * ah[1], reverse=True)
    y = y_f + y_b
    mu = jnp.mean(y, axis=-1, keepdims=True)
    var = jnp.mean(jnp.square(y - mu), axis=-1, keepdims=True)
    y = ((y - mu) * lax.rsqrt(var + GN_EPS)).reshape(B, S, D) * gn_g + gn_b
    bonus = jnp.sum(rh[None] * kdh * r_k[:, None, None], axis=(0, 4))[..., None] * vh
    out = (y + bonus.reshape(B, S, D)).astype(h.dtype) * jax.nn.silu(z)
    return out @ w_out, v_first


def trunk(x, norm_pre, norm_post, att_w_in, att_w_out, rwkv_mu_prev, rwkv_mu_next, rwkv_w_in,
          rwkv_w0, rwkv_w1, rwkv_w2, rwkv_a0, rwkv_a1, rwkv_a2, rwkv_v0, rwkv_v1, rwkv_v2,
          rwkv_k_k, rwkv_k_a, rwkv_r_k, rwkv_gn_g, rwkv_gn_b, rwkv_w_out):
    v_first = None
    for layer in range(DEPTH):
        h = rmsnorm(x, norm_pre[layer])
        j = layer // N_MIXERS
        if layer % N_MIXERS == 0:
            out = attention_mixer(h, att_w_in[j], att_w_out[j])
        else:
            vres = None if j == 0 else (rwkv_v0[j - 1], rwkv_v1[j - 1], rwkv_v2[j - 1])
            out, v_first = rwkv_mixer(h, v_first, vres, rwkv_mu_prev[j], rwkv_mu_next[j], rwkv_w_in[j],
                                      rwkv_w0[j], rwkv_w1[j], rwkv_w2[j], rwkv_a0[j], rwkv_a1[j], rwkv_a2[j],
                                      rwkv_k_k[j], rwkv_k_a[j], rwkv_r_k[j], rwkv_gn_g[j], rwkv_gn_b[j],
                                      rwkv_w_out[j])
        x = x + rmsnorm(out, norm_post[layer])
    return x


def setup_inputs(seed: int = 0) -> dict:
    key = jax.random.key(seed)
    ks = jax.random.split(key, 24)
    f32 = jnp.float32
    D, H, N = D_MODEL, RWKV_HEADS, RWKV_HEAD_DIM
    NA, NR, NV = N_ATT_LAYERS, N_RWKV_LAYERS, N_VRES_LAYERS
    nrm = lambda k, shape, scale: scale * jax.random.normal(k, shape, f32)
    uni = lambda k, shape, lo, hi: jax.random.uniform(k, shape, f32, lo, hi)
    return {
        "x_prompt": nrm(ks[0], (BATCH, SEQ, D), 1.0),
        "x_sample": nrm(ks[1], (DEC_BATCH, DEC_SEQ, D), 1.0),
        "norm_pre": 1.0 + nrm(ks[2], (DEPTH, D), 0.02),
        "norm_post": 1.0 + nrm(ks[3], (DEPTH, D), 0.02),
        "att_w_in": nrm(ks[4], (NA, D, ATT_IN_COLS), D ** -0.5),
        "att_w_out": nrm(ks[5], (NA, ATT_WIDTH, D), ATT_WIDTH ** -0.5),
        "rwkv_mu_prev": uni(ks[6], (NR, N_SHIFT_TARGETS, D), 0.0, 0.5),
        "rwkv_mu_next": uni(ks[7], (NR, N_SHIFT_TARGETS, D), 0.0, 0.5),
        "rwkv_w_in": nrm(ks[8], (NR, D, 4 * D), D ** -0.5),
        "rwkv_w0": uni(ks[9], (NR, N_DIRS, D), -4.0, 0.0),
        "rwkv_w1": nrm(ks[10], (NR, N_DIRS, D, DECAY_LORA), D ** -0.5),
        "rwkv_w2": nrm(ks[11], (NR, N_DIRS, DECAY_LORA, D), 0.5 * DECAY_LORA ** -0.5),
        "rwkv_a0": nrm(ks[12], (NR, N_DIRS, D), 0.5),
        "rwkv_a1": nrm(ks[13], (NR, N_DIRS, D, ICLR_LORA), D ** -0.5),
        "rwkv_a2": nrm(ks[14], (NR, N_DIRS, ICLR_LORA, D), 0.5 * ICLR_LORA ** -0.5),
        "rwkv_v0": nrm(ks[15], (NV, D), 0.5),
        "rwkv_v1": nrm(ks[16], (NV, D, VRES_LORA), D ** -0.5),
        "rwkv_v2": nrm(ks[17], (NV, VRES_LORA, D), 0.5 * VRES_LORA ** -0.5),
        "rwkv_k_k": 0.85 + nrm(ks[18], (NR, D), 0.1),
        "rwkv_k_a": 1.0 + nrm(ks[19], (NR, D), 0.1),
        "rwkv_r_k": nrm(ks[20], (NR, N_DIRS, H, N), 0.1),
        "rwkv_gn_g": 1.0 + nrm(ks[21], (NR, D), 0.02),
        "rwkv_gn_b": nrm(ks[22], (NR, D), 0.02),
        "rwkv_w_out": nrm(ks[23], (NR, D, D), D ** -0.5),
    }


def reference(x_prompt, x_sample, norm_pre, norm_post, att_w_in, att_w_out, rwkv_mu_prev, rwkv_mu_next,
              rwkv_w_in, rwkv_w0, rwkv_w1, rwkv_w2, rwkv_a0, rwkv_a1, rwkv_a2, rwkv_v0, rwkv_v1, rwkv_v2,
              rwkv_k_k, rwkv_k_a, rwkv_r_k, rwkv_gn_g, rwkv_gn_b, rwkv_w_out):
    y_prompt = trunk(x_prompt, norm_pre, norm_post, att_w_in, att_w_out, rwkv_mu_prev, rwkv_mu_next,
                     rwkv_w_in, rwkv_w0, rwkv_w1, rwkv_w2, rwkv_a0, rwkv_a1, rwkv_a2, rwkv_v0, rwkv_v1,
                     rwkv_v2, rwkv_k_k, rwkv_k_a, rwkv_r_k, rwkv_gn_g, rwkv_gn_b, rwkv_w_out)
    y_sample = trunk(x_sample, norm_pre, norm_post, att_w_in, att_w_out, rwkv_mu_prev, rwkv_mu_next,
                     rwkv_w_in, rwkv_w0, rwkv_w1, rwkv_w2, rwkv_a0, rwkv_a1, rwkv_a2, rwkv_v0, rwkv_v1,
                     rwkv_v2, rwkv_k_k, rwkv_k_a, rwkv_r_k, rwkv_gn_g, rwkv_gn_b, rwkv_w_out)
    return (y_prompt, y_sample)
```

```python
import contextlib
import math
import numpy as np
import ml_dtypes
import concourse.bass as bass
import concourse.mybir as mybir
from concourse.bass_utils import run_bass_kernel_spmd

F32 = mybir.dt.float32
BF16 = mybir.dt.bfloat16
AF = mybir.ActivationFunctionType
ALU = mybir.AluOpType
AX = mybir.AxisListType

D = 2048
NCH = 16
NH_ATT = 16
DH = 128
GROUPS = ((128, 1), (512, 4), (2048, 16))
ATT_COLS = 20480
RMS_EPS = 1e-6
GN_EPS = 64e-5
NHR = 32
NR = 64
CH = 64
SAME_ENGINE_SYNC = True


class Buf:
    __slots__ = ("name", "w", "r")

    def __init__(self, name):
        self.name = name
        self.w = None
        self.r = []


class View:
    __slots__ = ("b", "ap")

    def __init__(self, b, ap):
        self.b = b
        self.ap = ap


class T:
    def __init__(self, prog, es, name, shape, dtype, psum=False):
        nc = prog.nc
        prog.uid += 1
        name = "%s_u%d" % (name, prog.uid)
        self.t = es.enter_context(nc.psum_tensor(name, shape, dtype) if psum else nc.sbuf_tensor(name, shape, dtype))
        self.b = Buf(name)
        prog.bufs.append(self.b)

    def __getitem__(self, idx):
        return View(self.b, self.t[idx])

    def v(self, ap):
        return View(self.b, ap)


class Op:
    __slots__ = ("eng", "fn", "deps", "is_dma", "needs_inc", "ms", "dma_idx")


class Prog:
    CE = ("pe", "act", "dve", "pool")
    QS = ("sp", "act", "pool")

    def __init__(self, nc, es, K=8):
        self.nc = nc
        self.engs = {"pe": nc.tensor, "act": nc.scalar, "dve": nc.vector, "pool": nc.gpsimd, "sp": nc.sync}
        self.sem = {e: es.enter_context(nc.semaphore("sem_" + e)) for e in self.CE}
        self.K = K
        self.dsem = {q: [es.enter_context(nc.semaphore("dsem_%s%d" % (q, i))) for i in range(K)] for q in self.QS}
        self.ms = {e: 0 for e in self.CE}
        self.dcount = {q: 0 for q in self.QS}
        self.seen = {}
        self.ops = []
        self.bufs = []
        self.n_instr = 0
        self.uid = 0

    def buf(self, name):
        b = Buf(name)
        self.bufs.append(b)
        return b

    def add(self, eng, fn, reads=(), writes=(), dma=False):
        op = Op()
        op.eng = eng
        op.fn = fn
        op.is_dma = dma
        op.needs_inc = False
        op.ms = 0
        op.dma_idx = -1
        deps = {}
        for b in reads:
            if b.w is not None:
                deps[id(b.w)] = b.w
        for b in writes:
            if b.w is not None:
                deps[id(b.w)] = b.w
            for o in b.r:
                deps[id(o)] = o
        for b in reads:
            if dma:
                b.r.append(op)
            else:
                b.r = [o for o in b.r if o.is_dma or o.eng != eng]
                b.r.append(op)
        for b in writes:
            b.w = op
            b.r = []
        dl = []
        for d in deps.values():
            if d is op:
                continue
            if (not d.is_dma) and (not dma) and d.eng == eng:
                if eng == "pe" or not SAME_ENGINE_SYNC:
                    continue
            dl.append(d)
            if not d.is_dma:
                d.needs_inc = True
        op.deps = dl
        if dma:
            op.dma_idx = self.dcount[eng]
            self.dcount[eng] += 1
        self.ops.append(op)
        return op

    def _wait(self, eng, key, sem, val):
        k = (eng, key)
        if self.seen.get(k, 0) >= val:
            return
        self.seen[k] = val
        self.engs[eng].wait_ge(sem, val)

    def flush(self):
        K = self.K
        last = {}
        for op in self.ops:
            if not op.is_dma:
                last[op.eng] = op
        for op in last.values():
            op.needs_inc = True
        for op in self.ops:
            e = self.engs[op.eng]
            if op.is_dma:
                i = op.dma_idx
                q = op.eng
                if i >= K:
                    self._wait(q, ("d", q, i % K), self.dsem[q][i % K], 16 * (i // K))
            for d in op.deps:
                if d.is_dma:
                    j = d.dma_idx
                    self._wait(op.eng, ("d", d.eng, j % K), self.dsem[d.eng][j % K], 16 * (j // K + 1))
                else:
                    self._wait(op.eng, ("c", d.eng), self.sem[d.eng], d.ms)
            ins = op.fn(e)
            self.n_instr += 1
            if op.is_dma:
                ins.then_inc(self.dsem[op.eng][op.dma_idx % K], 16)
            elif op.needs_inc:
                self.ms[op.eng] += 1
                op.ms = self.ms[op.eng]
                ins.then_inc(self.sem[op.eng], 1)
        for eng in ("pe", "act", "dve", "pool", "sp"):
            for c in self.CE:
                if self.ms[c] > 0:
                    self._wait(eng, ("c", c), self.sem[c], self.ms[c])
            for q in self.QS:
                n = self.dcount[q]
                for j in range(K):
                    cnt = (n - j + K - 1) // K if n > j else 0
                    if cnt > 0:
                        self._wait(eng, ("d", q, j), self.dsem[q][j], 16 * cnt)
        for b in self.bufs:
            b.w = None
            b.r = []
        self.bufs = [b for b in self.bufs if not b.name.startswith("~")]
        self.ops = []

    def mm(self, out, lhsT, rhs, start=True, stop=True, **kw):
        return self.add("pe", lambda e: e.matmul(out.ap, lhsT.ap, rhs.ap, start=start, stop=stop, **kw),
                        reads=[lhsT.b, rhs.b], writes=[out.b])

    def transpose(self, out, in_, ident):
        return self.add("pe", lambda e: e.transpose(out.ap, in_.ap, ident.ap), reads=[in_.b, ident.b], writes=[out.b])

    def act(self, out, in_, func, bias=None, scale=None, accum_out=None, extra_reads=()):
        kw = {}
        reads = [in_.b] + list(extra_reads)
        writes = [out.b]
        if bias is not None:
            if isinstance(bias, View):
                kw["bias"] = bias.ap
                reads.append(bias.b)
            else:
                kw["bias"] = bias
        if scale is not None:
            if isinstance(scale, View):
                kw["scale"] = scale.ap
                reads.append(scale.b)
            else:
                kw["scale"] = scale
        if accum_out is not None:
            kw["accum_out"] = accum_out.ap
            writes.append(accum_out.b)
        return self.add("act", lambda e: e.activation(out=out.ap, in_=in_.ap, func=func, **kw), reads=reads, writes=writes)

    def tt(self, eng, out, in0, in1, op):
        return self.add(eng, lambda e: e.tensor_tensor(out=out.ap, in0=in0.ap, in1=in1.ap, op=op),
                        reads=[in0.b, in1.b], writes=[out.b])

    def ts(self, eng, out, in0, s1, op0, s2=None, op1=None):
        reads = [in0.b]
        a1 = s1
        a2 = s2
        if isinstance(s1, View):
            a1 = s1.ap
            reads.append(s1.b)
        if isinstance(s2, View):
            a2 = s2.ap
            reads.append(s2.b)
        if op1 is None:
            return self.add(eng, lambda e: e.tensor_scalar(out=out.ap, in0=in0.ap, scalar1=a1, scalar2=None, op0=op0),
                            reads=reads, writes=[out.b])
        return self.add(eng, lambda e: e.tensor_scalar(out=out.ap, in0=in0.ap, scalar1=a1, scalar2=a2, op0=op0, op1=op1),
                        reads=reads, writes=[out.b])

    def stt(self, out, in0, scalar, in1, op0, op1):
        reads = [in0.b, in1.b]
        sc = scalar
        if isinstance(scalar, View):
            sc = scalar.ap
            reads.append(scalar.b)
        return self.add("dve", lambda e: e.scalar_tensor_tensor(out=out.ap, in0=in0.ap, scalar=sc, in1=in1.ap, op0=op0, op1=op1),
                        reads=reads, writes=[out.b])

    def copy(self, eng, out, in_):
        if eng == "act":
            return self.add("act", lambda e: e.copy(out=out.ap, in_=in_.ap), reads=[in_.b], writes=[out.b])
        return self.add(eng, lambda e: e.tensor_copy(out=out.ap, in_=in_.ap), reads=[in_.b], writes=[out.b])

    def recip(self, out, in_):
        return self.add("dve", lambda e: e.reciprocal(out=out.ap, in_=in_.ap), reads=[in_.b], writes=[out.b])

    def memset(self, eng, out, val):
        return self.add(eng, lambda e: e.memset(out.ap, val), reads=[], writes=[out.b])

    def dma(self, q, out, in_, reads=(), writes=(), **kw):
        return self.add(q, lambda e: e.dma_start(out=out, in_=in_, **kw), reads=list(reads), writes=list(writes), dma=True)


def _const_tables(S):
    bf = ml_dtypes.bfloat16
    ident = np.eye(128, dtype=np.float32)
    rot = np.zeros((128, 128), np.float32)
    for m in range(64):
        rot[m + 64, m] = -1.0
    for m in range(64, 128):
        rot[m - 64, m] = 1.0
    i = np.arange(128)[:, None]
    j = np.arange(128)[None, :]
    m0 = (i >= j).astype(np.float32)
    m1 = (i <= j).astype(np.float32)
    ones = np.ones((128, 128), np.float32)
    s = np.arange(64)[:, None]
    t = np.arange(64)[None, :]
    su = (s < t).astype(np.float32)
    iu = (s <= t).astype(np.float32)
    sl = (s > t).astype(np.float32)
    il = (s >= t).astype(np.float32)
    mf = np.tile(np.concatenate([su, iu, su, iu], axis=1), (2, 1))
    mb = np.tile(np.concatenate([sl, il, sl, il], axis=1), (2, 1))
    lf = np.tile(np.concatenate([sl, sl], axis=1), (2, 1))
    lb = np.tile(np.concatenate([su, su], axis=1), (2, 1))
    i64 = np.tile(np.concatenate([np.eye(64, dtype=np.float32)] * 2, axis=1), (2, 1))
    cb = np.concatenate([ident, rot, m0, m1, ones, mf, mb, lf, lb, i64], axis=1).astype(bf)
    half = 64
    inv_freq = (1.0 / (np.float32(10000.0) ** (np.arange(half, dtype=np.float32) * np.float32(2.0) / np.float32(128)))).astype(np.float32)
    ang = (np.arange(S, dtype=np.float32)[:, None] * inv_freq[None, :]).astype(np.float32)
    cos = np.cos(ang).astype(np.float32).T
    sin = np.sin(ang).astype(np.float32).T
    cs = np.concatenate([np.concatenate([cos, cos], 0), np.concatenate([sin, sin], 0)], axis=1)
    s2 = np.arange(128)[:, None]
    t2 = np.arange(128)[None, :]
    same = (s2 // 64) == (t2 // 64)
    p_i = (same & (s2 <= t2)).astype(np.float32)
    s_i = (same & (s2 >= t2)).astype(np.float32)
    p_s = (same & (s2 < t2)).astype(np.float32)
    s_s = (same & (s2 > t2)).astype(np.float32)
    ind = np.zeros((128, 2), np.float32)
    ind[:64, 0] = 1.0
    ind[64:, 1] = 1.0
    cf = np.concatenate([cs, p_i, s_i, p_s, s_s, ind], axis=1).astype(np.float32)
    return cb, cf


CB_IDENT, CB_ROT, CB_M0, CB_M1, CB_ONES, CB_MF, CB_MB, CB_LF, CB_LB, CB_I64 = 0, 128, 256, 384, 512, 640, 896, 1152, 1280, 1408
CB_W = 1536


def _valid_tables(S, valid_len):
    cols = []
    for (_, d) in GROUPS:
        L = S // d
        nblk = L // 128 + 1
        for r in range(d):
            n = np.arange(nblk * 128) - 64
            pos = n * d + r
            ok = (n >= 0) & (n < L) & (pos < valid_len)
            cols.append(ok.reshape(nblk, 128).T.astype(np.float32))
    return np.concatenate(cols, axis=1).astype(ml_dtypes.bfloat16)


def _tokmask(S, valid_len):
    return np.broadcast_to((np.arange(S) < valid_len).astype(np.float32)[None, :], (128, S)).astype(ml_dtypes.bfloat16).copy()


class Builder:
    def __init__(self, S, n_layers=4, debug=None, debug_out=(), rw_stop=99):
        self.debug_out = set(debug_out)
        self.rw_stop = rw_stop
        self.S = S
        self.n_layers = n_layers
        self.debug = debug or {}
        self.nc = bass.Bass("TRN2", target_bir_lowering=False)
        self.es = contextlib.ExitStack()

    def dram(self, name, shape, dtype, kind="Internal"):
        if name in self.debug_out:
            kind = "ExternalOutput"
        return self.nc.dram_tensor(name, list(shape), dtype, kind=kind).ap()

    def build(self):
        nc = self.nc
        S = self.S
        with self.es as es:
            self.P = Prog(nc, es)
            self.declare_io()
            self.persistent(es)
            self.prologue()
            x_cur = self.x_in
            bufs = [self.xa, self.xb]
            for layer in range(self.n_layers):
                x_next = self.y_out if layer == self.n_layers - 1 else bufs[layer % 2]
                j = layer // 2
                self.phase_norm(x_cur, layer)
                if layer % 2 == 0:
                    self.phase_att_inproj(j)
                    self.phase_att_core()
                    self.phase_out(x_cur, x_next, self.wb_att_out[j], layer)
                else:
                    self.phase_rwkv(j)
                    self.phase_out(x_cur, x_next, self.wb_rwkv_out[j], layer)
                x_cur = x_next
        return nc

    def declare_io(self):
        S = self.S
        d = self.dram
        self.x_in = d("x", [S, D], F32, "ExternalInput")
        self.y_out = d("y", [S, D], F32, "ExternalOutput")
        self.in_norm_pre = d("norm_pre", [4, D], F32, "ExternalInput")
        self.in_norm_post = d("norm_post", [4, D], F32, "ExternalInput")
        self.in_att_w_in = d("att_w_in", [2, D, ATT_COLS], F32, "ExternalInput")
        self.in_att_w_out = d("att_w_out", [2, D, D], F32, "ExternalInput")
        self.in_rw = {}
        for name, shape in (("rwkv_mu_prev", [2, 6, D]), ("rwkv_mu_next", [2, 6, D]), ("rwkv_w_in", [2, D, 4 * D]),
                            ("rwkv_w0", [2, 2, D]), ("rwkv_w1", [2, 2, D, 96]), ("rwkv_w2", [2, 2, 96, D]),
                            ("rwkv_a0", [2, 2, D]), ("rwkv_a1", [2, 2, D, 96]), ("rwkv_a2", [2, 2, 96, D]),
                            ("rwkv_v0", [1, D]), ("rwkv_v1", [1, D, 64]), ("rwkv_v2", [1, 64, D]),
                            ("rwkv_k_k", [2, D]), ("rwkv_k_a", [2, D]), ("rwkv_r_k", [2, 2, D]),
                            ("rwkv_gn_g", [2, D]), ("rwkv_gn_b", [2, D]), ("rwkv_w_out", [2, D, D])):
            self.in_rw[name] = d(name, shape, F32, "ExternalInput")
        self.in_cb = d("const_bf", [128, CB_W], BF16, "ExternalInput")
        self.cf_w = 2 * S + 4 * 128 + 2
        self.in_cf = d("const_f32", [128, self.cf_w], F32, "ExternalInput")
        self.nvalid = sum(dd * ((S // dd) // 128 + 1) for (_, dd) in GROUPS)
        self.in_valid = d("valid", [128, self.nvalid], BF16, "ExternalInput")
        self.in_tokmask = d("tokmask", [128, S], BF16, "ExternalInput")
        self.xa = d("xa", [S, D], F32)
        self.xb = d("xb", [S, D], F32)
        self.hT = d("hT", [NCH, 128, S + 2], BF16)
        self.wb_att_in = [d("wb_att_in%d" % j, [D, ATT_COLS], BF16) for j in range(2)]
        self.wb_att_out = [d("wb_att_out%d" % j, [D, D], BF16) for j in range(2)]
        self.wb_rwkv_in = [d("wb_rwkv_in%d" % j, [D, 4 * D], BF16) for j in range(2)]
        self.wb_rwkv_out = [d("wb_rwkv_out%d" % j, [D, D], BF16) for j in range(2)]
        self.qT = []
        self.kT = []
        self.vv = []
        for g, (_, dd) in enumerate(GROUPS):
            L = S // dd
            self.qT.append(d("qT%d" % g, [NH_ATT, 128, dd, L], BF16))
            self.kT.append(d("kT%d" % g, [NH_ATT, 128, dd, L + 128], BF16))
            self.vv.append(d("vv%d" % g, [dd, L + 128, D], BF16))
        self.zT = d("zT", [D, S], BF16)
        self.yT = d("yT", [D, S], BF16)
        self.declare_rwkv()
        for name, (shape, dt) in self.debug.items():
            setattr(self, "dbg_" + name, d("dbg_" + name, shape, dt, "ExternalOutput"))

    def persistent(self, es):
        P = self.P
        self.cb = T(P, es, "cb", [128, CB_W], BF16)
        P.dma("sp", self.cb.t[:], self.in_cb[:, :], writes=[self.cb.b])
        self.zeros = T(P, es, "zeros", [128, 2048], BF16)
        P.memset("pool", self.zeros[:], 0.0)
        P.flush()

    def cbv(self, off, w=128, rows=128):
        return self.cb[0:rows, off:off + w]

    def prologue(self):
        P = self.P
        S = self.S

        def cast(dst, src, rows, cols):
            cw = min(cols, 2048)
            nseg = cols // cw
            rstep = max(1, 4096 // nseg)
            for r0 in range(0, rows, rstep):
                r1 = min(rows, r0 + rstep)
                if nseg == 1:
                    o = dst[r0:r1, :]
                    i = src[r0:r1, :]
                else:
                    o = dst[r0:r1, :].rearrange("r (s c) -> r s c", c=cw)
                    i = src[r0:r1, :].rearrange("r (s c) -> r s c", c=cw)
                P.dma("pool", o, i)

        nl = self.n_layers
        for j in range(2):
            if nl > 2 * j:
                cast(self.wb_att_in[j], self.in_att_w_in[j], D, ATT_COLS)
                cast(self.wb_att_out[j], self.in_att_w_out[j], D, D)
            if nl > 2 * j + 1:
                cast(self.wb_rwkv_in[j], self.in_rw["rwkv_w_in"][j], D, 4 * D)
                cast(self.wb_rwkv_out[j], self.in_rw["rwkv_w_out"][j], D, D)
        if nl > 1:
            cast(self.wb_w1, self.in_rw["rwkv_w1"].rearrange("j c d r -> (j c d) r"), 4 * D, 96)
            cast(self.wb_a1, self.in_rw["rwkv_a1"].rearrange("j c d r -> (j c d) r"), 4 * D, 96)
            cast(self.wb_w2, self.in_rw["rwkv_w2"].rearrange("j c r d -> (j c r) d"), 4 * 96, D)
            cast(self.wb_a2, self.in_rw["rwkv_a2"].rearrange("j c r d -> (j c r) d"), 4 * 96, D)
            cast(self.wb_v1, self.in_rw["rwkv_v1"][0], D, 64)
            cast(self.wb_v2, self.in_rw["rwkv_v2"][0], 64, D)
        for g, (_, dd) in enumerate(GROUPS):
            L = S // dd
            for h in range(NH_ATT):
                for side in (0, L + 64):
                    P.dma("sp", self.kT[g][h][:, :, side:side + 64],
                          self.zeros.t[:, 0:dd * 64].rearrange("p (r n) -> p r n", n=64), reads=[self.zeros.b])
            for r in range(dd):
                for side in (0, L + 64):
                    P.dma("sp", self.vv[g][r, side:side + 64, :], self.zeros.t[0:64, :], reads=[self.zeros.b])
        P.flush()

    def load_bcast(self, es, name, src_row):
        P = self.P
        t = T(P, es, name, [128, D], F32)
        P.dma("sp", t.t[:], src_row.partition_broadcast(128), writes=[t.b])
        return t

    def rstd_from_ss(self, ss, tmp, rstd, eps, n):
        P = self.P
        P.act(tmp, ss, AF.Sqrt, bias=self.eps_t[eps], scale=1.0 / n)
        P.recip(rstd, tmp)

    def phase_norm(self, x_src, layer):
        P = self.P
        S = self.S
        with contextlib.ExitStack() as es:
            g = self.load_bcast(es, "g_pre", self.in_norm_pre[layer:layer + 1, :])
            self.eps_t = {}
            epst = T(P, es, "epst", [128, 2], F32)
            P.memset("pool", epst[:, 0:1], RMS_EPS)
            self.eps_t[RMS_EPS] = epst[:, 0:1]
            NS = 3
            xt = [T(P, es, "xt%d" % i, [128, D], F32) for i in range(NS)]
            junk = T(P, es, "junk", [128, D], BF16)
            hb = [T(P, es, "hb%d" % i, [128, D], BF16) for i in range(2)]
            st = [T(P, es, "st%d" % i, [128, 3], F32) for i in range(4)]
            tp = [T(P, es, "tp%d" % i, [128, 1024], BF16, psum=True) for i in range(4)]
            hs = [T(P, es, "hs%d" % i, [128, NCH, 512], BF16) for i in range(2)]
            ident = self.cbv(CB_IDENT)
            nblk = S // 128
            for i in range(min(2, nblk)):
                P.dma("sp", xt[i % NS].t[:], x_src[i * 128:(i + 1) * 128, :], writes=[xt[i % NS].b])
            k = 0
            for i in range(nblk):
                if i + 2 < nblk:
                    P.dma("sp", xt[(i + 2) % NS].t[:], x_src[(i + 2) * 128:(i + 3) * 128, :], writes=[xt[(i + 2) % NS].b])
                x = xt[i % NS]
                s = st[i % 4]
                P.act(junk[:], x[:], AF.Square, accum_out=s[:, 0:1])
                self.rstd_from_ss(s[:, 0:1], s[:, 1:2], s[:, 2:3], RMS_EPS, D)
                h = hb[i % 2]
                P.stt(h[:], x[:], s[:, 2:3], g[:], ALU.mult, ALU.mult)
                u = i // 4
                sub = i % 4
                hst = hs[u % 2]
                for half in range(2):
                    pt = tp[k % 4]
                    k += 1
                    for c in range(8):
                        cc = half * 8 + c
                        P.transpose(pt[:, c * 128:(c + 1) * 128], h[:, cc * 128:(cc + 1) * 128], ident)
                    src = pt.v(pt.t[:, :].rearrange("p (c t) -> p c t", t=128))
                    dst = hst[:, half * 8:(half + 1) * 8, sub * 128:(sub + 1) * 128]
                    if half == 0:
                        P.copy("act", dst, src)
                    else:
                        P.copy("dve", dst, src)
                if sub == 3 or i == nblk - 1:
                    t0 = u * 512
                    n = (sub + 1) * 128
                    P.dma("sp", self.hT[:, :, 1 + t0:1 + t0 + n].rearrange("c p t -> p c t"), hst.t[:, :, 0:n], reads=[hst.b])
            zc = T(P, es, "zc", [128, NCH, 2], BF16)
            P.memset("pool", zc[:], 0.0)
            P.dma("sp", self.hT[:, :, 0:1].rearrange("c p t -> p c t"), zc.t[:, :, 0:1], reads=[zc.b], allow_slow_non_contiguous=True)
            P.dma("sp", self.hT[:, :, S + 1:S + 2].rearrange("c p t -> p c t"), zc.t[:, :, 1:2], reads=[zc.b], allow_slow_non_contiguous=True)
            P.flush()

    def phase_att_inproj(self, j):
        P = self.P
        S = self.S
        ST = min(2048, S)
        W = self.wb_att_in[j]
        with contextlib.ExitStack() as es:
            hts = T(P, es, "hts", [128, NCH, ST], BF16)
            cs = T(P, es, "cs", [128, 2, ST], F32)
            NW = 3
            wt = [T(P, es, "wt%d" % i, [128, NCH, 512], BF16) for i in range(NW)]
            ps = [T(P, es, "ps%d" % i, [128, 512], F32, psum=True) for i in range(3)]
            pr = [T(P, es, "pr%d" % i, [128, 512], F32, psum=True) for i in range(2)]
            tb = [T(P, es, "tb%d" % i, [128, 512], BF16) for i in range(3)]
            t1 = [T(P, es, "t1_%d" % i, [128, 512], F32) for i in range(3)]
            t2 = [T(P, es, "t2_%d" % i, [128, 512], F32) for i in range(3)]
            qs = [T(P, es, "qs%d" % i, [128, ST], BF16) for i in range(3)]
            vs = [T(P, es, "vs%d" % i, [128, ST // 128, 512], BF16) for i in range(2)]
            rot = self.cbv(CB_ROT)
            NCG = ATT_COLS // 512
            Wv = W.rearrange("(c p) n -> p c n", p=128)
            cnt = {"ps": 0, "pr": 0, "tb": 0, "qs": 0, "vs": 0, "ev": 0}

            def load_w(cg):
                P.dma("sp", wt[cg % NW].t[:], Wv[:, :, cg * 512:(cg + 1) * 512], writes=[wt[cg % NW].b])

            for st_i in range(S // ST):
                t0 = st_i * ST
                P.dma("sp", hts.t[:], self.hT[:, :, 1 + t0:1 + t0 + ST].rearrange("c p t -> p c t"), writes=[hts.b])
                P.dma("sp", cs.t[:, 0, :], self.in_cf[:, t0:t0 + ST], writes=[cs.b])
                P.dma("sp", cs.t[:, 1, :], self.in_cf[:, S + t0:S + t0 + ST], writes=[cs.b])
                load_w(0)
                load_w(1)
                for cg in range(NCG):
                    if cg + 2 < NCG:
                        load_w(cg + 2)
                    w = wt[cg % NW]
                    if cg < 36:
                        g = cg // 12
                        typ = (cg % 12) // 4
                        h0 = (cg % 4) * 4
                    else:
                        g = -1
                        typ = 3
                        h0 = (cg - 36) * 4
                    if typ == 2:
                        dd = GROUPS[g][1]
                        vst = vs[cnt["vs"] % 2]
                        cnt["vs"] += 1
                        for tbi in range(ST // 128):
                            p = ps[cnt["ps"] % 3]
                            cnt["ps"] += 1
                            for c in range(NCH):
                                P.mm(p[:], hts[:, c, tbi * 128:(tbi + 1) * 128], w[:, c, :], start=(c == 0), stop=(c == NCH - 1))
                            if cnt["ev"] % 2 == 0:
                                P.copy("act", vst[:, tbi, :], p[:])
                            else:
                                P.copy("dve", vst[:, tbi, :], p[:])
                            cnt["ev"] += 1
                        npb = 128 // dd
                        for r in range(dd):
                            n0 = 64 + t0 // dd
                            dst = self.vv[g][r, n0:n0 + ST // dd, h0 * 128:h0 * 128 + 512].rearrange("(b n) c -> n b c", n=npb)
                            src = vst.t[r::dd, :, :] if dd > 1 else vst.t[:, :, :]
                            P.dma("sp", dst, src, reads=[vst.b])
                    else:
                        for jh in range(4):
                            h = h0 + jh
                            qst = qs[cnt["qs"] % 3]
                            cnt["qs"] += 1
                            dd = GROUPS[g][1] if typ < 2 else 1
                            for sub in range(ST // 512):
                                p = ps[cnt["ps"] % 3]
                                cnt["ps"] += 1
                                for c in range(NCH):
                                    P.mm(p[:], w[:, c, jh * 128:(jh + 1) * 128], hts[:, c, sub * 512:(sub + 1) * 512],
                                         start=(c == 0), stop=(c == NCH - 1))
                                if typ == 3:
                                    P.act(qst[:, sub * 512:(sub + 1) * 512], p[:], AF.Silu)
                                    continue
                                k = cnt["tb"] % 3
                                cnt["tb"] += 1
                                tbf = tb[k]
                                P.copy("act", tbf[:], p[:])
                                rp = pr[cnt["pr"] % 2]
                                cnt["pr"] += 1
                                P.mm(rp[:], rot, tbf[:])
                                P.tt("dve", t2[k][:], rp[:], cs[:, 1, sub * 512:(sub + 1) * 512], ALU.mult)
                                P.tt("dve", t1[k][:], p[:], cs[:, 0, sub * 512:(sub + 1) * 512], ALU.mult)
                                nsub = 512 // dd
                                if dd == 1:
                                    dst = qst[:, sub * 512:(sub + 1) * 512]
                                    P.tt("pool", dst, t1[k][:], t2[k][:], ALU.add)
                                else:
                                    dst = qst.v(qst.t[:, :].rearrange("p (r n) -> p r n", r=dd)[:, :, sub * nsub:(sub + 1) * nsub])
                                    a = t1[k].v(t1[k].t[:, :].rearrange("p (n r) -> p r n", r=dd))
                                    b = t2[k].v(t2[k].t[:, :].rearrange("p (n r) -> p r n", r=dd))
                                    P.tt("pool", dst, a, b, ALU.add)
                            if typ == 3:
                                P.dma("sp", self.zT[h * 128:(h + 1) * 128, t0:t0 + ST], qst.t[:, :], reads=[qst.b])
                            elif typ == 0:
                                dst = self.qT[g][h][:, :, t0 // dd:(t0 + ST) // dd]
                                P.dma("sp", dst, qst.t[:, :].rearrange("p (r n) -> p r n", r=dd), reads=[qst.b])
                            else:
                                dst = self.kT[g][h][:, :, 64 + t0 // dd:64 + (t0 + ST) // dd]
                                P.dma("sp", dst, qst.t[:, :].rearrange("p (r n) -> p r n", r=dd), reads=[qst.b])
            P.flush()

    def phase_att_core(self):
        P = self.P
        S = self.S
        scale = 1.0 / math.sqrt(DH)
        with contextlib.ExitStack() as es:
            vt = T(P, es, "valid", [128, self.nvalid], BF16)
            P.dma("sp", vt.t[:], self.in_valid[:, :], writes=[vt.b])
            accn = [T(P, es, "accn%d" % i, [128, S], F32) for i in range(2)]
            accd = [T(P, es, "accd%d" % i, [128, S], F32) for i in range(2)]
            NSL = 4
            qsb = [T(P, es, "qsb%d" % i, [128, 512], BF16) for i in range(NSL)]
            ksb = [T(P, es, "ksb%d" % i, [128, 640], BF16) for i in range(NSL)]
            vsb = [T(P, es, "vsb%d" % i, [128, 5, 128], BF16) for i in range(NSL)]
            pss = [T(P, es, "pss%d" % i, [128, 512], F32, psum=True) for i in range(2)]
            psn = [T(P, es, "psn%d" % i, [128, 512], F32, psum=True) for i in range(2)]
            psd = [T(P, es, "psd%d" % i, [128, 512], F32, psum=True) for i in range(2)]
            pe = [T(P, es, "pe%d" % i, [128, 256], BF16) for i in range(4)]
            pm = [T(P, es, "pm%d" % i, [128, 256], BF16) for i in range(4)]
            rc = [T(P, es, "rc%d" % i, [128, 512], F32) for i in range(2)]
            ob = [T(P, es, "ob%d" % i, [128, 512], F32) for i in range(2)]
            zb = [T(P, es, "zb%d" % i, [128, 512], BF16) for i in range(2)]
            yb = [T(P, es, "yb%d" % i, [128, 512], BF16) for i in range(2)]
            ones = self.cbv(CB_ONES)
            mask = self.cb[:, CB_M0:CB_M0 + 256]
            tiles = []
            voff = 0
            for g, (_, dd) in enumerate(GROUPS):
                L = S // dd
                QT = min(512, L)
                nblk_r = L // 128 + 1
                for r in range(dd):
                    for qt in range(L // QT):
                        tiles.append((g, dd, L, QT, r, qt, voff + r * nblk_r))
                voff += dd * nblk_r
            cnt = {"sl": 0, "ps": 0, "pe": 0, "fin": 0}

            def load_tile(h, ti, slot):
                g, dd, L, QT, r, qt, vo = tiles[ti]
                nb = QT // 128 + 1
                P.dma("sp", qsb[slot].t[:, 0:QT], self.qT[g][h][:, r, qt * QT:(qt + 1) * QT], writes=[qsb[slot].b])
                P.dma("sp", ksb[slot].t[:, 0:QT + 128], self.kT[g][h][:, r, qt * QT:(qt + 1) * QT + 128], writes=[ksb[slot].b])
                P.dma("sp", vsb[slot].t[:, 0:nb, :],
                      self.vv[g][r, qt * QT:qt * QT + QT + 128, h * 128:(h + 1) * 128].rearrange("(b p) c -> p b c", p=128),
                      writes=[vsb[slot].b])

            seq = [(h, ti) for h in range(NH_ATT) for ti in range(len(tiles))]
            PF = 2
            for i in range(min(PF, len(seq))):
                load_tile(seq[i][0], seq[i][1], i % NSL)
            for i, (h, ti) in enumerate(seq):
                if i + PF < len(seq):
                    load_tile(seq[i + PF][0], seq[i + PF][1], (i + PF) % NSL)
                slot = i % NSL
                g, dd, L, QT, r, qt, vo = tiles[ti]
                an = accn[h % 2]
                ad = accd[h % 2]
                pn = psn[i % 2]
                pd = psd[i % 2]
                nqb = QT // 128
                for qb in range(nqb):
                    sc = pss[cnt["ps"] % 2]
                    half = 0
                    cnt["ps"] += 1
                    qv = qsb[slot][:, qb * 128:(qb + 1) * 128]
                    for kb in range(2):
                        P.mm(sc[:, kb * 128:(kb + 1) * 128], ksb[slot][:, (qb + kb) * 128:(qb + kb + 1) * 128], qv)
                    e = pe[cnt["pe"] % 4]
                    m = pm[cnt["pe"] % 4]
                    cnt["pe"] += 1
                    P.act(e[:], sc[:, 0:256], AF.Exp, scale=scale)
                    for kb in range(2):
                        blk = vo + qt * nqb + qb + kb
                        P.stt(m[:, kb * 128:(kb + 1) * 128], e[:, kb * 128:(kb + 1) * 128], vt[:, blk:blk + 1],
                              self.cb[:, CB_M0 + kb * 128:CB_M0 + (kb + 1) * 128], ALU.mult, ALU.mult)
                    for kb in range(2):
                        P.mm(pn[:, qb * 128:(qb + 1) * 128], vsb[slot][:, qb + kb, :], m[:, kb * 128:(kb + 1) * 128],
                             start=(kb == 0), stop=(kb == 1))
                    for kb in range(2):
                        P.mm(pd[:, qb * 128:(qb + 1) * 128], ones, m[:, kb * 128:(kb + 1) * 128],
                             start=(kb == 0), stop=(kb == 1))
                base = qt * QT * dd + r
                if dd == 1:
                    dn = an[:, base:base + QT]
                    dden = ad[:, base:base + QT]
                else:
                    dn = an.v(an.t[:, qt * QT * dd:(qt + 1) * QT * dd].rearrange("p (n r) -> p r n", r=dd)[:, r, :])
                    dden = ad.v(ad.t[:, qt * QT * dd:(qt + 1) * QT * dd].rearrange("p (n r) -> p r n", r=dd)[:, r, :])
                if g == 0:
                    P.copy("dve", dn, pn[:, 0:QT])
                    P.copy("act", dden, pd[:, 0:QT])
                else:
                    P.tt("dve", dn, pn[:, 0:QT], dn, ALU.add)
                    P.tt("dve", dden, pd[:, 0:QT], dden, ALU.add)
                if ti == len(tiles) - 1:
                    for c0 in range(0, S, 512):
                        k = cnt["fin"] % 2
                        cnt["fin"] += 1
                        w = min(512, S - c0)
                        P.dma("act", zb[k].t[:, 0:w], self.zT[h * 128:(h + 1) * 128, c0:c0 + w], writes=[zb[k].b])
                        P.ts("dve", rc[k][:, 0:w], ad[:, c0:c0 + w], 1e-30, ALU.max)
                        P.recip(rc[k][:, 0:w], rc[k][:, 0:w])
                        P.tt("pool", ob[k][:, 0:w], an[:, c0:c0 + w], rc[k][:, 0:w], ALU.mult)
                        P.tt("pool", yb[k][:, 0:w], ob[k][:, 0:w], zb[k][:, 0:w], ALU.mult)
                        P.dma("act", self.yT[h * 128:(h + 1) * 128, c0:c0 + w], yb[k].t[:, 0:w], reads=[yb[k].b])
            P.flush()

    def phase_out(self, x_src, x_dst, Wb, layer):
        P = self.P
        S = self.S
        with contextlib.ExitStack() as es:
            g = self.load_bcast(es, "g_post", self.in_norm_post[layer:layer + 1, :])
            epst = T(P, es, "epst", [128, 2], F32)
            P.memset("pool", epst[:, 0:1], RMS_EPS)
            self.eps_t = {RMS_EPS: epst[:, 0:1]}
            w = T(P, es, "wout", [128, NCH, D], BF16)
            P.dma("sp", w.t[:], Wb.rearrange("(c p) n -> p c n", p=128), writes=[w.b])
            ys = [T(P, es, "ys%d" % i, [128, NCH, 512], BF16) for i in range(2)]
            xt = [T(P, es, "xo%d" % i, [128, D], F32) for i in range(3)]
            ot = [T(P, es, "ot%d" % i, [128, D], F32) for i in range(2)]
            junk = T(P, es, "junk", [128, D], BF16)
            st = [T(P, es, "st%d" % i, [128, 3], F32) for i in range(4)]
            po = [T(P, es, "po%d" % i, [128, D], F32, psum=True) for i in range(2)]
            yTv = self.yT.rearrange("(c p) t -> p c t", p=128)
            nu = (S + 511) // 512

            def load_u(u):
                t0 = u * 512
                n = min(512, S - t0)
                P.dma("sp", ys[u % 2].t[:, :, 0:n], yTv[:, :, t0:t0 + n], writes=[ys[u % 2].b])

            load_u(0)
            nblk = S // 128
            for i in range(min(2, nblk)):
                P.dma("sp", xt[i % 3].t[:], x_src[i * 128:(i + 1) * 128, :], writes=[xt[i % 3].b])
            for i in range(nblk):
                u = i // 4
                sub = i % 4
                if sub == 0 and u + 1 < nu:
                    load_u(u + 1)
                if i + 2 < nblk:
                    P.dma("sp", xt[(i + 2) % 3].t[:], x_src[(i + 2) * 128:(i + 3) * 128, :], writes=[xt[(i + 2) % 3].b])
                y = ys[u % 2]
                p = po[i % 2]
                for nb in range(4):
                    for c in range(NCH):
                        P.mm(p[:, nb * 512:(nb + 1) * 512], y[:, c, sub * 128:(sub + 1) * 128], w[:, c, nb * 512:(nb + 1) * 512],
                             start=(c == 0), stop=(c == NCH - 1))
                s = st[i % 4]
                P.act(junk[:], p[:], AF.Square, accum_out=s[:, 0:1])
                self.rstd_from_ss(s[:, 0:1], s[:, 1:2], s[:, 2:3], RMS_EPS, D)
                o = ot[i % 2]
                P.stt(o[:], p[:], s[:, 2:3], g[:], ALU.mult, ALU.mult)
                P.tt("pool", o[:], o[:], xt[i % 3][:], ALU.add)
                P.dma("act", x_dst[i * 128:(i + 1) * 128, :], o.t[:], reads=[o.b])
            P.flush()

    def declare_rwkv(self):
        S = self.S
        d = self.dram
        self.rw_r = d("rw_r", [S, D], BF16)
        self.rw_k = d("rw_k", [S, D], BF16)
        self.rw_sz = d("rw_sz", [S, D], BF16)
        self.rw_v = [d("rw_v%d" % j, [S, D], BF16) for j in range(2)]
        self.rw_sw = [d("rw_sw%d" % c, [S, D], F32) for c in range(2)]
        self.rw_a = [d("rw_a%d" % c, [S, D], BF16) for c in range(2)]
        self.rw_gate = d("rw_gate", [S, D], BF16)
        self.wb_w1 = d("wb_w1", [4 * D, 96], BF16)
        self.wb_a1 = d("wb_a1", [4 * D, 96], BF16)
        self.wb_w2 = d("wb_w2", [4 * 96, D], BF16)
        self.wb_a2 = d("wb_a2", [4 * 96, D], BF16)
        self.wb_v1 = d("wb_v1", [D, 64], BF16)
        self.wb_v2 = d("wb_v2", [64, D], BF16)
        self.ft = [d("ft%d" % c, [4, 16, 128, S], BF16) for c in range(2)]
        self.ah = [d("ah%d" % c, [S, D], BF16) for c in range(2)]
        self.kh = [d("kh%d" % c, [S, D], BF16) for c in range(2)]
        self.vp = d("vp", [S, D], BF16)
        self.bon = d("bon", [S, NHR], F32)
        self.gam = [d("gam%d" % c, [128, 16, S // 64], F32) for c in range(2)]
        self.am = [d("am%d" % c, [S // 128, 128, 16, 512], BF16) for c in range(2)]
        self.ttm = [d("ttm%d" % c, [S // 128, 128, 32, 64], BF16) for c in range(2)]
        self.yd = [d("yd%d" % c, [S, D], F32) for c in range(2)]

    def phase_rwkv(self, j):
        self.vp_src = self.vp if j == 1 else self.rw_v[j]
        for i, f in enumerate((lambda: self.phase_rwkv_proj(j), lambda: self.phase_rwkv_prep(j), self.phase_rwkv_chain,
                               self.phase_rwkv_scan, lambda: self.phase_rwkv_post(j))):
            if i < self.rw_stop:
                f()

    def phase_rwkv_proj(self, j):
        P = self.P
        S = self.S
        TT = min(512, S)
        NTB = TT // 128
        vres = (j == 1)
        Wv = self.wb_rwkv_in[j].rearrange("(c p) n -> p c n", p=128)
        R = self.in_rw
        with contextlib.ExitStack() as es:
            mup = T(P, es, "mup", [128, 6, NCH], F32)
            mun = T(P, es, "mun", [128, 6, NCH], F32)
            P.dma("sp", mup.t[:], R["rwkv_mu_prev"][j].rearrange("g (c p) -> p g c", p=128), writes=[mup.b], allow_slow_non_contiguous=True)
            P.dma("sp", mun.t[:], R["rwkv_mu_next"][j].rearrange("g (c p) -> p g c", p=128), writes=[mun.b], allow_slow_non_contiguous=True)
            w1 = []
            a1 = []
            w2 = []
            a2 = []
            for c in range(2):
                o = (j * 2 + c)
                t_ = T(P, es, "w1_%d" % c, [128, NCH, 96], BF16)
                P.dma("sp", t_.t[:], self.wb_w1[o * D:(o + 1) * D, :].rearrange("(k p) r -> p k r", p=128), writes=[t_.b])
                w1.append(t_)
                t_ = T(P, es, "a1_%d" % c, [128, NCH, 96], BF16)
                P.dma("sp", t_.t[:], self.wb_a1[o * D:(o + 1) * D, :].rearrange("(k p) r -> p k r", p=128), writes=[t_.b])
                a1.append(t_)
                t_ = T(P, es, "w2_%d" % c, [98, D], BF16)
                P.dma("sp", t_.t[0:96, :], self.wb_w2[o * 96:(o + 1) * 96, :], writes=[t_.b])
                w2.append(t_)
                t_ = T(P, es, "a2_%d" % c, [98, D], BF16)
                P.dma("sp", t_.t[0:96, :], self.wb_a2[o * 96:(o + 1) * 96, :], writes=[t_.b])
                a2.append(t_)
            if vres:
                v1 = T(P, es, "v1", [128, NCH, 64], BF16)
                P.dma("sp", v1.t[:], self.wb_v1.rearrange("(k p) r -> p k r", p=128), writes=[v1.b])
                v2 = T(P, es, "v2", [66, D], BF16)
                P.dma("sp", v2.t[0:64, :], self.wb_v2[:, :], writes=[v2.b])
            with contextlib.ExitStack() as es2:
                bf_ = T(P, es2, "bias_f", [1, D], F32)
                bh = T(P, es2, "bias_h", [1, D], BF16)
                b32 = T(P, es2, "bias_32", [1, D], F32)
                bl = T(P, es2, "bias_l", [1, D], BF16)
                rows = [(R["rwkv_w0"][j, 0:1, :], w2[0], 96), (R["rwkv_w0"][j, 1:2, :], w2[1], 96),
                        (R["rwkv_a0"][j, 0:1, :], a2[0], 96), (R["rwkv_a0"][j, 1:2, :], a2[1], 96)]
                if vres:
                    rows.append((R["rwkv_v0"][0:1, :], v2, 64))
                for src, dst, r0 in rows:
                    P.dma("sp", bf_.t[:], src, writes=[bf_.b])
                    P.copy("dve", bh[:], bf_[:])
                    P.copy("dve", b32[:], bh[:])
                    P.tt("dve", b32[:], bf_[:], b32[:], ALU.subtract)
                    P.copy("dve", bl[:], b32[:])
                    P.dma("sp", dst.t[r0:r0 + 1, :], bh.t[:], reads=[bh.b], writes=[dst.b])
                    P.dma("sp", dst.t[r0 + 1:r0 + 2, :], bl.t[:], reads=[bl.b], writes=[dst.b])
                P.flush()
            hts = [T(P, es, "rhts%d" % i, [128, NCH, TT + 2], BF16) for i in range(2)]
            dp = T(P, es, "dp", [128, NCH, TT], BF16)
            tmk = [T(P, es, "tmk%d" % i, [128, TT], BF16) for i in range(2)]
            dn = T(P, es, "dn", [128, NCH, TT], BF16)
            xs = [T(P, es, "xs%d" % i, [128, NCH, TT], BF16) for i in range(2)]
            tmp = [T(P, es, "xtmp%d" % i, [128, TT], F32) for i in range(2)]
            wt = [T(P, es, "rwt%d" % i, [128, NCH, 512], BF16) for i in range(3)]
            ps = [T(P, es, "rps%d" % i, [128, 512], F32, psum=True) for i in range(4)]
            pl = [T(P, es, "rpl%d" % i, [128, 512], F32, psum=True) for i in range(2)]
            stg = [T(P, es, "rstg%d" % i, [128, NTB, 512], BF16) for i in range(2)]
            stf = [T(P, es, "rstf%d" % i, [128, 512], F32) for i in range(2)]
            l1 = [T(P, es, "l1_%d" % i, [98, TT], BF16) for i in range(2)]
            for t_ in l1:
                P.memset("pool", t_[96:98, :], 1.0)
            l1v = None
            if vres:
                l1v = T(P, es, "l1v", [66, TT], BF16)
                P.memset("pool", l1v[64:66, :], 1.0)
            cnt = {"ps": 0, "pl": 0, "stg": 0, "stf": 0, "wt": 0, "ev": 0, "xs": 0, "l1": 0}
            ntile = S // TT

            def load_h(ti):
                t0 = ti * TT
                P.dma("sp", hts[ti % 2].t[:], self.hT[:, :, t0:t0 + TT + 2].rearrange("c p t -> p c t"), writes=[hts[ti % 2].b])

            def evac(dst, src, func=None):
                if func is not None:
                    P.act(dst, src, func)
                else:
                    P.copy("act", dst, src)
                cnt["ev"] += 1

            def lora(x, wA, wB, krows, func1, dst_dram, fp32_out, l1t, t0):
                p1 = pl[cnt["pl"] % 2]
                cnt["pl"] += 1
                for c in range(NCH):
                    P.mm(p1[0:krows, 0:TT], wA[:, c, 0:krows], x[:, c, :], start=(c == 0), stop=(c == NCH - 1))
                if func1 is None:
                    P.copy("act", l1t[0:krows, :], p1[0:krows, 0:TT])
                else:
                    P.act(l1t[0:krows, :], p1[0:krows, 0:TT], func1)
                for tb in range(NTB):
                    if not fp32_out:
                        sg = stg[cnt["stg"] % 2]
                        cnt["stg"] += 1
                    for cg in range(4):
                        p2 = ps[cnt["ps"] % 4]
                        cnt["ps"] += 1
                        P.mm(p2[:], l1t[0:krows + 2, tb * 128:(tb + 1) * 128], wB[0:krows + 2, cg * 512:(cg + 1) * 512])
                        if fp32_out:
                            sf = stf[cnt["stf"] % 2]
                            cnt["stf"] += 1
                            P.act(sf[:], p2[:], AF.Sigmoid)
                            P.dma("sp", dst_dram[t0 + tb * 128:t0 + (tb + 1) * 128, cg * 512:(cg + 1) * 512], sf.t[:], reads=[sf.b])
                        else:
                            P.act(sg[:, 0, :], p2[:], AF.Sigmoid)
                            P.dma("sp", dst_dram[t0 + tb * 128:t0 + (tb + 1) * 128, cg * 512:(cg + 1) * 512], sg.t[:, 0, :], reads=[sg.b])

            wseq = [(gi, cg) for _ in range(ntile) for gi in range(4) for cg in range(4)]

            def issue_w(k):
                if k < len(wseq):
                    gi_, cg_ = wseq[k]
                    col0_ = gi_ * D + cg_ * 512
                    P.dma("sp", wt[k % 3].t[:], Wv[:, :, col0_:col0_ + 512], writes=[wt[k % 3].b])

            load_h(0)
            issue_w(0)
            issue_w(1)
            for ti in range(ntile):
                t0 = ti * TT
                if ti + 1 < ntile:
                    load_h(ti + 1)
                h = hts[ti % 2]
                P.tt("pool", dp[:], h[:, :, 0:TT], h[:, :, 1:TT + 1], ALU.subtract)
                mk_ = tmk[ti % 2]
                P.dma("sp", mk_.t[:], self.in_tokmask[:, t0:t0 + TT], writes=[mk_.b])
                P.tt("pool", dp[:], dp[:], View(mk_.b, mk_.t[:, :].unsqueeze(1).to_broadcast([128, NCH, TT])), ALU.mult)
                P.tt("pool", dn[:], h[:, :, 2:TT + 2], h[:, :, 1:TT + 1], ALU.subtract)
                for g in (0, 2, 3, 5, 1, 4):
                    x = xs[cnt["xs"] % 2]
                    cnt["xs"] += 1
                    for c in range(NCH):
                        tm = tmp[c % 2]
                        P.stt(tm[:], dp[:, c, :], mup[:, g, c:c + 1], h[:, c, 1:TT + 1], ALU.mult, ALU.add)
                        P.stt(x[:, c, :], dn[:, c, :], mun[:, g, c:c + 1], tm[:], ALU.mult, ALU.add)
                    if g in (0, 2, 3, 5):
                        gi = {0: 0, 2: 1, 3: 2, 5: 3}[g]
                        dst = {0: self.rw_r, 2: self.rw_k, 3: self.rw_v[j], 5: self.rw_sz}[g]
                        for cg in range(4):
                            w = wt[cnt["wt"] % 3]
                            issue_w(cnt["wt"] + 2)
                            cnt["wt"] += 1
                            sg = stg[cnt["stg"] % 2]
                            cnt["stg"] += 1
                            for tb in range(NTB):
                                p = ps[cnt["ps"] % 4]
                                cnt["ps"] += 1
                                for c in range(NCH):
                                    P.mm(p[:], x[:, c, tb * 128:(tb + 1) * 128], w[:, c, :], start=(c == 0), stop=(c == NCH - 1))
                                evac(sg[:, tb, :], p[:], AF.Silu if g == 5 else None)
                            P.dma("sp", dst[t0:t0 + TT, cg * 512:(cg + 1) * 512].rearrange("(b p) c -> p b c", p=128), sg.t[:], reads=[sg.b])
                        if g == 3 and vres:
                            lora(x, v1, v2, 64, None, self.rw_gate, False, l1v, t0)
                    elif g == 1:
                        for c in range(2):
                            lt = l1[cnt["l1"] % 2]
                            cnt["l1"] += 1
                            lora(x, w1[c], w2[c], 96, AF.Tanh, self.rw_sw[c], True, lt, t0)
                    else:
                        for c in range(2):
                            lt = l1[cnt["l1"] % 2]
                            cnt["l1"] += 1
                            lora(x, a1[c], a2[c], 96, None, self.rw_a[c], False, lt, t0)
            P.flush()
    def phase_rwkv_prep(self, j):
        P = self.P
        S = self.S
        vres = (j == 1)
        R = self.in_rw
        CDEC = math.exp(-0.5)
        nblk = S // 128
        self.vp_src = self.vp if vres else self.rw_v[j]
        with contextlib.ExitStack() as es:
            kkB = self.load_bcast(es, "kkB", R["rwkv_k_k"][j:j + 1, :])
            kaB = self.load_bcast(es, "kaB", R["rwkv_k_a"][j:j + 1, :])
            rkB = []
            for c in range(2):
                t_ = T(P, es, "rkB%d" % c, [128, D], BF16)
                P.dma("pool", t_.t[:], R["rwkv_r_k"][j, c:c + 1, :].partition_broadcast(128), writes=[t_.b])
                rkB.append(t_)
            tri = T(P, es, "tri", [128, 4, 128], F32)
            P.dma("sp", tri.t[:], self.in_cf[:, 2 * S:2 * S + 512].rearrange("p (a b) -> p a b", b=128), writes=[tri.b])
            ind = T(P, es, "ind", [128, 2], F32)
            P.dma("sp", ind.t[:], self.in_cf[:, 2 * S + 512:2 * S + 514], writes=[ind.b])
            ident = self.cbv(CB_IDENT)
            inb = [T(P, es, "inb%d" % i, [128, 5, D], BF16) for i in range(2)]
            swt = [T(P, es, "swt%d" % c, [128, D], F32) for c in range(2)]
            if vres:
                gv = T(P, es, "gv", [128, 2, D], BF16)
            kkf = T(P, es, "kkf", [128, D], F32)
            sq = T(P, es, "sq", [128, D], F32)
            kk = T(P, es, "kk", [128, D], BF16)
            kkac = T(P, es, "kkac", [128, D], BF16)
            kmk = T(P, es, "kmk", [128, D], BF16)
            vpt = T(P, es, "vpt", [128, D], BF16)
            tb_ = T(P, es, "tb_", [128, D], BF16)
            kd = T(P, es, "kd", [128, D], BF16)
            kka = T(P, es, "kka", [128, D], BF16)
            ub = kkf
            sm = [T(P, es, "sm%d" % i, [128, 4, NHR], F32) for i in range(2)]
            Eb = [T(P, es, "Eb%d" % i, [128, 4, 512], BF16) for i in range(2)]
            prod = [T(P, es, "prod%d" % i, [128, D], BF16) for i in range(6)]
            fts = T(P, es, "fts", [128, 4, 16, 128], BF16)
            gall = [T(P, es, "gall%d" % i, [128, 16, 2], F32) for i in range(4)]
            pc = [T(P, es, "pc%d" % i, [128, 512], F32, psum=True) for i in range(5)]
            ptr = [T(P, es, "ptr%d" % i, [128, 1024], BF16, psum=True) for i in range(2)]
            pg = T(P, es, "pg", [128, 16, 2], F32, psum=True)
            cnt = {"pc": 0, "ptr": 0, "E": 0, "alt": 0}

            def load_in(bi):
                t0 = bi * 128
                ib = inb[bi % 2]
                for q, src in enumerate((self.rw_r, self.rw_k, self.rw_v[j], self.rw_a[0], self.rw_a[1])):
                    P.dma("sp", ib.t[:, q, :], src[t0:t0 + 128, :], writes=[ib.b])

            def alt():
                cnt["alt"] += 1
                return "dve" if cnt["alt"] % 2 == 0 else "pool"

            load_in(0)
            for bi in range(nblk):
                t0 = bi * 128
                if bi + 1 < nblk:
                    load_in(bi + 1)
                ib = inb[bi % 2]
                r_, k_, v_ = ib[:, 0, :], ib[:, 1, :], ib[:, 2, :]
                for c in range(2):
                    P.dma("sp", swt[c].t[:], self.rw_sw[c][t0:t0 + 128, :], writes=[swt[c].b])
                if vres:
                    P.dma("sp", gv.t[:, 0, :], self.rw_gate[t0:t0 + 128, :], writes=[gv.b])
                    P.dma("sp", gv.t[:, 1, :], self.rw_v[0][t0:t0 + 128, :], writes=[gv.b])
                    P.tt("pool", tb_[:], gv[:, 1, :], v_, ALU.subtract)
                    P.tt("pool", tb_[:], tb_[:], gv[:, 0, :], ALU.mult)
                    P.tt("pool", vpt[:], v_, tb_[:], ALU.add)
                    P.dma("sp", self.vp[t0:t0 + 128, :], vpt.t[:], reads=[vpt.b])
                s_ = sm[bi % 2]
                P.tt("pool", kkf[:], k_, kkB[:], ALU.mult)
                P.tt("pool", sq[:], kkf[:], kkf[:], ALU.mult)
                P.add("dve", lambda e, o=s_.t[:, 0, :], i=sq.t[:, :].rearrange("p (h n) -> p h n", n=NR): e.tensor_reduce(out=o, in_=i, axis=AX.X, op=ALU.add),
                      reads=[sq.b], writes=[s_.b])
                P.act(s_[:, 1, :], s_[:, 0, :], AF.Sqrt)
                P.ts("dve", s_[:, 1, :], s_[:, 1, :], 1e-12, ALU.max)
                P.recip(s_[:, 2, :], s_[:, 1, :])
                P.tt("pool", kk.v(kk.t[:, :].rearrange("p (h n) -> p h n", n=NR)), kkf.v(kkf.t[:, :].rearrange("p (h n) -> p h n", n=NR)),
                     s_.v(s_.t[:, 2, :].unsqueeze(2).to_broadcast([128, NHR, NR])), ALU.mult)
                P.tt("pool", kkac[:], k_, kaB[:], ALU.mult)
                P.tt("pool", kmk[:], k_, kkac[:], ALU.subtract)
                for c in range(2):
                    a_ = ib[:, 3 + c, :]
                    P.tt(alt(), tb_[:], a_, kkac[:], ALU.mult)
                    P.tt(alt(), kd[:], tb_[:], kmk[:], ALU.add)
                    P.tt(alt(), kka[:], kk[:], a_, ALU.mult)
                    if c == 0:
                        P.tt(alt(), ub[:], kd[:], rkB[0][:], ALU.mult)
                    else:
                        P.tt(alt(), sq[:], kd[:], rkB[1][:], ALU.mult)
                        P.tt(alt(), ub[:], ub[:], sq[:], ALU.add)
                        P.tt(alt(), ub[:], ub[:], r_, ALU.mult)
                        P.add("dve", lambda e, o=s_.t[:, 3, :], i=ub.t[:, :].rearrange("p (h n) -> p h n", n=NR): e.tensor_reduce(out=o, in_=i, axis=AX.X, op=ALU.add),
                              reads=[ub.b], writes=[s_.b])
                        P.dma("sp", self.bon[t0:t0 + 128, :], s_.t[:, 3, :], reads=[s_.b])
                    mats = (0, 2, 3) if c == 0 else (1, 3, 2)
                    sw = swt[c]
                    for cg in range(4):
                        cs_ = slice(cg * 512, (cg + 1) * 512)
                        pcs = []
                        for mi in mats:
                            p = pc[cnt["pc"] % 5]
                            cnt["pc"] += 1
                            P.mm(p[:], tri[:, mi, :], sw[:, cs_])
                            pcs.append(p)
                        E = Eb[cnt["E"] % 2]
                        cnt["E"] += 1
                        P.act(E[:, 0, :], pcs[0][:], AF.Exp, scale=-CDEC)
                        P.act(E[:, 1, :], pcs[1][:], AF.Exp, scale=-CDEC)
                        P.act(E[:, 2, :], pcs[0][:], AF.Exp, scale=CDEC)
                        P.act(E[:, 3, :], pcs[2][:], AF.Exp, scale=-CDEC)
                        P.tt(alt(), prod[0][:, cs_], kk[:, cs_], E[:, 1, :], ALU.mult)
                        P.tt(alt(), prod[1][:, cs_], ib[:, 0, cs_], E[:, 0, :], ALU.mult)
                        P.tt(alt(), prod[2][:, cs_], kka[:, cs_], E[:, 2, :], ALU.mult)
                        P.tt(alt(), prod[3][:, cs_], kd[:, cs_], E[:, 2, :], ALU.mult)
                        P.tt(alt(), prod[4][:, cs_], kka[:, cs_], E[:, 3, :], ALU.mult)
                        P.tt(alt(), prod[5][:, cs_], kd[:, cs_], E[:, 3, :], ALU.mult)
                    for p_ in range(16):
                        P.mm(pg[:, p_, :], sw[:, p_ * 128:(p_ + 1) * 128], ind[:, :])
                    gt_ = gall[(bi * 2 + c) % 4]
                    P.act(gt_[:], pg[:], AF.Exp, scale=-CDEC)
                    P.dma("act", self.gam[c][:, :, bi * 2:bi * 2 + 2], gt_.t[:], reads=[gt_.b])
                    for q in range(4):
                        for half in range(2):
                            pt = ptr[cnt["ptr"] % 2]
                            cnt["ptr"] += 1
                            for pp in range(8):
                                p_ = half * 8 + pp
                                P.transpose(pt[:, pp * 128:(pp + 1) * 128], prod[q][:, p_ * 128:(p_ + 1) * 128], ident)
                            src = pt.v(pt.t[:, :].rearrange("p (c t) -> p c t", t=128))
                            dst = fts[:, q, half * 8:(half + 1) * 8, :]
                            if cnt["ptr"] % 2 == 0:
                                P.copy("act", dst, src)
                            else:
                                P.copy("dve", dst, src)
                    P.dma("sp", self.ft[c][:, :, :, t0:t0 + 128].rearrange("q p c t -> c q p t"), fts.t[:], reads=[fts.b])
                    P.dma("sp", self.ah[c][t0:t0 + 128, :], prod[4].t[:], reads=[prod[4].b])
                    P.dma("sp", self.kh[c][t0:t0 + 128, :], prod[5].t[:], reads=[prod[5].b])
            P.flush()
    def phase_rwkv_chain(self):
        P = self.P
        S = self.S
        nblk = S // 128
        with contextlib.ExitStack() as es:
            ftl = [T(P, es, "ftl%d" % i, [128, 4, 16, 128], BF16) for i in range(2)]
            amt = [T(P, es, "amt%d" % i, [128, 16, 512], BF16) for i in range(2)]
            Q0 = [T(P, es, "Q0_%d" % i, [128, 32, 64], BF16) for i in range(2)]
            G0 = [T(P, es, "G0_%d" % i, [128, 32, 64], BF16) for i in range(2)]
            N0 = [T(P, es, "N0_%d" % i, [128, 32, 64], BF16) for i in range(2)]
            Pst = [T(P, es, "Pst%d" % i, [128, 32, 64], BF16) for i in range(2)]
            Qst = [T(P, es, "Qst%d" % i, [128, 32, 64], BF16) for i in range(2)]
            Pbd = [T(P, es, "Pbd%d" % i, [128, 32, 128], BF16) for i in range(2)]
            Qbd = [T(P, es, "Qbd%d" % i, [128, 32, 128], BF16) for i in range(2)]
            for t_ in Pbd + Qbd:
                P.memset("pool", t_[:], 0.0)
            Gm = [T(P, es, "Gm%d" % i, [128, 32, 64], BF16) for i in range(2)]
            pb = [T(P, es, "pb%d" % i, [128, 512], F32, psum=True) for i in range(8)]
            cnt = {"pa": 0, "b": 0, "ev": 0}
            seq = [(c, bi) for bi in range(nblk) for c in range(2)]

            def load(i):
                c, bi = seq[i]
                P.dma("sp", ftl[i % 2].t[:], self.ft[c][:, :, :, bi * 128:(bi + 1) * 128].rearrange("q p c t -> c q p t"), writes=[ftl[i % 2].b])

            import os as _os
            def evcopy(dst, src):
                cnt["ev"] += 1
                ev_ = _os.environ.get('EVENG')
                P.copy("dve", dst, src)

            load(0)
            for i, (c, bi) in enumerate(seq):
                if i + 1 < len(seq):
                    load(i + 1)
                f = ftl[i % 2]
                am_ = amt[i % 2]
                q0 = Q0[i % 2]
                g0 = G0[i % 2]
                mk = CB_MF if c == 0 else CB_MB
                lk = CB_LF if c == 0 else CB_LB
                for p2 in range(8):
                    base = (cnt["pa"] % 4) * 2
                    cnt["pa"] += 1
                    for hh in range(2):
                        ps_ = pb[base + hh]
                        kr = slice(hh * 64, (hh + 1) * 64)
                        for pl in range(2):
                            p = p2 * 2 + pl
                            for cc in range(2):
                                tk = slice(cc * 64, (cc + 1) * 64)
                                P.mm(ps_[tk, pl * 256:pl * 256 + 128], f[kr, 2, p, tk], f[kr, 0:2, p, tk])
                                P.mm(ps_[tk, pl * 256 + 128:pl * 256 + 256], f[kr, 3, p, tk], f[kr, 0:2, p, tk])
                        P.tt("dve", am_[:, p2 * 2:p2 * 2 + 2, hh * 256:(hh + 1) * 256], ps_.v(ps_.t[:, :].rearrange("p (a m) -> p a m", a=2)),
                             View(self.cb.b, self.cb.t[:, mk:mk + 256].unsqueeze(1).to_broadcast([128, 2, 256])), ALU.mult)
                q0v = q0.t[:, :, :].rearrange("p (a h) n -> p a h n", h=2)
                import os as _os
                CHS = int(_os.environ.get("CHSTOP", "9"))
                if CHS < 2:
                    continue
                for grp in range(2):
                    base = (cnt["pa"] % 4) * 2
                    cnt["pa"] += 1
                    for hh in range(2):
                        ps_ = pb[base + hh]
                        kr = slice(hh * 64, (hh + 1) * 64)
                        for pl in range(8):
                            p = grp * 8 + pl
                            for cc in range(2):
                                tk = slice(cc * 64, (cc + 1) * 64)
                                P.mm(ps_[tk, pl * 64:(pl + 1) * 64], f[kr, 0, p, tk], f[kr, 2, p, tk])
                        P.tt("dve", View(q0.b, q0v[:, grp * 8:(grp + 1) * 8, hh, :]), ps_.v(ps_.t[:, :].rearrange("p (a m) -> p a m", m=64)),
                             View(self.cb.b, self.cb.t[:, lk:lk + 64].unsqueeze(1).to_broadcast([128, 8, 64])), ALU.mult)
                P.dma("act", self.am[c][bi], am_.t[:], reads=[am_.b])
                if CHS < 3:
                    continue
                nview = am_.t[:, :, :].rearrange("p a (h q n) -> p a h q n", h=2, q=4)[:, :, :, 0, :]
                P.tt("pool", g0.v(g0.t[:, :, :].rearrange("p (a h) n -> p a h n", h=2)),
                     View(self.cb.b, self.cb.t[:, CB_I64:CB_I64 + 64].unsqueeze(1).unsqueeze(1).to_broadcast([128, 16, 2, 64])),
                     View(am_.b, nview), ALU.subtract)

                nst = N0[i % 2]
                nv4 = am_.t[:, :, :].rearrange("p a (h q n) -> p a h q n", h=2, q=4)[:, :, :, 0, :]
                P.copy("act", nst.v(nst.t[:, :, :].rearrange("p (a h) n -> p a h n", h=2)), View(am_.b, nv4))
                for cc in range(2):
                    rows = slice(cc * 64, (cc + 1) * 64)
                    eng_ = "act" if cc == 0 else "pool"
                    P.copy(eng_, Pbd[1][rows, :, cc * 64:(cc + 1) * 64], nst[rows, :, :])
                    P.copy(eng_, Qbd[1][rows, :, cc * 64:(cc + 1) * 64], q0[rows, :, :])
                for bg in range(2):
                    for st in range(6):
                        pst = nst if st == 0 else Pst[(st - 1) % 2]
                        qst = q0 if st == 0 else Qst[(st - 1) % 2]
                        pbd = Pbd[(st - 1) % 2]
                        qbd = Qbd[(st - 1) % 2]
                        gsrc = g0 if st == 1 else Gm[(st - 2) % 2]
                        for bt in (bg * 2, bg * 2 + 1):
                            base = (bt % 2) * 3
                            pP, pQ, pG = pb[base], pb[base + 1], pb[base + 2]
                            bs = slice(bt * 8, bt * 8 + 8)
                            for hl in range(8):
                                hd = bt * 8 + hl
                                cs_ = slice(hl * 64, (hl + 1) * 64)
                                if st <= 3:
                                    P.mm(pP[:, cs_], qbd[:, hd, :], pst[:, hd, :])
                                if st <= 4:
                                    P.mm(pQ[:, cs_], pbd[:, hd, :], qst[:, hd, :])
                                if st >= 1:
                                    P.mm(pG[:, cs_], qbd[:, hd, :], gsrc[:, hd, :])
                            v3 = lambda t_, r_=slice(0, 128): View(t_.b, t_.t[r_, :].rearrange("p (h m) -> p h m", m=64))
                            if st <= 3:
                                P.copy("act", Pst[st % 2][:, bs, :], v3(pP))
                                P.copy("act", Pbd[st % 2][0:64, bs, 0:64], v3(pP, slice(0, 64)))
                                P.copy("dve", Pbd[st % 2][64:128, bs, 64:128], v3(pP, slice(64, 128)))
                            if st <= 4:
                                if st <= 3:
                                    P.copy("act", Qst[st % 2][:, bs, :], v3(pQ))
                                P.copy("act", Qbd[st % 2][0:64, bs, 0:64], v3(pQ, slice(0, 64)))
                                P.copy("dve", Qbd[st % 2][64:128, bs, 64:128], v3(pQ, slice(64, 128)))
                            if st >= 1:
                                P.tt("dve", Gm[(st - 1) % 2][:, bs, :], v3(pG), gsrc[:, bs, :], ALU.add)
                P.dma("act", self.ttm[c][bi], Gm[0].t[:], reads=[Gm[0].b])
            P.flush()

    def phase_rwkv_scan(self):
        P = self.P
        S = self.S
        nblk = S // 128
        with contextlib.ExitStack() as es:
            gl = [T(P, es, "gl%d" % c, [128, 16, S // 64], F32) for c in range(2)]
            for c in range(2):
                P.dma("sp", gl[c].t[:], self.gam[c][:, :, :], writes=[gl[c].b])
            ST = [T(P, es, "ST%d" % c, [128, 16, 64], F32) for c in range(2)]
            STb = [[T(P, es, "STb%d_%d" % (c, i), [128, 16, 2, 64], BF16) for i in range(2)] for c in range(2)]
            for c in range(2):
                P.memset("pool", ST[c][:], 0.0)
                P.memset("pool", STb[c][0][:], 0.0)
                P.memset("pool", STb[c][1][:], 0.0)
            fK = [[T(P, es, "fK%d_%d" % (c, i), [128, 2, 16, 128], BF16) for i in range(2)] for c in range(2)]
            amt = [T(P, es, "samt%d" % c, [128, 16, 512], BF16) for c in range(2)]
            ttl = [[T(P, es, "ttl%d_%d" % (c, i), [128, 32, 64], BF16) for i in range(2)] for c in range(2)]
            akv = [[T(P, es, "akv%d_%d" % (c, i), [128, 3, D], BF16) for i in range(2)] for c in range(2)]
            yst = [T(P, es, "yst%d" % c, [128, D], F32) for c in range(2)]
            NB = 4
            Bt = [T(P, es, "Bt%d" % i, [128, 256], BF16) for i in range(NB)]
            Ut = [T(P, es, "Ut%d" % i, [128, 256], BF16) for i in range(NB)]
            tmpS = [T(P, es, "tmpS%d" % i, [128, 2, 64], F32) for i in range(NB)]
            pBU = [T(P, es, "pBU%d" % i, [128, 512], F32, psum=True) for i in range(4)]
            pMY = [T(P, es, "pMY%d" % i, [128, 512], F32, psum=True) for i in range(4)]

            def load(c, step):
                b = step if c == 0 else nblk - 1 - step
                t0 = b * 128
                sl = step % 2
                P.dma("sp", fK[c][sl].t[:], self.ft[c][0:2, :, :, t0:t0 + 128].rearrange("q p c t -> c q p t"), writes=[fK[c][sl].b])
                P.dma("sp", ttl[c][sl].t[:], self.ttm[c][b], writes=[ttl[c][sl].b])
                P.dma("sp", akv[c][sl].t[:, 0, :], self.ah[c][t0:t0 + 128, :], writes=[akv[c][sl].b])
                P.dma("sp", akv[c][sl].t[:, 1, :], self.kh[c][t0:t0 + 128, :], writes=[akv[c][sl].b])
                P.dma("sp", akv[c][sl].t[:, 2, :], self.vp_src[t0:t0 + 128, :], writes=[akv[c][sl].b])

            for c in range(2):
                load(c, 0)
            kctr = [0]
            for step in range(nblk):
                for c in range(2):
                    if step + 1 < nblk:
                        load(c, step + 1)
                    b = step if c == 0 else nblk - 1 - step
                    P.dma("sp", amt[c].t[:], self.am[c][b], writes=[amt[c].b])
                for ci_ in range(2):
                    combos = []
                    for e8 in range(8):
                        for c in range(2):
                            b = step if c == 0 else nblk - 1 - step
                            cc = ci_ if c == 0 else 1 - ci_
                            combos.append((c, b, cc, b * 2 + cc, e8))
                    nco = len(combos)
                    ctx = {}

                    def stageA(k):
                        c, b, cc, gch, e8 = combos[k]
                        sl = step % 2
                        R_ = slice(cc * 64, (cc + 1) * 64)
                        kk_ = kctr[0]
                        kctr[0] += 1
                        pbu = pBU[kk_ % 4]
                        pmy = pMY[kk_ % 4]
                        bt_ = Bt[kk_ % NB]
                        ut_ = Ut[kk_ % NB]
                        ctx[k] = (pbu, pmy, bt_, ut_, kk_)
                        stb = STb[c][(step * 2 + ci_) % 2]
                        for pl in range(2):
                            p = e8 * 2 + pl
                            P.mm(pbu[R_, pl * 128:(pl + 1) * 128], fK[c][sl][:, 0, p, R_], stb[:, p, :, :], start=True, stop=False)
                            for hh in range(2):
                                hd = p * 2 + hh
                                i = pl * 2 + hh
                                P.mm(pbu[R_, i * 64:(i + 1) * 64], amt[c][R_, p, hh * 256 + 128:hh * 256 + 192],
                                     akv[c][sl][R_, 2, hd * 64:(hd + 1) * 64], start=False, stop=(hh == 1))
                        P.copy("dve", bt_[R_, :], pbu[R_, 0:256])

                    def stageA2(k):
                        c, b, cc, gch, e8 = combos[k]
                        sl = step % 2
                        R_ = slice(cc * 64, (cc + 1) * 64)
                        pbu, pmy, bt_, ut_, kk_ = ctx[k]
                        for i in range(4):
                            hd = e8 * 4 + i
                            P.mm(pbu[R_, 256 + i * 64:256 + (i + 1) * 64], ttl[c][sl][R_, hd, :], bt_[R_, i * 64:(i + 1) * 64])
                        P.ts("dve", ut_[R_, :], pbu[R_, 256:512], -1.0, ALU.mult)

                    def stageC(k):
                        c, b, cc, gch, e8 = combos[k]
                        sl = step % 2
                        R_ = slice(cc * 64, (cc + 1) * 64)
                        pbu, pmy, bt_, ut_, kk_ = ctx[k]
                        stb = STb[c][(step * 2 + ci_) % 2]
                        stn = STb[c][(step * 2 + ci_ + 1) % 2]
                        a_ = akv[c][sl]
                        for i in range(4):
                            hd = e8 * 4 + i
                            p, hh = hd // 2, hd % 2
                            kr = slice(hh * 64, (hh + 1) * 64)
                            hc = slice(hd * 64, (hd + 1) * 64)
                            o = pmy[kr, (i // 2) * 64:(i // 2 + 1) * 64]
                            P.mm(o, a_[R_, 1, hc], a_[R_, 2, hc], start=True, stop=False)
                            P.mm(o, a_[R_, 0, hc], ut_[R_, i * 64:(i + 1) * 64], start=False, stop=True)
                        for pl in range(2):
                            p = e8 * 2 + pl
                            P.mm(pmy[R_, 256 + pl * 128:256 + (pl + 1) * 128], fK[c][sl][:, 1, p, R_], stb[:, p, :, :], start=True, stop=False)
                            for hh in range(2):
                                hd = p * 2 + hh
                                i = pl * 2 + hh
                                hc = slice(hd * 64, (hd + 1) * 64)
                                o = pmy[R_, 256 + i * 64:256 + (i + 1) * 64]
                                P.mm(o, amt[c][R_, p, hh * 256 + 64:hh * 256 + 128], ut_[R_, i * 64:(i + 1) * 64], start=False, stop=False)
                                P.mm(o, amt[c][R_, p, hh * 256 + 192:hh * 256 + 256], a_[R_, 2, hc], start=False, stop=(hh == 1))
                        P.copy("dve", yst[c][R_, e8 * 256:(e8 + 1) * 256], pmy[R_, 256:512])
                        ps2 = slice(e8 * 2, e8 * 2 + 2)
                        tm = tmpS[kk_ % NB]
                        P.tt("dve", tm[:], ST[c][:, ps2, :], View(gl[c].b, gl[c].t[:, ps2, gch:gch + 1].broadcast_to([128, 2, 64])), ALU.mult)
                        P.tt("dve", ST[c][:, ps2, :], tm[:], pmy.v(pmy.t[:, 0:128].rearrange("p (a v) -> p a v", v=64)), ALU.add)
                        for hh in range(2):
                            kr = slice(hh * 64, (hh + 1) * 64)
                            P.copy("pool", stn[kr, ps2, hh, :], ST[c][kr, ps2, :])

                    for k in range(nco + 2):
                        if k < nco:
                            stageA(k)
                        if 0 <= k - 1 < nco:
                            stageA2(k - 1)
                        if 0 <= k - 2 < nco:
                            stageC(k - 2)
                for c in range(2):
                    b = step if c == 0 else nblk - 1 - step
                    P.dma("act", self.yd[c][b * 128:(b + 1) * 128, :], yst[c].t[:], reads=[yst[c].b])
            P.flush()

    def phase_rwkv_post(self, j):
        P = self.P
        S = self.S
        R = self.in_rw
        nblk = S // 128
        with contextlib.ExitStack() as es:
            gG = self.load_bcast(es, "gnG", R["rwkv_gn_g"][j:j + 1, :])
            gB = self.load_bcast(es, "gnB", R["rwkv_gn_b"][j:j + 1, :])
            yin = [T(P, es, "yin%d" % i, [128, 2, D], F32) for i in range(2)]
            vz = [T(P, es, "vz%d" % i, [128, 2, D], BF16) for i in range(2)]
            bn = [T(P, es, "bn%d" % i, [128, NHR], F32) for i in range(2)]
            y = T(P, es, "ypost", [128, D], F32)
            sq = T(P, es, "sqpost", [128, D], F32)
            yo = [T(P, es, "yo%d" % i, [128, D], BF16) for i in range(2)]
            sm = [T(P, es, "smp%d" % i, [128, 6, NHR], F32) for i in range(2)]
            tp = [T(P, es, "tpp%d" % i, [128, 1024], BF16, psum=True) for i in range(4)]
            hs = [T(P, es, "hsp%d" % i, [128, NCH, 512], BF16) for i in range(2)]
            ident = self.cbv(CB_IDENT)

            def load(bi):
                t0 = bi * 128
                sl = bi % 2
                P.dma("sp", yin[sl].t[:, 0, :], self.yd[0][t0:t0 + 128, :], writes=[yin[sl].b])
                P.dma("sp", yin[sl].t[:, 1, :], self.yd[1][t0:t0 + 128, :], writes=[yin[sl].b])
                P.dma("sp", vz[sl].t[:, 0, :], self.vp_src[t0:t0 + 128, :], writes=[vz[sl].b])
                P.dma("sp", vz[sl].t[:, 1, :], self.rw_sz[t0:t0 + 128, :], writes=[vz[sl].b])
                P.dma("sp", bn[sl].t[:], self.bon[t0:t0 + 128, :], writes=[bn[sl].b])

            def v3(t_, ap=None):
                a = t_.t[:, :] if ap is None else ap
                return View(t_.b, a.rearrange("p (h n) -> p h n", n=NR))

            def bc(t_, ap):
                return View(t_.b, ap.unsqueeze(2).to_broadcast([128, NHR, NR]))

            load(0)
            k = 0
            for bi in range(nblk):
                if bi + 1 < nblk:
                    load(bi + 1)
                sl = bi % 2
                s_ = sm[sl]
                P.tt("pool", y[:], yin[sl][:, 0, :], yin[sl][:, 1, :], ALU.add)
                P.tt("pool", sq[:], y[:], y[:], ALU.mult)
                P.add("dve", lambda e, o=s_.t[:, 0, :], i=y.t[:, :].rearrange("p (h n) -> p h n", n=NR): e.tensor_reduce(out=o, in_=i, axis=AX.X, op=ALU.add),
                      reads=[y.b], writes=[s_.b])
                P.add("dve", lambda e, o=s_.t[:, 1, :], i=sq.t[:, :].rearrange("p (h n) -> p h n", n=NR): e.tensor_reduce(out=o, in_=i, axis=AX.X, op=ALU.add),
                      reads=[sq.b], writes=[s_.b])
                P.ts("dve", s_[:, 2, :], s_[:, 0, :], 1.0 / NR, ALU.mult)
                P.tt("dve", s_[:, 3, :], s_[:, 2, :], s_[:, 2, :], ALU.mult)
                P.stt(s_[:, 4, :], s_[:, 1, :], 1.0 / NR, s_[:, 3, :], ALU.mult, ALU.subtract)
                P.ts("dve", s_[:, 4, :], s_[:, 4, :], GN_EPS, ALU.add)
                P.act(s_[:, 5, :], s_[:, 4, :], AF.Sqrt)
                P.recip(s_[:, 4, :], s_[:, 5, :])
                P.tt("pool", v3(y), v3(y), bc(s_, s_.t[:, 2, :]), ALU.subtract)
                P.tt("pool", v3(y), v3(y), bc(s_, s_.t[:, 4, :]), ALU.mult)
                P.tt("dve", y[:], y[:], gG[:], ALU.mult)
                P.tt("dve", y[:], y[:], gB[:], ALU.add)
                P.tt("pool", v3(sq), View(vz[sl].b, vz[sl].t[:, 0, :].rearrange("p (h n) -> p h n", n=NR)), bc(bn[sl], bn[sl].t[:, :]), ALU.mult)
                P.tt("pool", y[:], y[:], sq[:], ALU.add)
                o = yo[bi % 2]
                P.tt("dve", o[:], y[:], vz[sl][:, 1, :], ALU.mult)
                u = bi // 4
                sub = bi % 4
                hst = hs[u % 2]
                for half in range(2):
                    pt = tp[k % 4]
                    k += 1
                    for c in range(8):
                        cc = half * 8 + c
                        P.transpose(pt[:, c * 128:(c + 1) * 128], o[:, cc * 128:(cc + 1) * 128], ident)
                    src = pt.v(pt.t[:, :].rearrange("p (c t) -> p c t", t=128))
                    dst = hst[:, half * 8:(half + 1) * 8, sub * 128:(sub + 1) * 128]
                    P.copy("act", dst, src)
                if sub == 3 or bi == nblk - 1:
                    t0 = u * 512
                    n = (sub + 1) * 128
                    P.dma("sp", self.yT.rearrange("(c p) t -> p c t", p=128)[:, :, t0:t0 + n], hst.t[:, :, 0:n], reads=[hst.b])
            P.flush()


S_FULL = 8192
_CACHE = {}


def kernel(**inputs):
    S = S_FULL
    x_prompt = np.asarray(inputs["x_prompt"], dtype=np.float32)
    x_sample = np.asarray(inputs["x_sample"], dtype=np.float32)
    b = Builder(S)
    nc = b.build()
    cb, cf = _const_tables(S)
    xs = []
    valid_lens = []
    xs.append(np.ascontiguousarray(x_prompt[0]))
    valid_lens.append(S)
    for i in range(4):
        xp = np.zeros((S, D), np.float32)
        xp[:4096] = x_sample[i]
        xs.append(xp)
        valid_lens.append(4096)
    for i in range(3):
        xs.append(np.zeros((S, D), np.float32))
        valid_lens.append(S)
    shared = {}
    for k, v in inputs.items():
        if k in ("x_prompt", "x_sample"):
            continue
        a = np.ascontiguousarray(np.asarray(v, dtype=np.float32))
        if k == "rwkv_r_k":
            a = a.reshape(2, 2, D)
        shared[k] = a
    in_maps = []
    for c in range(8):
        m = dict(shared)
        m["x"] = xs[c]
        m["const_bf"] = cb
        m["const_f32"] = cf
        m["valid"] = _valid_tables(S, valid_lens[c])
        m["tokmask"] = _tokmask(S, valid_lens[c])
        in_maps.append(m)
    res = run_bass_kernel_spmd(nc, in_maps, core_ids=list(range(8)))
    y_prompt = np.asarray(res.results[0]["y"], dtype=np.float32)[None]
    y_sample = np.stack([np.asarray(res.results[1 + i]["y"], dtype=np.float32)[:4096] for i in range(4)], axis=0)
    return (y_prompt, y_sample)
```

```python
import contextlib
import math
import numpy as np
import ml_dtypes
import concourse.bass as bass
import concourse.mybir as mybir
from concourse.bass_utils import run_bass_kernel_spmd

F32 = mybir.dt.float32
BF16 = mybir.dt.bfloat16
AF = mybir.ActivationFunctionType
ALU = mybir.AluOpType
AX = mybir.AxisListType

D = 2048
NCH = 16
NH_ATT = 16
DH = 128
GROUPS = ((128, 1), (512, 4), (2048, 16))
ATT_COLS = 20480
RMS_EPS = 1e-6
GN_EPS = 64e-5
NHR = 32
NR = 64
CH = 64
SAME_ENGINE_SYNC = True


class Buf:
    __slots__ = ("name", "w", "r")

    def __init__(self, name):
        self.name = name
        self.w = None
        self.r = []


class View:
    __slots__ = ("b", "ap")

    def __init__(self, b, ap):
        self.b = b
        self.ap = ap


class T:
    def __init__(self, prog, es, name, shape, dtype, psum=False):
        nc = prog.nc
        prog.uid += 1
        name = "%s_u%d" % (name, prog.uid)
        self.t = es.enter_context(nc.psum_tensor(name, shape, dtype) if psum else nc.sbuf_tensor(name, shape, dtype))
        self.b = Buf(name)
        prog.bufs.append(self.b)

    def __getitem__(self, idx):
        return View(self.b, self.t[idx])

    def v(self, ap):
        return View(self.b, ap)


class Op:
    __slots__ = ("eng", "fn", "deps", "is_dma", "needs_inc", "ms", "dma_idx")


class Prog:
    CE = ("pe", "act", "dve", "pool")
    QS = ("sp", "act", "pool")

    def __init__(self, nc, es, K=8):
        self.nc = nc
        self.engs = {"pe": nc.tensor, "act": nc.scalar, "dve": nc.vector, "pool": nc.gpsimd, "sp": nc.sync}
        self.sem = {e: es.enter_context(nc.semaphore("sem_" + e)) for e in self.CE}
        self.K = K
        self.dsem = {q: [es.enter_context(nc.semaphore("dsem_%s%d" % (q, i))) for i in range(K)] for q in self.QS}
        self.ms = {e: 0 for e in self.CE}
        self.dcount = {q: 0 for q in self.QS}
        self.seen = {}
        self.ops = []
        self.bufs = []
        self.n_instr = 0
        self.uid = 0

    def buf(self, name):
        b = Buf(name)
        self.bufs.append(b)
        return b

    def add(self, eng, fn, reads=(), writes=(), dma=False):
        op = Op()
        op.eng = eng
        op.fn = fn
        op.is_dma = dma
        op.needs_inc = False
        op.ms = 0
        op.dma_idx = -1
        deps = {}
        for b in reads:
            if b.w is not None:
                deps[id(b.w)] = b.w
        for b in writes:
            if b.w is not None:
                deps[id(b.w)] = b.w
            for o in b.r:
                deps[id(o)] = o
        for b in reads:
            if dma:
                b.r.append(op)
            else:
                b.r = [o for o in b.r if o.is_dma or o.eng != eng]
                b.r.append(op)
        for b in writes:
            b.w = op
            b.r = []
        dl = []
        for d in deps.values():
            if d is op:
                continue
            if (not d.is_dma) and (not dma) and d.eng == eng:
                if eng == "pe" or not SAME_ENGINE_SYNC:
                    continue
            dl.append(d)
            if not d.is_dma:
                d.needs_inc = True
        op.deps = dl
        if dma:
            op.dma_idx = self.dcount[eng]
            self.dcount[eng] += 1
        self.ops.append(op)
        return op

    def _wait(self, eng, key, sem, val):
        k = (eng, key)
        if self.seen.get(k, 0) >= val:
            return
        self.seen[k] = val
        self.engs[eng].wait_ge(sem, val)

    def flush(self):
        K = self.K
        last = {}
        for op in self.ops:
            if not op.is_dma:
                last[op.eng] = op
        for op in last.values():
            op.needs_inc = True
        for op in self.ops:
            e = self.engs[op.eng]
            if op.is_dma:
                i = op.dma_idx
                q = op.eng
                if i >= K:
                    self._wait(q, ("d", q, i % K), self.dsem[q][i % K], 16 * (i // K))
            for d in op.deps:
                if d.is_dma:
                    j = d.dma_idx
                    self._wait(op.eng, ("d", d.eng, j % K), self.dsem[d.eng][j % K], 16 * (j // K + 1))
                else:
                    self._wait(op.eng, ("c", d.eng), self.sem[d.eng], d.ms)
            ins = op.fn(e)
            self.n_instr += 1
            if op.is_dma:
                ins.then_inc(self.dsem[op.eng][op.dma_idx % K], 16)
            elif op.needs_inc:
                self.ms[op.eng] += 1
                op.ms = self.ms[op.eng]
                ins.then_inc(self.sem[op.eng], 1)
        for eng in ("pe", "act", "dve", "pool", "sp"):
            for c in self.CE:
                if self.ms[c] > 0:
                    self._wait(eng, ("c", c), self.sem[c], self.ms[c])
            for q in self.QS:
                n = self.dcount[q]
                for j in range(K):
                    cnt = (n - j + K - 1) // K if n > j else 0
                    if cnt > 0:
                        self._wait(eng, ("d", q, j), self.dsem[q][j], 16 * cnt)
        for b in self.bufs:
            b.w = None
            b.r = []
        self.bufs = [b for b in self.bufs if not b.name.startswith("~")]
        self.ops = []

    def mm(self, out, lhsT, rhs, start=True, stop=True, **kw):
        return self.add("pe", lambda e: e.matmul(out.ap, lhsT.ap, rhs.ap, start=start, stop=stop, **kw),
                        reads=[lhsT.b, rhs.b], writes=[out.b])

    def transpose(self, out, in_, ident):
        return self.add("pe", lambda e: e.transpose(out.ap, in_.ap, ident.ap), reads=[in_.b, ident.b], writes=[out.b])

    def act(self, out, in_, func, bias=None, scale=None, accum_out=None, extra_reads=()):
        kw = {}
        reads = [in_.b] + list(extra_reads)
        writes = [out.b]
        if bias is not None:
            if isinstance(bias, View):
                kw["bias"] = bias.ap
                reads.append(bias.b)
            else:
                kw["bias"] = bias
        if scale is not None:
            if isinstance(scale, View):
                kw["scale"] = scale.ap
                reads.append(scale.b)
            else:
                kw["scale"] = scale
        if accum_out is not None:
            kw["accum_out"] = accum_out.ap
            writes.append(accum_out.b)
        return self.add("act", lambda e: e.activation(out=out.ap, in_=in_.ap, func=func, **kw), reads=reads, writes=writes)

    def tt(self, eng, out, in0, in1, op):
        return self.add(eng, lambda e: e.tensor_tensor(out=out.ap, in0=in0.ap, in1=in1.ap, op=op),
                        reads=[in0.b, in1.b], writes=[out.b])

    def ts(self, eng, out, in0, s1, op0, s2=None, op1=None):
        reads = [in0.b]
        a1 = s1
        a2 = s2
        if isinstance(s1, View):
            a1 = s1.ap
            reads.append(s1.b)
        if isinstance(s2, View):
            a2 = s2.ap
            reads.append(s2.b)
        if op1 is None:
            return self.add(eng, lambda e: e.tensor_scalar(out=out.ap, in0=in0.ap, scalar1=a1, scalar2=None, op0=op0),
                            reads=reads, writes=[out.b])
        return self.add(eng, lambda e: e.tensor_scalar(out=out.ap, in0=in0.ap, scalar1=a1, scalar2=a2, op0=op0, op1=op1),
                        reads=reads, writes=[out.b])

    def stt(self, out, in0, scalar, in1, op0, op1):
        reads = [in0.b, in1.b]
        sc = scalar
        if isinstance(scalar, View):
            sc = scalar.ap
            reads.append(scalar.b)
        return self.add("dve", lambda e: e.scalar_tensor_tensor(out=out.ap, in0=in0.ap, scalar=sc, in1=in1.ap, op0=op0, op1=op1),
                        reads=reads, writes=[out.b])

    def copy(self, eng, out, in_):
        if eng == "act":
            return self.add("act", lambda e: e.copy(out=out.ap, in_=in_.ap), reads=[in_.b], writes=[out.b])
        return self.add(eng, lambda e: e.tensor_copy(out=out.ap, in_=in_.ap), reads=[in_.b], writes=[out.b])

    def recip(self, out, in_):
        return self.add("dve", lambda e: e.reciprocal(out=out.ap, in_=in_.ap), reads=[in_.b], writes=[out.b])

    def memset(self, eng, out, val):
        return self.add(eng, lambda e: e.memset(out.ap, val), reads=[], writes=[out.b])

    def dma(self, q, out, in_, reads=(), writes=(), **kw):
        return self.add(q, lambda e: e.dma_start(out=out, in_=in_, **kw), reads=list(reads), writes=list(writes), dma=True)


def _const_tables(S):
    bf = ml_dtypes.bfloat16
    ident = np.eye(128, dtype=np.float32)
    rot = np.zeros((128, 128), np.float32)
    for m in range(64):
        rot[m + 64, m] = -1.0
    for m in range(64, 128):
        rot[m - 64, m] = 1.0
    i = np.arange(128)[:, None]
    j = np.arange(128)[None, :]
    m0 = (i >= j).astype(np.float32)
    m1 = (i <= j).astype(np.float32)
    ones = np.ones((128, 128), np.float32)
    s = np.arange(64)[:, None]
    t = np.arange(64)[None, :]
    su = (s < t).astype(np.float32)
    iu = (s <= t).astype(np.float32)
    sl = (s > t).astype(np.float32)
    il = (s >= t).astype(np.float32)
    mf = np.tile(np.concatenate([su, iu, su, iu], axis=1), (2, 1))
    mb = np.tile(np.concatenate([sl, il, sl, il], axis=1), (2, 1))
    lf = np.tile(np.concatenate([sl, sl], axis=1), (2, 1))
    lb = np.tile(np.concatenate([su, su], axis=1), (2, 1))
    i64 = np.tile(np.concatenate([np.eye(64, dtype=np.float32)] * 2, axis=1), (2, 1))
    cb = np.concatenate([ident, rot, m0, m1, ones, mf, mb, lf, lb, i64], axis=1).astype(bf)
    half = 64
    inv_freq = (1.0 / (np.float32(10000.0) ** (np.arange(half, dtype=np.float32) * np.float32(2.0) / np.float32(128)))).astype(np.float32)
    ang = (np.arange(S, dtype=np.float32)[:, None] * inv_freq[None, :]).astype(np.float32)
    cos = np.cos(ang).astype(np.float32).T
    sin = np.sin(ang).astype(np.float32).T
    cs = np.concatenate([np.concatenate([cos, cos], 0), np.concatenate([sin, sin], 0)], axis=1)
    s2 = np.arange(128)[:, None]
    t2 = np.arange(128)[None, :]
    same = (s2 // 64) == (t2 // 64)
    p_i = (same & (s2 <= t2)).astype(np.float32)
    s_i = (same & (s2 >= t2)).astype(np.float32)
    p_s = (same & (s2 < t2)).astype(np.float32)
    s_s = (same & (s2 > t2)).astype(np.float32)
    ind = np.zeros((128, 2), np.float32)
    ind[:64, 0] = 1.0
    ind[64:, 1] = 1.0
    cf = np.concatenate([cs, p_i, s_i, p_s, s_s, ind], axis=1).astype(np.float32)
    return cb, cf


CB_IDENT, CB_ROT, CB_M0, CB_M1, CB_ONES, CB_MF, CB_MB, CB_LF, CB_LB, CB_I64 = 0, 128, 256, 384, 512, 640, 896, 1152, 1280, 1408
CB_W = 1536


def _valid_tables(S, valid_len):
    cols = []
    for (_, d) in GROUPS:
        L = S // d
        nblk = L // 128 + 1
        for r in range(d):
            n = np.arange(nblk * 128) - 64
            pos = n * d + r
            ok = (n >= 0) & (n < L) & (pos < valid_len)
            cols.append(ok.reshape(nblk, 128).T.astype(np.float32))
    return np.concatenate(cols, axis=1).astype(ml_dtypes.bfloat16)


def _tokmask(S, valid_len):
    return np.broadcast_to((np.arange(S) < valid_len).astype(np.float32)[None, :], (128, S)).astype(ml_dtypes.bfloat16).copy()


class Builder:
    def __init__(self, S, n_layers=4, debug=None, debug_out=(), rw_stop=99):
        self.debug_out = set(debug_out)
        self.rw_stop = rw_stop
        self.S = S
        self.n_layers = n_layers
        self.debug = debug or {}
        self.nc = bass.Bass("TRN2", target_bir_lowering=False)
        self.es = contextlib.ExitStack()

    def dram(self, name, shape, dtype, kind="Internal"):
        if name in self.debug_out:
            kind = "ExternalOutput"
        return self.nc.dram_tensor(name, list(shape), dtype, kind=kind).ap()

    def build(self):
        nc = self.nc
        S = self.S
        with self.es as es:
            self.P = Prog(nc, es)
            self.declare_io()
            self.persistent(es)
            self.prologue()
            x_cur = self.x_in
            bufs = [self.xa, self.xb]
            for layer in range(self.n_layers):
                import os as _os
                if int(_os.environ.get("ATTSTOP", "9")) == 0:
                    break
                x_next = self.y_out if layer == self.n_layers - 1 else bufs[layer % 2]
                j = layer // 2
                self.phase_norm(x_cur, layer)
                if layer % 2 == 0:
                    import os as _os
                    _as = int(_os.environ.get("ATTSTOP", "9"))
                    if _as >= 2:
                        self.phase_att_inproj(j)
                    if _as >= 3:
                        self.phase_att_core()
                    if _as >= 4:
                        self.phase_out(x_cur, x_next, self.wb_att_out[j], layer)
                else:
                    self.phase_rwkv(j)
                    self.phase_out(x_cur, x_next, self.wb_rwkv_out[j], layer)
                x_cur = x_next
        return nc

    def declare_io(self):
        S = self.S
        d = self.dram
        self.x_in = d("x", [S, D], F32, "ExternalInput")
        self.y_out = d("y", [S, D], F32, "ExternalOutput")
        self.in_norm_pre = d("norm_pre", [4, D], F32, "ExternalInput")
        self.in_norm_post = d("norm_post", [4, D], F32, "ExternalInput")
        self.in_att_w_in = d("att_w_in", [2, D, ATT_COLS], F32, "ExternalInput")
        self.in_att_w_out = d("att_w_out", [2, D, D], F32, "ExternalInput")
        self.in_rw = {}
        for name, shape in (("rwkv_mu_prev", [2, 6, D]), ("rwkv_mu_next", [2, 6, D]), ("rwkv_w_in", [2, D, 4 * D]),
                            ("rwkv_w0", [2, 2, D]), ("rwkv_w1", [2, 2, D, 96]), ("rwkv_w2", [2, 2, 96, D]),
                            ("rwkv_a0", [2, 2, D]), ("rwkv_a1", [2, 2, D, 96]), ("rwkv_a2", [2, 2, 96, D]),
                            ("rwkv_v0", [1, D]), ("rwkv_v1", [1, D, 64]), ("rwkv_v2", [1, 64, D]),
                            ("rwkv_k_k", [2, D]), ("rwkv_k_a", [2, D]), ("rwkv_r_k", [2, 2, D]),
                            ("rwkv_gn_g", [2, D]), ("rwkv_gn_b", [2, D]), ("rwkv_w_out", [2, D, D])):
            self.in_rw[name] = d(name, shape, F32, "ExternalInput")
        self.in_cb = d("const_bf", [128, CB_W], BF16, "ExternalInput")
        self.cf_w = 2 * S + 4 * 128 + 2
        self.in_cf = d("const_f32", [128, self.cf_w], F32, "ExternalInput")
        self.nvalid = sum(dd * ((S // dd) // 128 + 1) for (_, dd) in GROUPS)
        self.in_valid = d("valid", [128, self.nvalid], BF16, "ExternalInput")
        self.in_tokmask = d("tokmask", [128, S], BF16, "ExternalInput")
        self.xa = d("xa", [S, D], F32)
        self.xb = d("xb", [S, D], F32)
        self.hT = d("hT", [NCH, 128, S + 2], BF16)
        self.wb_att_in = [d("wb_att_in%d" % j, [D, ATT_COLS], BF16) for j in range(2)]
        self.wb_att_out = [d("wb_att_out%d" % j, [D, D], BF16) for j in range(2)]
        self.wb_rwkv_in = [d("wb_rwkv_in%d" % j, [D, 4 * D], BF16) for j in range(2)]
        self.wb_rwkv_out = [d("wb_rwkv_out%d" % j, [D, D], BF16) for j in range(2)]
        self.qT = []
        self.kT = []
        self.vv = []
        for g, (_, dd) in enumerate(GROUPS):
            L = S // dd
            self.qT.append(d("qT%d" % g, [NH_ATT, 128, dd, L], BF16))
            self.kT.append(d("kT%d" % g, [NH_ATT, 128, dd, L + 128], BF16))
            self.vv.append(d("vv%d" % g, [dd, L + 128, D], BF16))
        self.zT = d("zT", [D, S], BF16)
        self.yT = d("yT", [D, S], BF16)
        self.declare_rwkv()
        for name, (shape, dt) in self.debug.items():
            setattr(self, "dbg_" + name, d("dbg_" + name, shape, dt, "ExternalOutput"))

    def persistent(self, es):
        P = self.P
        self.cb = T(P, es, "cb", [128, CB_W], BF16)
        P.dma("sp", self.cb.t[:], self.in_cb[:, :], writes=[self.cb.b])
        self.zeros = T(P, es, "zeros", [128, 2048], BF16)
        P.memset("pool", self.zeros[:], 0.0)
        P.flush()

    def cbv(self, off, w=128, rows=128):
        return self.cb[0:rows, off:off + w]

    def prologue(self):
        P = self.P
        S = self.S

        def cast(dst, src, rows, cols):
            cw = min(cols, 2048)
            nseg = cols // cw
            rstep = max(1, 4096 // nseg)
            for r0 in range(0, rows, rstep):
                r1 = min(rows, r0 + rstep)
                if nseg == 1:
                    o = dst[r0:r1, :]
                    i = src[r0:r1, :]
                else:
                    o = dst[r0:r1, :].rearrange("r (s c) -> r s c", c=cw)
                    i = src[r0:r1, :].rearrange("r (s c) -> r s c", c=cw)
                P.dma("pool", o, i)

        nl = self.n_layers
        for j in range(2):
            if nl > 2 * j:
                cast(self.wb_att_in[j], self.in_att_w_in[j], D, ATT_COLS)
                cast(self.wb_att_out[j], self.in_att_w_out[j], D, D)
            if nl > 2 * j + 1:
                cast(self.wb_rwkv_in[j], self.in_rw["rwkv_w_in"][j], D, 4 * D)
                cast(self.wb_rwkv_out[j], self.in_rw["rwkv_w_out"][j], D, D)
        if nl > 1:
            cast(self.wb_w1, self.in_rw["rwkv_w1"].rearrange("j c d r -> (j c d) r"), 4 * D, 96)
            cast(self.wb_a1, self.in_rw["rwkv_a1"].rearrange("j c d r -> (j c d) r"), 4 * D, 96)
            cast(self.wb_w2, self.in_rw["rwkv_w2"].rearrange("j c r d -> (j c r) d"), 4 * 96, D)
            cast(self.wb_a2, self.in_rw["rwkv_a2"].rearrange("j c r d -> (j c r) d"), 4 * 96, D)
            cast(self.wb_v1, self.in_rw["rwkv_v1"][0], D, 64)
            cast(self.wb_v2, self.in_rw["rwkv_v2"][0], 64, D)
        for g, (_, dd) in enumerate(GROUPS):
            L = S // dd
            for h in range(NH_ATT):
                for side in (0, L + 64):
                    P.dma("sp", self.kT[g][h][:, :, side:side + 64],
                          self.zeros.t[:, 0:dd * 64].rearrange("p (r n) -> p r n", n=64), reads=[self.zeros.b])
            for r in range(dd):
                for side in (0, L + 64):
                    P.dma("sp", self.vv[g][r, side:side + 64, :], self.zeros.t[0:64, :], reads=[self.zeros.b])
        P.flush()

    def load_bcast(self, es, name, src_row):
        P = self.P
        t = T(P, es, name, [128, D], F32)
        P.dma("sp", t.t[:], src_row.partition_broadcast(128), writes=[t.b])
        return t

    def rstd_from_ss(self, ss, tmp, rstd, eps, n):
        P = self.P
        P.act(tmp, ss, AF.Sqrt, bias=self.eps_t[eps], scale=1.0 / n)
        P.recip(rstd, tmp)

    def phase_norm(self, x_src, layer):
        P = self.P
        S = self.S
        with contextlib.ExitStack() as es:
            g = self.load_bcast(es, "g_pre", self.in_norm_pre[layer:layer + 1, :])
            self.eps_t = {}
            epst = T(P, es, "epst", [128, 2], F32)
            P.memset("pool", epst[:, 0:1], RMS_EPS)
            self.eps_t[RMS_EPS] = epst[:, 0:1]
            NS = 3
            xt = [T(P, es, "xt%d" % i, [128, D], F32) for i in range(NS)]
            junk = T(P, es, "junk", [128, D], BF16)
            hb = [T(P, es, "hb%d" % i, [128, D], BF16) for i in range(2)]
            st = [T(P, es, "st%d" % i, [128, 3], F32) for i in range(4)]
            tp = [T(P, es, "tp%d" % i, [128, 1024], BF16, psum=True) for i in range(4)]
            hs = [T(P, es, "hs%d" % i, [128, NCH, 512], BF16) for i in range(2)]
            ident = self.cbv(CB_IDENT)
            nblk = S // 128
            for i in range(min(2, nblk)):
                P.dma("sp", xt[i % NS].t[:], x_src[i * 128:(i + 1) * 128, :], writes=[xt[i % NS].b])
            k = 0
            for i in range(nblk):
                if i + 2 < nblk:
                    P.dma("sp", xt[(i + 2) % NS].t[:], x_src[(i + 2) * 128:(i + 3) * 128, :], writes=[xt[(i + 2) % NS].b])
                x = xt[i % NS]
                s = st[i % 4]
                P.act(junk[:], x[:], AF.Square, accum_out=s[:, 0:1])
                self.rstd_from_ss(s[:, 0:1], s[:, 1:2], s[:, 2:3], RMS_EPS, D)
                h = hb[i % 2]
                P.stt(h[:], x[:], s[:, 2:3], g[:], ALU.mult, ALU.mult)
                u = i // 4
                sub = i % 4
                hst = hs[u % 2]
                for half in range(2):
                    pt = tp[k % 4]
                    k += 1
                    for c in range(8):
                        cc = half * 8 + c
                        P.transpose(pt[:, c * 128:(c + 1) * 128], h[:, cc * 128:(cc + 1) * 128], ident)
                    src = pt.v(pt.t[:, :].rearrange("p (c t) -> p c t", t=128))
                    dst = hst[:, half * 8:(half + 1) * 8, sub * 128:(sub + 1) * 128]
                    if half == 0:
                        P.copy("act", dst, src)
                    else:
                        P.copy("dve", dst, src)
                if sub == 3 or i == nblk - 1:
                    t0 = u * 512
                    n = (sub + 1) * 128
                    P.dma("sp", self.hT[:, :, 1 + t0:1 + t0 + n].rearrange("c p t -> p c t"), hst.t[:, :, 0:n], reads=[hst.b])
            zc = T(P, es, "zc", [128, NCH, 2], BF16)
            P.memset("pool", zc[:], 0.0)
            P.dma("sp", self.hT[:, :, 0:1].rearrange("c p t -> p c t"), zc.t[:, :, 0:1], reads=[zc.b], allow_slow_non_contiguous=True)
            P.dma("sp", self.hT[:, :, S + 1:S + 2].rearrange("c p t -> p c t"), zc.t[:, :, 1:2], reads=[zc.b], allow_slow_non_contiguous=True)
            P.flush()

    def phase_att_inproj(self, j):
        P = self.P
        S = self.S
        ST = min(2048, S)
        W = self.wb_att_in[j]
        with contextlib.ExitStack() as es:
            hts = T(P, es, "hts", [128, NCH, ST], BF16)
            cs = T(P, es, "cs", [128, 2, ST], F32)
            NW = 3
            wt = [T(P, es, "wt%d" % i, [128, NCH, 512], BF16) for i in range(NW)]
            ps = [T(P, es, "ps%d" % i, [128, 512], F32, psum=True) for i in range(3)]
            pr = [T(P, es, "pr%d" % i, [128, 512], F32, psum=True) for i in range(2)]
            tb = [T(P, es, "tb%d" % i, [128, 512], BF16) for i in range(3)]
            t1 = [T(P, es, "t1_%d" % i, [128, 512], F32) for i in range(3)]
            t2 = [T(P, es, "t2_%d" % i, [128, 512], F32) for i in range(3)]
            qs = [T(P, es, "qs%d" % i, [128, ST], BF16) for i in range(3)]
            vs = [T(P, es, "vs%d" % i, [128, ST // 128, 512], BF16) for i in range(2)]
            rot = self.cbv(CB_ROT)
            NCG = ATT_COLS // 512
            Wv = W.rearrange("(c p) n -> p c n", p=128)
            cnt = {"ps": 0, "pr": 0, "tb": 0, "qs": 0, "vs": 0, "ev": 0}

            def load_w(cg):
                P.dma("sp", wt[cg % NW].t[:], Wv[:, :, cg * 512:(cg + 1) * 512], writes=[wt[cg % NW].b])

            for st_i in range(S // ST):
                t0 = st_i * ST
                P.dma("sp", hts.t[:], self.hT[:, :, 1 + t0:1 + t0 + ST].rearrange("c p t -> p c t"), writes=[hts.b])
                P.dma("sp", cs.t[:, 0, :], self.in_cf[:, t0:t0 + ST], writes=[cs.b])
                P.dma("sp", cs.t[:, 1, :], self.in_cf[:, S + t0:S + t0 + ST], writes=[cs.b])
                load_w(0)
                load_w(1)
                for cg in range(NCG):
                    if cg + 2 < NCG:
                        load_w(cg + 2)
                    w = wt[cg % NW]
                    if cg < 36:
                        g = cg // 12
                        typ = (cg % 12) // 4
                        h0 = (cg % 4) * 4
                    else:
                        g = -1
                        typ = 3
                        h0 = (cg - 36) * 4
                    if typ == 2:
                        dd = GROUPS[g][1]
                        vst = vs[cnt["vs"] % 2]
                        cnt["vs"] += 1
                        for tbi in range(ST // 128):
                            p = ps[cnt["ps"] % 3]
                            cnt["ps"] += 1
                            for c in range(NCH):
                                P.mm(p[:], hts[:, c, tbi * 128:(tbi + 1) * 128], w[:, c, :], start=(c == 0), stop=(c == NCH - 1))
                            if cnt["ev"] % 2 == 0:
                                P.copy("act", vst[:, tbi, :], p[:])
                            else:
                                P.copy("dve", vst[:, tbi, :], p[:])
                            cnt["ev"] += 1
                        npb = 128 // dd
                        for r in range(dd):
                            n0 = 64 + t0 // dd
                            dst = self.vv[g][r, n0:n0 + ST // dd, h0 * 128:h0 * 128 + 512].rearrange("(b n) c -> n b c", n=npb)
                            src = vst.t[r::dd, :, :] if dd > 1 else vst.t[:, :, :]
                            P.dma("sp", dst, src, reads=[vst.b])
                    else:
                        for jh in range(4):
                            h = h0 + jh
                            qst = qs[cnt["qs"] % 3]
                            cnt["qs"] += 1
                            dd = GROUPS[g][1] if typ < 2 else 1
                            for sub in range(ST // 512):
                                p = ps[cnt["ps"] % 3]
                                cnt["ps"] += 1
                                for c in range(NCH):
                                    P.mm(p[:], w[:, c, jh * 128:(jh + 1) * 128], hts[:, c, sub * 512:(sub + 1) * 512],
                                         start=(c == 0), stop=(c == NCH - 1))
                                if typ == 3:
                                    P.act(qst[:, sub * 512:(sub + 1) * 512], p[:], AF.Silu)
                                    continue
                                k = cnt["tb"] % 3
                                cnt["tb"] += 1
                                tbf = tb[k]
                                P.copy("act", tbf[:], p[:])
                                rp = pr[cnt["pr"] % 2]
                                cnt["pr"] += 1
                                P.mm(rp[:], rot, tbf[:])
                                P.tt("dve", t2[k][:], rp[:], cs[:, 1, sub * 512:(sub + 1) * 512], ALU.mult)
                                P.tt("dve", t1[k][:], p[:], cs[:, 0, sub * 512:(sub + 1) * 512], ALU.mult)
                                nsub = 512 // dd
                                if dd == 1:
                                    dst = qst[:, sub * 512:(sub + 1) * 512]
                                    P.tt("pool", dst, t1[k][:], t2[k][:], ALU.add)
                                else:
                                    dst = qst.v(qst.t[:, :].rearrange("p (r n) -> p r n", r=dd)[:, :, sub * nsub:(sub + 1) * nsub])
                                    a = t1[k].v(t1[k].t[:, :].rearrange("p (n r) -> p r n", r=dd))
                                    b = t2[k].v(t2[k].t[:, :].rearrange("p (n r) -> p r n", r=dd))
                                    P.tt("pool", dst, a, b, ALU.add)
                            if typ == 3:
                                P.dma("sp", self.zT[h * 128:(h + 1) * 128, t0:t0 + ST], qst.t[:, :], reads=[qst.b])
                            elif typ == 0:
                                dst = self.qT[g][h][:, :, t0 // dd:(t0 + ST) // dd]
                                P.dma("sp", dst, qst.t[:, :].rearrange("p (r n) -> p r n", r=dd), reads=[qst.b])
                            else:
                                dst = self.kT[g][h][:, :, 64 + t0 // dd:64 + (t0 + ST) // dd]
                                P.dma("sp", dst, qst.t[:, :].rearrange("p (r n) -> p r n", r=dd), reads=[qst.b])
            P.flush()

    def phase_att_core(self):
        P = self.P
        S = self.S
        scale = 1.0 / math.sqrt(DH)
        with contextlib.ExitStack() as es:
            vt = T(P, es, "valid", [128, self.nvalid], BF16)
            P.dma("sp", vt.t[:], self.in_valid[:, :], writes=[vt.b])
            accn = [T(P, es, "accn%d" % i, [128, S], F32) for i in range(2)]
            accd = [T(P, es, "accd%d" % i, [128, S], F32) for i in range(2)]
            NSL = 6
            qsb = [T(P, es, "qsb%d" % i, [128, 512], BF16) for i in range(NSL)]
            ksb = [T(P, es, "ksb%d" % i, [128, 640], BF16) for i in range(NSL)]
            vsb = [T(P, es, "vsb%d" % i, [128, 5, 128], BF16) for i in range(NSL)]
            pss = [T(P, es, "pss%d" % i, [128, 512], F32, psum=True) for i in range(4)]
            psn = [T(P, es, "psn%d" % i, [128, 512], F32, psum=True) for i in range(2)]
            psd = [T(P, es, "psd%d" % i, [128, 512], F32, psum=True) for i in range(2)]
            pe = [T(P, es, "pe%d" % i, [128, 256], BF16) for i in range(6)]
            pm = [T(P, es, "pm%d" % i, [128, 256], BF16) for i in range(6)]
            rc = [T(P, es, "rc%d" % i, [128, 512], F32) for i in range(2)]
            ob = [T(P, es, "ob%d" % i, [128, 512], F32) for i in range(2)]
            zb = [T(P, es, "zb%d" % i, [128, 512], BF16) for i in range(2)]
            yb = [T(P, es, "yb%d" % i, [128, 512], BF16) for i in range(2)]
            ones = self.cbv(CB_ONES)
            mask = self.cb[:, CB_M0:CB_M0 + 256]
            tiles = []
            voff = 0
            for g, (_, dd) in enumerate(GROUPS):
                L = S // dd
                QT = min(512, L)
                nblk_r = L // 128 + 1
                for r in range(dd):
                    for qt in range(L // QT):
                        tiles.append((g, dd, L, QT, r, qt, voff + r * nblk_r))
                voff += dd * nblk_r
            cnt = {"sl": 0, "ps": 0, "pe": 0, "fin": 0}

            def load_tile(h, ti, slot):
                g, dd, L, QT, r, qt, vo = tiles[ti]
                nb = QT // 128 + 1
                P.dma("sp", qsb[slot].t[:, 0:QT], self.qT[g][h][:, r, qt * QT:(qt + 1) * QT], writes=[qsb[slot].b])
                P.dma("sp", ksb[slot].t[:, 0:QT + 128], self.kT[g][h][:, r, qt * QT:(qt + 1) * QT + 128], writes=[ksb[slot].b])
                P.dma("sp", vsb[slot].t[:, 0:nb, :],
                      self.vv[g][r, qt * QT:qt * QT + QT + 128, h * 128:(h + 1) * 128].rearrange("(b p) c -> p b c", p=128),
                      writes=[vsb[slot].b])

            seq = [(h, ti) for h in range(NH_ATT) for ti in range(len(tiles))]
            PF = 2
            qbs = []
            for i, (h, ti) in enumerate(seq):
                for qb in range(tiles[ti][3] // 128):
                    qbs.append((i, h, ti, qb))
            st_ = {}
            for i in range(min(PF, len(seq))):
                load_tile(seq[i][0], seq[i][1], i % NSL)

            def stageS(k):
                i, h, ti, qb = qbs[k]
                g, dd, L, QT, r, qt, vo = tiles[ti]
                nqb = QT // 128
                if qb == 0 and i + PF < len(seq):
                    load_tile(seq[i + PF][0], seq[i + PF][1], (i + PF) % NSL)
                slot = i % NSL
                sc = pss[k % len(pss)]
                qv = qsb[slot][:, qb * 128:(qb + 1) * 128]
                for kb in range(2):
                    P.mm(sc[:, kb * 128:(kb + 1) * 128], ksb[slot][:, (qb + kb) * 128:(qb + kb + 1) * 128], qv)
                e = pe[k % len(pe)]
                m = pm[k % len(pm)]
                P.act(e[:], sc[:, 0:256], AF.Exp, scale=scale)
                P.tt("pool", m[:], e[:], self.cb[:, CB_M0:CB_M0 + 256], ALU.mult)
                st_[k] = m

            def stageV(k):
                i, h, ti, qb = qbs[k]
                g, dd, L, QT, r, qt, vo = tiles[ti]
                nqb = QT // 128
                slot = i % NSL
                m = st_.pop(k)
                an = accn[h % 2]
                ad = accd[h % 2]
                pn = psn[i % 2]
                pd = psd[i % 2]
                for kb in range(2):
                    P.mm(pn[:, qb * 128:(qb + 1) * 128], vsb[slot][:, qb + kb, :], m[:, kb * 128:(kb + 1) * 128],
                         start=(kb == 0), stop=(kb == 1))
                for kb in range(2):
                    blk = vo + qt * nqb + qb + kb
                    P.mm(pd[:, qb * 128:(qb + 1) * 128], View(vt.b, vt.t[:, blk:blk + 1].broadcast_to([128, 128])), m[:, kb * 128:(kb + 1) * 128],
                         start=(kb == 0), stop=(kb == 1))
                if qb != nqb - 1:
                    return
                base = qt * QT * dd + r
                if dd == 1:
                    dn = an[:, base:base + QT]
                    dden = ad[:, base:base + QT]
                else:
                    dn = an.v(an.t[:, qt * QT * dd:(qt + 1) * QT * dd].rearrange("p (n r) -> p r n", r=dd)[:, r, :])
                    dden = ad.v(ad.t[:, qt * QT * dd:(qt + 1) * QT * dd].rearrange("p (n r) -> p r n", r=dd)[:, r, :])
                if g == 0:
                    P.copy("dve", dn, pn[:, 0:QT])
                    P.copy("act", dden, pd[:, 0:QT])
                else:
                    P.tt("dve", dn, pn[:, 0:QT], dn, ALU.add)
                    P.tt("dve", dden, pd[:, 0:QT], dden, ALU.add)
                if ti == len(tiles) - 1:
                    for c0 in range(0, S, 512):
                        kf = cnt["fin"] % 2
                        cnt["fin"] += 1
                        w = min(512, S - c0)
                        P.dma("act", zb[kf].t[:, 0:w], self.zT[h * 128:(h + 1) * 128, c0:c0 + w], writes=[zb[kf].b])
                        P.ts("dve", rc[kf][:, 0:w], ad[:, c0:c0 + w], 1e-30, ALU.max)
                        P.recip(rc[kf][:, 0:w], rc[kf][:, 0:w])
                        P.tt("pool", ob[kf][:, 0:w], an[:, c0:c0 + w], rc[kf][:, 0:w], ALU.mult)
                        P.tt("pool", yb[kf][:, 0:w], ob[kf][:, 0:w], zb[kf][:, 0:w], ALU.mult)
                        P.dma("act", self.yT[h * 128:(h + 1) * 128, c0:c0 + w], yb[kf].t[:, 0:w], reads=[yb[kf].b])

            LAG = 2
            for k in range(len(qbs) + LAG):
                if k < len(qbs):
                    stageS(k)
                if k - LAG >= 0:
                    stageV(k - LAG)
            P.flush()

    def phase_out(self, x_src, x_dst, Wb, layer):
        P = self.P
        S = self.S
        with contextlib.ExitStack() as es:
            g = self.load_bcast(es, "g_post", self.in_norm_post[layer:layer + 1, :])
            epst = T(P, es, "epst", [128, 2], F32)
            P.memset("pool", epst[:, 0:1], RMS_EPS)
            self.eps_t = {RMS_EPS: epst[:, 0:1]}
            w = T(P, es, "wout", [128, NCH, D], BF16)
            P.dma("sp", w.t[:], Wb.rearrange("(c p) n -> p c n", p=128), writes=[w.b])
            ys = [T(P, es, "ys%d" % i, [128, NCH, 512], BF16) for i in range(2)]
            xt = [T(P, es, "xo%d" % i, [128, D], F32) for i in range(3)]
            ot = [T(P, es, "ot%d" % i, [128, D], F32) for i in range(2)]
            junk = T(P, es, "junk", [128, D], BF16)
            st = [T(P, es, "st%d" % i, [128, 3], F32) for i in range(4)]
            po = [T(P, es, "po%d" % i, [128, D], F32, psum=True) for i in range(2)]
            yTv = self.yT.rearrange("(c p) t -> p c t", p=128)
            nu = (S + 511) // 512

            def load_u(u):
                t0 = u * 512
                n = min(512, S - t0)
                P.dma("sp", ys[u % 2].t[:, :, 0:n], yTv[:, :, t0:t0 + n], writes=[ys[u % 2].b])

            load_u(0)
            nblk = S // 128
            for i in range(min(2, nblk)):
                P.dma("sp", xt[i % 3].t[:], x_src[i * 128:(i + 1) * 128, :], writes=[xt[i % 3].b])
            for i in range(nblk):
                u = i // 4
                sub = i % 4
                if sub == 0 and u + 1 < nu:
                    load_u(u + 1)
                if i + 2 < nblk:
                    P.dma("sp", xt[(i + 2) % 3].t[:], x_src[(i + 2) * 128:(i + 3) * 128, :], writes=[xt[(i + 2) % 3].b])
                y = ys[u % 2]
                p = po[i % 2]
                for nb in range(4):
                    for c in range(NCH):
                        P.mm(p[:, nb * 512:(nb + 1) * 512], y[:, c, sub * 128:(sub + 1) * 128], w[:, c, nb * 512:(nb + 1) * 512],
                             start=(c == 0), stop=(c == NCH - 1))
                s = st[i % 4]
                P.act(junk[:], p[:], AF.Square, accum_out=s[:, 0:1])
                self.rstd_from_ss(s[:, 0:1], s[:, 1:2], s[:, 2:3], RMS_EPS, D)
                o = ot[i % 2]
                P.stt(o[:], p[:], s[:, 2:3], g[:], ALU.mult, ALU.mult)
                P.tt("pool", o[:], o[:], xt[i % 3][:], ALU.add)
                P.dma("act", x_dst[i * 128:(i + 1) * 128, :], o.t[:], reads=[o.b])
            P.flush()

    def declare_rwkv(self):
        S = self.S
        d = self.dram
        self.rw_r = d("rw_r", [S, D], BF16)
        self.rw_k = d("rw_k", [S, D], BF16)
        self.rw_sz = d("rw_sz", [S, D], BF16)
        self.rw_v = [d("rw_v%d" % j, [S, D], BF16) for j in range(2)]
        self.rw_sw = [d("rw_sw%d" % c, [S, D], F32) for c in range(2)]
        self.rw_a = [d("rw_a%d" % c, [S, D], BF16) for c in range(2)]
        self.rw_gate = d("rw_gate", [S, D], BF16)
        self.wb_w1 = d("wb_w1", [4 * D, 96], BF16)
        self.wb_a1 = d("wb_a1", [4 * D, 96], BF16)
        self.wb_w2 = d("wb_w2", [4 * 96, D], BF16)
        self.wb_a2 = d("wb_a2", [4 * 96, D], BF16)
        self.wb_v1 = d("wb_v1", [D, 64], BF16)
        self.wb_v2 = d("wb_v2", [64, D], BF16)
        self.ft = [d("ft%d" % c, [4, 16, 128, S], BF16) for c in range(2)]
        self.ah = [d("ah%d" % c, [S, D], BF16) for c in range(2)]
        self.kh = [d("kh%d" % c, [S, D], BF16) for c in range(2)]
        self.vp = d("vp", [S, D], BF16)
        self.bon = d("bon", [S, NHR], F32)
        self.gam = [d("gam%d" % c, [128, 16, S // 64], F32) for c in range(2)]
        self.am = [d("am%d" % c, [S // 128, 128, 16, 512], BF16) for c in range(2)]
        self.ttm = [d("ttm%d" % c, [S // 128, 128, 32, 64], BF16) for c in range(2)]
        self.yd = [d("yd%d" % c, [S, D], F32) for c in range(2)]

    def phase_rwkv(self, j):
        self.vp_src = self.vp if j == 1 else self.rw_v[j]
        for i, f in enumerate((lambda: self.phase_rwkv_proj(j), lambda: self.phase_rwkv_prep(j), self.phase_rwkv_chain,
                               self.phase_rwkv_scan, lambda: self.phase_rwkv_post(j))):
            if i < self.rw_stop:
                f()

    def phase_rwkv_proj(self, j):
        P = self.P
        S = self.S
        TT = min(512, S)
        NTB = TT // 128
        vres = (j == 1)
        Wv = self.wb_rwkv_in[j].rearrange("(c p) n -> p c n", p=128)
        R = self.in_rw
        with contextlib.ExitStack() as es:
            mup = T(P, es, "mup", [128, 6, NCH], F32)
            mun = T(P, es, "mun", [128, 6, NCH], F32)
            P.dma("sp", mup.t[:], R["rwkv_mu_prev"][j].rearrange("g (c p) -> p g c", p=128), writes=[mup.b], allow_slow_non_contiguous=True)
            P.dma("sp", mun.t[:], R["rwkv_mu_next"][j].rearrange("g (c p) -> p g c", p=128), writes=[mun.b], allow_slow_non_contiguous=True)
            w1 = []
            a1 = []
            w2 = []
            a2 = []
            for c in range(2):
                o = (j * 2 + c)
                t_ = T(P, es, "w1_%d" % c, [128, NCH, 96], BF16)
                P.dma("sp", t_.t[:], self.wb_w1[o * D:(o + 1) * D, :].rearrange("(k p) r -> p k r", p=128), writes=[t_.b])
                w1.append(t_)
                t_ = T(P, es, "a1_%d" % c, [128, NCH, 96], BF16)
                P.dma("sp", t_.t[:], self.wb_a1[o * D:(o + 1) * D, :].rearrange("(k p) r -> p k r", p=128), writes=[t_.b])
                a1.append(t_)
                t_ = T(P, es, "w2_%d" % c, [98, D], BF16)
                P.dma("sp", t_.t[0:96, :], self.wb_w2[o * 96:(o + 1) * 96, :], writes=[t_.b])
                w2.append(t_)
                t_ = T(P, es, "a2_%d" % c, [98, D], BF16)
                P.dma("sp", t_.t[0:96, :], self.wb_a2[o * 96:(o + 1) * 96, :], writes=[t_.b])
                a2.append(t_)
            if vres:
                v1 = T(P, es, "v1", [128, NCH, 64], BF16)
                P.dma("sp", v1.t[:], self.wb_v1.rearrange("(k p) r -> p k r", p=128), writes=[v1.b])
                v2 = T(P, es, "v2", [66, D], BF16)
                P.dma("sp", v2.t[0:64, :], self.wb_v2[:, :], writes=[v2.b])
            with contextlib.ExitStack() as es2:
                bf_ = T(P, es2, "bias_f", [1, D], F32)
                bh = T(P, es2, "bias_h", [1, D], BF16)
                b32 = T(P, es2, "bias_32", [1, D], F32)
                bl = T(P, es2, "bias_l", [1, D], BF16)
                rows = [(R["rwkv_w0"][j, 0:1, :], w2[0], 96), (R["rwkv_w0"][j, 1:2, :], w2[1], 96),
                        (R["rwkv_a0"][j, 0:1, :], a2[0], 96), (R["rwkv_a0"][j, 1:2, :], a2[1], 96)]
                if vres:
                    rows.append((R["rwkv_v0"][0:1, :], v2, 64))
                for src, dst, r0 in rows:
                    P.dma("sp", bf_.t[:], src, writes=[bf_.b])
                    P.copy("dve", bh[:], bf_[:])
                    P.copy("dve", b32[:], bh[:])
                    P.tt("dve", b32[:], bf_[:], b32[:], ALU.subtract)
                    P.copy("dve", bl[:], b32[:])
                    P.dma("sp", dst.t[r0:r0 + 1, :], bh.t[:], reads=[bh.b], writes=[dst.b])
                    P.dma("sp", dst.t[r0 + 1:r0 + 2, :], bl.t[:], reads=[bl.b], writes=[dst.b])
                P.flush()
            hts = [T(P, es, "rhts%d" % i, [128, NCH, TT + 2], BF16) for i in range(2)]
            dp = T(P, es, "dp", [128, NCH, TT], BF16)
            tmk = [T(P, es, "tmk%d" % i, [128, TT], BF16) for i in range(2)]
            dn = T(P, es, "dn", [128, NCH, TT], BF16)
            xs = [T(P, es, "xs%d" % i, [128, NCH, TT], BF16) for i in range(2)]
            tmp = [T(P, es, "xtmp%d" % i, [128, TT], F32) for i in range(2)]
            wt = [T(P, es, "rwt%d" % i, [128, NCH, 512], BF16) for i in range(3)]
            ps = [T(P, es, "rps%d" % i, [128, 512], F32, psum=True) for i in range(4)]
            pl = [T(P, es, "rpl%d" % i, [128, 512], F32, psum=True) for i in range(2)]
            stg = [T(P, es, "rstg%d" % i, [128, NTB, 512], BF16) for i in range(2)]
            stf = [T(P, es, "rstf%d" % i, [128, 512], F32) for i in range(2)]
            l1 = [T(P, es, "l1_%d" % i, [98, TT], BF16) for i in range(2)]
            for t_ in l1:
                P.memset("pool", t_[96:98, :], 1.0)
            l1v = None
            if vres:
                l1v = T(P, es, "l1v", [66, TT], BF16)
                P.memset("pool", l1v[64:66, :], 1.0)
            cnt = {"ps": 0, "pl": 0, "stg": 0, "stf": 0, "wt": 0, "ev": 0, "xs": 0, "l1": 0}
            ntile = S // TT

            def load_h(ti):
                t0 = ti * TT
                P.dma("sp", hts[ti % 2].t[:], self.hT[:, :, t0:t0 + TT + 2].rearrange("c p t -> p c t"), writes=[hts[ti % 2].b])

            def evac(dst, src, func=None):
                if func is not None:
                    P.act(dst, src, func)
                else:
                    P.copy("act", dst, src)
                cnt["ev"] += 1

            def lora(x, wA, wB, krows, func1, dst_dram, fp32_out, l1t, t0):
                p1 = pl[cnt["pl"] % 2]
                cnt["pl"] += 1
                for c in range(NCH):
                    P.mm(p1[0:krows, 0:TT], wA[:, c, 0:krows], x[:, c, :], start=(c == 0), stop=(c == NCH - 1))
                if func1 is None:
                    P.copy("act", l1t[0:krows, :], p1[0:krows, 0:TT])
                else:
                    P.act(l1t[0:krows, :], p1[0:krows, 0:TT], func1)
                for tb in range(NTB):
                    if not fp32_out:
                        sg = stg[cnt["stg"] % 2]
                        cnt["stg"] += 1
                    for cg in range(4):
                        p2 = ps[cnt["ps"] % 4]
                        cnt["ps"] += 1
                        P.mm(p2[:], l1t[0:krows + 2, tb * 128:(tb + 1) * 128], wB[0:krows + 2, cg * 512:(cg + 1) * 512])
                        if fp32_out:
                            sf = stf[cnt["stf"] % 2]
                            cnt["stf"] += 1
                            P.act(sf[:], p2[:], AF.Sigmoid)
                            P.dma("sp", dst_dram[t0 + tb * 128:t0 + (tb + 1) * 128, cg * 512:(cg + 1) * 512], sf.t[:], reads=[sf.b])
                        else:
                            P.act(sg[:, 0, :], p2[:], AF.Sigmoid)
                            P.dma("sp", dst_dram[t0 + tb * 128:t0 + (tb + 1) * 128, cg * 512:(cg + 1) * 512], sg.t[:, 0, :], reads=[sg.b])

            wseq = [(gi, cg) for _ in range(ntile) for gi in range(4) for cg in range(4)]

            def issue_w(k):
                if k < len(wseq):
                    gi_, cg_ = wseq[k]
                    col0_ = gi_ * D + cg_ * 512
                    P.dma("sp", wt[k % 3].t[:], Wv[:, :, col0_:col0_ + 512], writes=[wt[k % 3].b])

            load_h(0)
            issue_w(0)
            issue_w(1)
            for ti in range(ntile):
                t0 = ti * TT
                if ti + 1 < ntile:
                    load_h(ti + 1)
                h = hts[ti % 2]
                P.tt("pool", dp[:], h[:, :, 0:TT], h[:, :, 1:TT + 1], ALU.subtract)
                mk_ = tmk[ti % 2]
                P.dma("sp", mk_.t[:], self.in_tokmask[:, t0:t0 + TT], writes=[mk_.b])
                P.tt("pool", dp[:], dp[:], View(mk_.b, mk_.t[:, :].unsqueeze(1).to_broadcast([128, NCH, TT])), ALU.mult)
                P.tt("pool", dn[:], h[:, :, 2:TT + 2], h[:, :, 1:TT + 1], ALU.subtract)
                for g in (0, 2, 3, 5, 1, 4):
                    x = xs[cnt["xs"] % 2]
                    cnt["xs"] += 1
                    for c in range(NCH):
                        tm = tmp[c % 2]
                        P.stt(tm[:], dp[:, c, :], mup[:, g, c:c + 1], h[:, c, 1:TT + 1], ALU.mult, ALU.add)
                        P.stt(x[:, c, :], dn[:, c, :], mun[:, g, c:c + 1], tm[:], ALU.mult, ALU.add)
                    if g in (0, 2, 3, 5):
                        gi = {0: 0, 2: 1, 3: 2, 5: 3}[g]
                        dst = {0: self.rw_r, 2: self.rw_k, 3: self.rw_v[j], 5: self.rw_sz}[g]
                        for cg in range(4):
                            w = wt[cnt["wt"] % 3]
                            issue_w(cnt["wt"] + 2)
                            cnt["wt"] += 1
                            sg = stg[cnt["stg"] % 2]
                            cnt["stg"] += 1
                            for tb in range(NTB):
                                p = ps[cnt["ps"] % 4]
                                cnt["ps"] += 1
                                for c in range(NCH):
                                    P.mm(p[:], x[:, c, tb * 128:(tb + 1) * 128], w[:, c, :], start=(c == 0), stop=(c == NCH - 1))
                                evac(sg[:, tb, :], p[:], AF.Silu if g == 5 else None)
                            P.dma("sp", dst[t0:t0 + TT, cg * 512:(cg + 1) * 512].rearrange("(b p) c -> p b c", p=128), sg.t[:], reads=[sg.b])
                        if g == 3 and vres:
                            lora(x, v1, v2, 64, None, self.rw_gate, False, l1v, t0)
                    elif g == 1:
                        for c in range(2):
                            lt = l1[cnt["l1"] % 2]
                            cnt["l1"] += 1
                            lora(x, w1[c], w2[c], 96, AF.Tanh, self.rw_sw[c], True, lt, t0)
                    else:
                        for c in range(2):
                            lt = l1[cnt["l1"] % 2]
                            cnt["l1"] += 1
                            lora(x, a1[c], a2[c], 96, None, self.rw_a[c], False, lt, t0)
            P.flush()
    def phase_rwkv_prep(self, j):
        P = self.P
        S = self.S
        vres = (j == 1)
        R = self.in_rw
        CDEC = math.exp(-0.5)
        nblk = S // 128
        self.vp_src = self.vp if vres else self.rw_v[j]
        with contextlib.ExitStack() as es:
            kkB = self.load_bcast(es, "kkB", R["rwkv_k_k"][j:j + 1, :])
            kaB = self.load_bcast(es, "kaB", R["rwkv_k_a"][j:j + 1, :])
            rkB = []
            for c in range(2):
                t_ = T(P, es, "rkB%d" % c, [128, D], BF16)
                P.dma("pool", t_.t[:], R["rwkv_r_k"][j, c:c + 1, :].partition_broadcast(128), writes=[t_.b])
                rkB.append(t_)
            tri = T(P, es, "tri", [128, 4, 128], F32)
            P.dma("sp", tri.t[:], self.in_cf[:, 2 * S:2 * S + 512].rearrange("p (a b) -> p a b", b=128), writes=[tri.b])
            ind = T(P, es, "ind", [128, 2], F32)
            P.dma("sp", ind.t[:], self.in_cf[:, 2 * S + 512:2 * S + 514], writes=[ind.b])
            ident = self.cbv(CB_IDENT)
            inb = [T(P, es, "inb%d" % i, [128, 5, D], BF16) for i in range(2)]
            swt = [T(P, es, "swt%d" % c, [128, D], F32) for c in range(2)]
            if vres:
                gv = T(P, es, "gv", [128, 2, D], BF16)
            kkf = T(P, es, "kkf", [128, D], F32)
            sq = T(P, es, "sq", [128, D], F32)
            kk = T(P, es, "kk", [128, D], BF16)
            kkac = T(P, es, "kkac", [128, D], BF16)
            kmk = T(P, es, "kmk", [128, D], BF16)
            vpt = T(P, es, "vpt", [128, D], BF16)
            tb_ = T(P, es, "tb_", [128, D], BF16)
            kd = T(P, es, "kd", [128, D], BF16)
            kka = T(P, es, "kka", [128, D], BF16)
            ub = kkf
            sm = [T(P, es, "sm%d" % i, [128, 4, NHR], F32) for i in range(2)]
            Eb = [T(P, es, "Eb%d" % i, [128, 4, 512], BF16) for i in range(2)]
            prod = [T(P, es, "prod%d" % i, [128, D], BF16) for i in range(6)]
            fts = T(P, es, "fts", [128, 4, 16, 128], BF16)
            gall = [T(P, es, "gall%d" % i, [128, 16, 2], F32) for i in range(4)]
            pc = [T(P, es, "pc%d" % i, [128, 512], F32, psum=True) for i in range(5)]
            ptr = [T(P, es, "ptr%d" % i, [128, 1024], BF16, psum=True) for i in range(2)]
            pg = T(P, es, "pg", [128, 16, 2], F32, psum=True)
            cnt = {"pc": 0, "ptr": 0, "E": 0, "alt": 0}

            def load_in(bi):
                t0 = bi * 128
                ib = inb[bi % 2]
                for q, src in enumerate((self.rw_r, self.rw_k, self.rw_v[j], self.rw_a[0], self.rw_a[1])):
                    P.dma("sp", ib.t[:, q, :], src[t0:t0 + 128, :], writes=[ib.b])

            def alt():
                cnt["alt"] += 1
                return "dve" if cnt["alt"] % 2 == 0 else "pool"

            load_in(0)
            for bi in range(nblk):
                t0 = bi * 128
                if bi + 1 < nblk:
                    load_in(bi + 1)
                ib = inb[bi % 2]
                r_, k_, v_ = ib[:, 0, :], ib[:, 1, :], ib[:, 2, :]
                for c in range(2):
                    P.dma("sp", swt[c].t[:], self.rw_sw[c][t0:t0 + 128, :], writes=[swt[c].b])
                if vres:
                    P.dma("sp", gv.t[:, 0, :], self.rw_gate[t0:t0 + 128, :], writes=[gv.b])
                    P.dma("sp", gv.t[:, 1, :], self.rw_v[0][t0:t0 + 128, :], writes=[gv.b])
                    P.tt("pool", tb_[:], gv[:, 1, :], v_, ALU.subtract)
                    P.tt("pool", tb_[:], tb_[:], gv[:, 0, :], ALU.mult)
                    P.tt("pool", vpt[:], v_, tb_[:], ALU.add)
                    P.dma("sp", self.vp[t0:t0 + 128, :], vpt.t[:], reads=[vpt.b])
                s_ = sm[bi % 2]
                P.tt("pool", kkf[:], k_, kkB[:], ALU.mult)
                P.tt("pool", sq[:], kkf[:], kkf[:], ALU.mult)
                P.add("dve", lambda e, o=s_.t[:, 0, :], i=sq.t[:, :].rearrange("p (h n) -> p h n", n=NR): e.tensor_reduce(out=o, in_=i, axis=AX.X, op=ALU.add),
                      reads=[sq.b], writes=[s_.b])
                P.act(s_[:, 1, :], s_[:, 0, :], AF.Sqrt)
                P.ts("dve", s_[:, 1, :], s_[:, 1, :], 1e-12, ALU.max)
                P.recip(s_[:, 2, :], s_[:, 1, :])
                P.tt("pool", kk.v(kk.t[:, :].rearrange("p (h n) -> p h n", n=NR)), kkf.v(kkf.t[:, :].rearrange("p (h n) -> p h n", n=NR)),
                     s_.v(s_.t[:, 2, :].unsqueeze(2).to_broadcast([128, NHR, NR])), ALU.mult)
                P.tt("pool", kkac[:], k_, kaB[:], ALU.mult)
                P.tt("pool", kmk[:], k_, kkac[:], ALU.subtract)
                for c in range(2):
                    a_ = ib[:, 3 + c, :]
                    P.tt(alt(), tb_[:], a_, kkac[:], ALU.mult)
                    P.tt(alt(), kd[:], tb_[:], kmk[:], ALU.add)
                    P.tt(alt(), kka[:], kk[:], a_, ALU.mult)
                    if c == 0:
                        P.tt(alt(), ub[:], kd[:], rkB[0][:], ALU.mult)
                    else:
                        P.tt(alt(), sq[:], kd[:], rkB[1][:], ALU.mult)
                        P.tt(alt(), ub[:], ub[:], sq[:], ALU.add)
                        P.tt(alt(), ub[:], ub[:], r_, ALU.mult)
                        P.add("dve", lambda e, o=s_.t[:, 3, :], i=ub.t[:, :].rearrange("p (h n) -> p h n", n=NR): e.tensor_reduce(out=o, in_=i, axis=AX.X, op=ALU.add),
                              reads=[ub.b], writes=[s_.b])
                        P.dma("sp", self.bon[t0:t0 + 128, :], s_.t[:, 3, :], reads=[s_.b])
                    mats = (0, 2, 3) if c == 0 else (1, 3, 2)
                    sw = swt[c]
                    for cg in range(4):
                        cs_ = slice(cg * 512, (cg + 1) * 512)
                        pcs = []
                        for mi in mats:
                            p = pc[cnt["pc"] % 5]
                            cnt["pc"] += 1
                            P.mm(p[:], tri[:, mi, :], sw[:, cs_])
                            pcs.append(p)
                        E = Eb[cnt["E"] % 2]
                        cnt["E"] += 1
                        P.act(E[:, 0, :], pcs[0][:], AF.Exp, scale=-CDEC)
                        P.act(E[:, 1, :], pcs[1][:], AF.Exp, scale=-CDEC)
                        P.act(E[:, 2, :], pcs[0][:], AF.Exp, scale=CDEC)
                        P.act(E[:, 3, :], pcs[2][:], AF.Exp, scale=-CDEC)
                        P.tt(alt(), prod[0][:, cs_], kk[:, cs_], E[:, 1, :], ALU.mult)
                        P.tt(alt(), prod[1][:, cs_], ib[:, 0, cs_], E[:, 0, :], ALU.mult)
                        P.tt(alt(), prod[2][:, cs_], kka[:, cs_], E[:, 2, :], ALU.mult)
                        P.tt(alt(), prod[3][:, cs_], kd[:, cs_], E[:, 2, :], ALU.mult)
                        P.tt(alt(), prod[4][:, cs_], kka[:, cs_], E[:, 3, :], ALU.mult)
                        P.tt(alt(), prod[5][:, cs_], kd[:, cs_], E[:, 3, :], ALU.mult)
                    for p_ in range(16):
                        P.mm(pg[:, p_, :], sw[:, p_ * 128:(p_ + 1) * 128], ind[:, :])
                    gt_ = gall[(bi * 2 + c) % 4]
                    P.act(gt_[:], pg[:], AF.Exp, scale=-CDEC)
                    P.dma("act", self.gam[c][:, :, bi * 2:bi * 2 + 2], gt_.t[:], reads=[gt_.b])
                    for q in range(4):
                        for half in range(2):
                            pt = ptr[cnt["ptr"] % 2]
                            cnt["ptr"] += 1
                            for pp in range(8):
                                p_ = half * 8 + pp
                                P.transpose(pt[:, pp * 128:(pp + 1) * 128], prod[q][:, p_ * 128:(p_ + 1) * 128], ident)
                            src = pt.v(pt.t[:, :].rearrange("p (c t) -> p c t", t=128))
                            dst = fts[:, q, half * 8:(half + 1) * 8, :]
                            if cnt["ptr"] % 2 == 0:
                                P.copy("act", dst, src)
                            else:
                                P.copy("dve", dst, src)
                    P.dma("sp", self.ft[c][:, :, :, t0:t0 + 128].rearrange("q p c t -> c q p t"), fts.t[:], reads=[fts.b])
                    P.dma("sp", self.ah[c][t0:t0 + 128, :], prod[4].t[:], reads=[prod[4].b])
                    P.dma("sp", self.kh[c][t0:t0 + 128, :], prod[5].t[:], reads=[prod[5].b])
            P.flush()
    def phase_rwkv_chain(self):
        P = self.P
        S = self.S
        nblk = S // 128
        with contextlib.ExitStack() as es:
            ftl = [T(P, es, "ftl%d" % i, [128, 4, 16, 128], BF16) for i in range(2)]
            amt = [T(P, es, "amt%d" % i, [128, 16, 512], BF16) for i in range(2)]
            Q0 = [T(P, es, "Q0_%d" % i, [128, 32, 64], BF16) for i in range(2)]
            G0 = [T(P, es, "G0_%d" % i, [128, 32, 64], BF16) for i in range(2)]
            N0 = [T(P, es, "N0_%d" % i, [128, 32, 64], BF16) for i in range(2)]
            Pst = [T(P, es, "Pst%d" % i, [128, 32, 64], BF16) for i in range(2)]
            Qst = [T(P, es, "Qst%d" % i, [128, 32, 64], BF16) for i in range(2)]
            Pbd = [T(P, es, "Pbd%d" % i, [128, 32, 128], BF16) for i in range(2)]
            Qbd = [T(P, es, "Qbd%d" % i, [128, 32, 128], BF16) for i in range(2)]
            for t_ in Pbd + Qbd:
                P.memset("pool", t_[:], 0.0)
            Gm = [T(P, es, "Gm%d" % i, [128, 32, 64], BF16) for i in range(2)]
            pb = [T(P, es, "pb%d" % i, [128, 512], F32, psum=True) for i in range(8)]
            cnt = {"pa": 0, "b": 0, "ev": 0}
            seq = [(c, bi) for bi in range(nblk) for c in range(2)]

            def load(i):
                c, bi = seq[i]
                P.dma("sp", ftl[i % 2].t[:], self.ft[c][:, :, :, bi * 128:(bi + 1) * 128].rearrange("q p c t -> c q p t"), writes=[ftl[i % 2].b])

            import os as _os
            def evcopy(dst, src):
                cnt["ev"] += 1
                ev_ = _os.environ.get('EVENG')
                P.copy("dve", dst, src)

            load(0)
            for i, (c, bi) in enumerate(seq):
                if i + 1 < len(seq):
                    load(i + 1)
                f = ftl[i % 2]
                am_ = amt[i % 2]
                q0 = Q0[i % 2]
                g0 = G0[i % 2]
                mk = CB_MF if c == 0 else CB_MB
                lk = CB_LF if c == 0 else CB_LB
                for p2 in range(8):
                    base = (cnt["pa"] % 4) * 2
                    cnt["pa"] += 1
                    for hh in range(2):
                        ps_ = pb[base + hh]
                        kr = slice(hh * 64, (hh + 1) * 64)
                        for pl in range(2):
                            p = p2 * 2 + pl
                            for cc in range(2):
                                tk = slice(cc * 64, (cc + 1) * 64)
                                P.mm(ps_[tk, pl * 256:pl * 256 + 128], f[kr, 2, p, tk], f[kr, 0:2, p, tk])
                                P.mm(ps_[tk, pl * 256 + 128:pl * 256 + 256], f[kr, 3, p, tk], f[kr, 0:2, p, tk])
                        P.tt("dve", am_[:, p2 * 2:p2 * 2 + 2, hh * 256:(hh + 1) * 256], ps_.v(ps_.t[:, :].rearrange("p (a m) -> p a m", a=2)),
                             View(self.cb.b, self.cb.t[:, mk:mk + 256].unsqueeze(1).to_broadcast([128, 2, 256])), ALU.mult)
                q0v = q0.t[:, :, :].rearrange("p (a h) n -> p a h n", h=2)
                import os as _os
                CHS = int(_os.environ.get("CHSTOP", "9"))
                if CHS < 2:
                    continue
                for grp in range(2):
                    base = (cnt["pa"] % 4) * 2
                    cnt["pa"] += 1
                    for hh in range(2):
                        ps_ = pb[base + hh]
                        kr = slice(hh * 64, (hh + 1) * 64)
                        for pl in range(8):
                            p = grp * 8 + pl
                            for cc in range(2):
                                tk = slice(cc * 64, (cc + 1) * 64)
                                P.mm(ps_[tk, pl * 64:(pl + 1) * 64], f[kr, 0, p, tk], f[kr, 2, p, tk])
                        P.tt("dve", View(q0.b, q0v[:, grp * 8:(grp + 1) * 8, hh, :]), ps_.v(ps_.t[:, :].rearrange("p (a m) -> p a m", m=64)),
                             View(self.cb.b, self.cb.t[:, lk:lk + 64].unsqueeze(1).to_broadcast([128, 8, 64])), ALU.mult)
                P.dma("act", self.am[c][bi], am_.t[:], reads=[am_.b])
                if CHS < 3:
                    continue
                nview = am_.t[:, :, :].rearrange("p a (h q n) -> p a h q n", h=2, q=4)[:, :, :, 0, :]
                P.tt("pool", g0.v(g0.t[:, :, :].rearrange("p (a h) n -> p a h n", h=2)),
                     View(self.cb.b, self.cb.t[:, CB_I64:CB_I64 + 64].unsqueeze(1).unsqueeze(1).to_broadcast([128, 16, 2, 64])),
                     View(am_.b, nview), ALU.subtract)

                nst = N0[i % 2]
                nv4 = am_.t[:, :, :].rearrange("p a (h q n) -> p a h q n", h=2, q=4)[:, :, :, 0, :]
                P.copy("act", nst.v(nst.t[:, :, :].rearrange("p (a h) n -> p a h n", h=2)), View(am_.b, nv4))
                for cc in range(2):
                    rows = slice(cc * 64, (cc + 1) * 64)
                    eng_ = "act" if cc == 0 else "pool"
                    P.copy(eng_, Pbd[1][rows, :, cc * 64:(cc + 1) * 64], nst[rows, :, :])
                    P.copy(eng_, Qbd[1][rows, :, cc * 64:(cc + 1) * 64], q0[rows, :, :])
                for bg in range(2):
                    for st in range(6):
                        pst = nst if st == 0 else Pst[(st - 1) % 2]
                        qst = q0 if st == 0 else Qst[(st - 1) % 2]
                        pbd = Pbd[(st - 1) % 2]
                        qbd = Qbd[(st - 1) % 2]
                        gsrc = g0 if st == 1 else Gm[(st - 2) % 2]
                        for bt in (bg * 2, bg * 2 + 1):
                            base = (bt % 2) * 3
                            pP, pQ, pG = pb[base], pb[base + 1], pb[base + 2]
                            bs = slice(bt * 8, bt * 8 + 8)
                            for hl in range(8):
                                hd = bt * 8 + hl
                                cs_ = slice(hl * 64, (hl + 1) * 64)
                                if st <= 3:
                                    P.mm(pP[:, cs_], qbd[:, hd, :], pst[:, hd, :])
                                if st <= 4:
                                    P.mm(pQ[:, cs_], pbd[:, hd, :], qst[:, hd, :])
                                if st >= 1:
                                    P.mm(pG[:, cs_], qbd[:, hd, :], gsrc[:, hd, :])
                            v3 = lambda t_, r_=slice(0, 128): View(t_.b, t_.t[r_, :].rearrange("p (h m) -> p h m", m=64))
                            if st <= 3:
                                P.copy("act", Pst[st % 2][:, bs, :], v3(pP))
                                P.copy("act", Pbd[st % 2][0:64, bs, 0:64], v3(pP, slice(0, 64)))
                                P.copy("dve", Pbd[st % 2][64:128, bs, 64:128], v3(pP, slice(64, 128)))
                            if st <= 4:
                                if st <= 3:
                                    P.copy("act", Qst[st % 2][:, bs, :], v3(pQ))
                                P.copy("act", Qbd[st % 2][0:64, bs, 0:64], v3(pQ, slice(0, 64)))
                                P.copy("dve", Qbd[st % 2][64:128, bs, 64:128], v3(pQ, slice(64, 128)))
                            if st >= 1:
                                P.tt("dve", Gm[(st - 1) % 2][:, bs, :], v3(pG), gsrc[:, bs, :], ALU.add)
                P.dma("act", self.ttm[c][bi], Gm[0].t[:], reads=[Gm[0].b])
            P.flush()

    def phase_rwkv_scan(self):
        P = self.P
        S = self.S
        nblk = S // 128
        with contextlib.ExitStack() as es:
            gl = [T(P, es, "gl%d" % c, [128, 16, S // 64], F32) for c in range(2)]
            for c in range(2):
                P.dma("sp", gl[c].t[:], self.gam[c][:, :, :], writes=[gl[c].b])
            ST = [T(P, es, "ST%d" % c, [128, 16, 64], F32) for c in range(2)]
            STb = [[T(P, es, "STb%d_%d" % (c, i), [128, 16, 2, 64], BF16) for i in range(2)] for c in range(2)]
            for c in range(2):
                P.memset("pool", ST[c][:], 0.0)
                P.memset("pool", STb[c][0][:], 0.0)
                P.memset("pool", STb[c][1][:], 0.0)
            fK = [[T(P, es, "fK%d_%d" % (c, i), [128, 2, 16, 128], BF16) for i in range(2)] for c in range(2)]
            amt = [T(P, es, "samt%d" % c, [128, 16, 512], BF16) for c in range(2)]
            ttl = [[T(P, es, "ttl%d_%d" % (c, i), [128, 32, 64], BF16) for i in range(2)] for c in range(2)]
            akv = [[T(P, es, "akv%d_%d" % (c, i), [128, 3, D], BF16) for i in range(2)] for c in range(2)]
            yst = [T(P, es, "yst%d" % c, [128, D], F32) for c in range(2)]
            NB = 4
            Bt = [T(P, es, "Bt%d" % i, [128, 256], BF16) for i in range(NB)]
            Ut = [T(P, es, "Ut%d" % i, [128, 256], BF16) for i in range(NB)]
            tmpS = [T(P, es, "tmpS%d" % i, [128, 2, 64], F32) for i in range(NB)]
            pBU = [T(P, es, "pBU%d" % i, [128, 512], F32, psum=True) for i in range(4)]
            pMY = [T(P, es, "pMY%d" % i, [128, 512], F32, psum=True) for i in range(4)]

            def load(c, step):
                b = step if c == 0 else nblk - 1 - step
                t0 = b * 128
                sl = step % 2
                P.dma("sp", fK[c][sl].t[:], self.ft[c][0:2, :, :, t0:t0 + 128].rearrange("q p c t -> c q p t"), writes=[fK[c][sl].b])
                P.dma("sp", ttl[c][sl].t[:], self.ttm[c][b], writes=[ttl[c][sl].b])
                P.dma("sp", akv[c][sl].t[:, 0, :], self.ah[c][t0:t0 + 128, :], writes=[akv[c][sl].b])
                P.dma("sp", akv[c][sl].t[:, 1, :], self.kh[c][t0:t0 + 128, :], writes=[akv[c][sl].b])
                P.dma("sp", akv[c][sl].t[:, 2, :], self.vp_src[t0:t0 + 128, :], writes=[akv[c][sl].b])

            for c in range(2):
                load(c, 0)
            kctr = [0]
            for step in range(nblk):
                for c in range(2):
                    if step + 1 < nblk:
                        load(c, step + 1)
                    b = step if c == 0 else nblk - 1 - step
                    P.dma("sp", amt[c].t[:], self.am[c][b], writes=[amt[c].b])
                for ci_ in range(2):
                    combos = []
                    for e8 in range(8):
                        for c in range(2):
                            b = step if c == 0 else nblk - 1 - step
                            cc = ci_ if c == 0 else 1 - ci_
                            combos.append((c, b, cc, b * 2 + cc, e8))
                    nco = len(combos)
                    ctx = {}

                    def stageA(k):
                        c, b, cc, gch, e8 = combos[k]
                        sl = step % 2
                        R_ = slice(cc * 64, (cc + 1) * 64)
                        kk_ = kctr[0]
                        kctr[0] += 1
                        pbu = pBU[kk_ % 4]
                        pmy = pMY[kk_ % 4]
                        bt_ = Bt[kk_ % NB]
                        ut_ = Ut[kk_ % NB]
                        ctx[k] = (pbu, pmy, bt_, ut_, kk_)
                        stb = STb[c][(step * 2 + ci_) % 2]
                        for pl in range(2):
                            p = e8 * 2 + pl
                            P.mm(pbu[R_, pl * 128:(pl + 1) * 128], fK[c][sl][:, 0, p, R_], stb[:, p, :, :], start=True, stop=False)
                            for hh in range(2):
                                hd = p * 2 + hh
                                i = pl * 2 + hh
                                P.mm(pbu[R_, i * 64:(i + 1) * 64], amt[c][R_, p, hh * 256 + 128:hh * 256 + 192],
                                     akv[c][sl][R_, 2, hd * 64:(hd + 1) * 64], start=False, stop=(hh == 1))
                        P.copy("dve", bt_[R_, :], pbu[R_, 0:256])

                    def stageA2(k):
                        c, b, cc, gch, e8 = combos[k]
                        sl = step % 2
                        R_ = slice(cc * 64, (cc + 1) * 64)
                        pbu, pmy, bt_, ut_, kk_ = ctx[k]
                        for i in range(4):
                            hd = e8 * 4 + i
                            P.mm(pbu[R_, 256 + i * 64:256 + (i + 1) * 64], ttl[c][sl][R_, hd, :], bt_[R_, i * 64:(i + 1) * 64])
                        P.ts("dve", ut_[R_, :], pbu[R_, 256:512], -1.0, ALU.mult)

                    def stageC(k):
                        c, b, cc, gch, e8 = combos[k]
                        sl = step % 2
                        R_ = slice(cc * 64, (cc + 1) * 64)
                        pbu, pmy, bt_, ut_, kk_ = ctx[k]
                        stb = STb[c][(step * 2 + ci_) % 2]
                        stn = STb[c][(step * 2 + ci_ + 1) % 2]
                        a_ = akv[c][sl]
                        for i in range(4):
                            hd = e8 * 4 + i
                            p, hh = hd // 2, hd % 2
                            kr = slice(hh * 64, (hh + 1) * 64)
                            hc = slice(hd * 64, (hd + 1) * 64)
                            o = pmy[kr, (i // 2) * 64:(i // 2 + 1) * 64]
                            P.mm(o, a_[R_, 1, hc], a_[R_, 2, hc], start=True, stop=False)
                            P.mm(o, a_[R_, 0, hc], ut_[R_, i * 64:(i + 1) * 64], start=False, stop=True)
                        for pl in range(2):
                            p = e8 * 2 + pl
                            P.mm(pmy[R_, 256 + pl * 128:256 + (pl + 1) * 128], fK[c][sl][:, 1, p, R_], stb[:, p, :, :], start=True, stop=False)
                            for hh in range(2):
                                hd = p * 2 + hh
                                i = pl * 2 + hh
                                hc = slice(hd * 64, (hd + 1) * 64)
                                o = pmy[R_, 256 + i * 64:256 + (i + 1) * 64]
                                P.mm(o, amt[c][R_, p, hh * 256 + 64:hh * 256 + 128], ut_[R_, i * 64:(i + 1) * 64], start=False, stop=False)
                                P.mm(o, amt[c][R_, p, hh * 256 + 192:hh * 256 + 256], a_[R_, 2, hc], start=False, stop=(hh == 1))
                        P.copy("dve", yst[c][R_, e8 * 256:(e8 + 1) * 256], pmy[R_, 256:512])
                        ps2 = slice(e8 * 2, e8 * 2 + 2)
                        tm = tmpS[kk_ % NB]
                        P.tt("dve", tm[:], ST[c][:, ps2, :], View(gl[c].b, gl[c].t[:, ps2, gch:gch + 1].broadcast_to([128, 2, 64])), ALU.mult)
                        P.tt("dve", ST[c][:, ps2, :], tm[:], pmy.v(pmy.t[:, 0:128].rearrange("p (a v) -> p a v", v=64)), ALU.add)
                        for hh in range(2):
                            kr = slice(hh * 64, (hh + 1) * 64)
                            P.copy("pool", stn[kr, ps2, hh, :], ST[c][kr, ps2, :])

                    for k in range(nco + 2):
                        if k < nco:
                            stageA(k)
                        if 0 <= k - 1 < nco:
                            stageA2(k - 1)
                        if 0 <= k - 2 < nco:
                            stageC(k - 2)
                for c in range(2):
                    b = step if c == 0 else nblk - 1 - step
                    P.dma("act", self.yd[c][b * 128:(b + 1) * 128, :], yst[c].t[:], reads=[yst[c].b])
            P.flush()

    def phase_rwkv_post(self, j):
        P = self.P
        S = self.S
        R = self.in_rw
        nblk = S // 128
        with contextlib.ExitStack() as es:
            gG = self.load_bcast(es, "gnG", R["rwkv_gn_g"][j:j + 1, :])
            gB = self.load_bcast(es, "gnB", R["rwkv_gn_b"][j:j + 1, :])
            yin = [T(P, es, "yin%d" % i, [128, 2, D], F32) for i in range(2)]
            vz = [T(P, es, "vz%d" % i, [128, 2, D], BF16) for i in range(2)]
            bn = [T(P, es, "bn%d" % i, [128, NHR], F32) for i in range(2)]
            y = T(P, es, "ypost", [128, D], F32)
            sq = T(P, es, "sqpost", [128, D], F32)
            yo = [T(P, es, "yo%d" % i, [128, D], BF16) for i in range(2)]
            sm = [T(P, es, "smp%d" % i, [128, 6, NHR], F32) for i in range(2)]
            tp = [T(P, es, "tpp%d" % i, [128, 1024], BF16, psum=True) for i in range(4)]
            hs = [T(P, es, "hsp%d" % i, [128, NCH, 512], BF16) for i in range(2)]
            ident = self.cbv(CB_IDENT)

            def load(bi):
                t0 = bi * 128
                sl = bi % 2
                P.dma("sp", yin[sl].t[:, 0, :], self.yd[0][t0:t0 + 128, :], writes=[yin[sl].b])
                P.dma("sp", yin[sl].t[:, 1, :], self.yd[1][t0:t0 + 128, :], writes=[yin[sl].b])
                P.dma("sp", vz[sl].t[:, 0, :], self.vp_src[t0:t0 + 128, :], writes=[vz[sl].b])
                P.dma("sp", vz[sl].t[:, 1, :], self.rw_sz[t0:t0 + 128, :], writes=[vz[sl].b])
                P.dma("sp", bn[sl].t[:], self.bon[t0:t0 + 128, :], writes=[bn[sl].b])

            def v3(t_, ap=None):
                a = t_.t[:, :] if ap is None else ap
                return View(t_.b, a.rearrange("p (h n) -> p h n", n=NR))

            def bc(t_, ap):
                return View(t_.b, ap.unsqueeze(2).to_broadcast([128, NHR, NR]))

            load(0)
            k = 0
            for bi in range(nblk):
                if bi + 1 < nblk:
                    load(bi + 1)
                sl = bi % 2
                s_ = sm[sl]
                P.tt("pool", y[:], yin[sl][:, 0, :], yin[sl][:, 1, :], ALU.add)
                P.tt("pool", sq[:], y[:], y[:], ALU.mult)
                P.add("dve", lambda e, o=s_.t[:, 0, :], i=y.t[:, :].rearrange("p (h n) -> p h n", n=NR): e.tensor_reduce(out=o, in_=i, axis=AX.X, op=ALU.add),
                      reads=[y.b], writes=[s_.b])
                P.add("dve", lambda e, o=s_.t[:, 1, :], i=sq.t[:, :].rearrange("p (h n) -> p h n", n=NR): e.tensor_reduce(out=o, in_=i, axis=AX.X, op=ALU.add),
                      reads=[sq.b], writes=[s_.b])
                P.ts("dve", s_[:, 2, :], s_[:, 0, :], 1.0 / NR, ALU.mult)
                P.tt("dve", s_[:, 3, :], s_[:, 2, :], s_[:, 2, :], ALU.mult)
                P.stt(s_[:, 4, :], s_[:, 1, :], 1.0 / NR, s_[:, 3, :], ALU.mult, ALU.subtract)
                P.ts("dve", s_[:, 4, :], s_[:, 4, :], GN_EPS, ALU.add)
                P.act(s_[:, 5, :], s_[:, 4, :], AF.Sqrt)
                P.recip(s_[:, 4, :], s_[:, 5, :])
                P.tt("pool", v3(y), v3(y), bc(s_, s_.t[:, 2, :]), ALU.subtract)
                P.tt("pool", v3(y), v3(y), bc(s_, s_.t[:, 4, :]), ALU.mult)
                P.tt("dve", y[:], y[:], gG[:], ALU.mult)
                P.tt("dve", y[:], y[:], gB[:], ALU.add)
                P.tt("pool", v3(sq), View(vz[sl].b, vz[sl].t[:, 0, :].rearrange("p (h n) -> p h n", n=NR)), bc(bn[sl], bn[sl].t[:, :]), ALU.mult)
                P.tt("pool", y[:], y[:], sq[:], ALU.add)
                o = yo[bi % 2]
                P.tt("dve", o[:], y[:], vz[sl][:, 1, :], ALU.mult)
                u = bi // 4
                sub = bi % 4
                hst = hs[u % 2]
                for half in range(2):
                    pt = tp[k % 4]
                    k += 1
                    for c in range(8):
                        cc = half * 8 + c
                        P.transpose(pt[:, c * 128:(c + 1) * 128], o[:, cc * 128:(cc + 1) * 128], ident)
                    src = pt.v(pt.t[:, :].rearrange("p (c t) -> p c t", t=128))
                    dst = hst[:, half * 8:(half + 1) * 8, sub * 128:(sub + 1) * 128]
                    P.copy("act", dst, src)
                if sub == 3 or bi == nblk - 1:
                    t0 = u * 512
                    n = (sub + 1) * 128
                    P.dma("sp", self.yT.rearrange("(c p) t -> p c t", p=128)[:, :, t0:t0 + n], hst.t[:, :, 0:n], reads=[hst.b])
            P.flush()


S_FULL = 8192
_CACHE = {}


def kernel(**inputs):
    S = S_FULL
    x_prompt = np.asarray(inputs["x_prompt"], dtype=np.float32)
    x_sample = np.asarray(inputs["x_sample"], dtype=np.float32)
    b = Builder(S)
    nc = b.build()
    cb, cf = _const_tables(S)
    xs = []
    valid_lens = []
    xs.append(np.ascontiguousarray(x_prompt[0]))
    valid_lens.append(S)
    for i in range(4):
        xp = np.zeros((S, D), np.float32)
        xp[:4096] = x_sample[i]
        xs.append(xp)
        valid_lens.append(4096)
    for i in range(3):
        xs.append(np.zeros((S, D), np.float32))
        valid_lens.append(S)
    shared = {}
    for k, v in inputs.items():
        if k in ("x_prompt", "x_sample"):
            continue
        a = np.ascontiguousarray(np.asarray(v, dtype=np.float32))
        if k == "rwkv_r_k":
            a = a.reshape(2, 2, D)
        shared[k] = a
    in_maps = []
    for c in range(8):
        m = dict(shared)
        m["x"] = xs[c]
        m["const_bf"] = cb
        m["const_f32"] = cf
        m["valid"] = _valid_tables(S, valid_lens[c])
        m["tokmask"] = _tokmask(S, valid_lens[c])
        in_maps.append(m)
    res = run_bass_kernel_spmd(nc, in_maps, core_ids=list(range(8)))
    y_prompt = np.asarray(res.results[0]["y"], dtype=np.float32)[None]
    y_sample = np.stack([np.asarray(res.results[1 + i]["y"], dtype=np.float32)[:4096] for i in range(4)], axis=0)
    return (y_prompt, y_sample)
```

```python
import contextlib
import math
import numpy as np
import ml_dtypes
import concourse.bass as bass
import concourse.mybir as mybir
from concourse.bass_utils import run_bass_kernel_spmd

F32 = mybir.dt.float32
BF16 = mybir.dt.bfloat16
AF = mybir.ActivationFunctionType
ALU = mybir.AluOpType
AX = mybir.AxisListType

D = 2048
NCH = 16
NH_ATT = 16
DH = 128
GROUPS = ((128, 1), (512, 4), (2048, 16))
ATT_COLS = 20480
RMS_EPS = 1e-6
GN_EPS = 64e-5
NHR = 32
NR = 64
CH = 64
SAME_ENGINE_SYNC = True


class Buf:
    __slots__ = ("name", "w", "r")

    def __init__(self, name):
        self.name = name
        self.w = None
        self.r = []


class View:
    __slots__ = ("b", "ap")

    def __init__(self, b, ap):
        self.b = b
        self.ap = ap


class T:
    def __init__(self, prog, es, name, shape, dtype, psum=False):
        nc = prog.nc
        prog.uid += 1
        name = "%s_u%d" % (name, prog.uid)
        self.t = es.enter_context(nc.psum_tensor(name, shape, dtype) if psum else nc.sbuf_tensor(name, shape, dtype))
        self.b = Buf(name)
        prog.bufs.append(self.b)

    def __getitem__(self, idx):
        return View(self.b, self.t[idx])

    def v(self, ap):
        return View(self.b, ap)


class Op:
    __slots__ = ("eng", "fn", "deps", "is_dma", "needs_inc", "ms", "dma_idx")


class Prog:
    CE = ("pe", "act", "dve", "pool")
    QS = ("sp", "act", "pool")

    def __init__(self, nc, es, K=8):
        self.nc = nc
        self.engs = {"pe": nc.tensor, "act": nc.scalar, "dve": nc.vector, "pool": nc.gpsimd, "sp": nc.sync}
        self.sem = {e: es.enter_context(nc.semaphore("sem_" + e)) for e in self.CE}
        self.K = K
        self.dsem = {q: [es.enter_context(nc.semaphore("dsem_%s%d" % (q, i))) for i in range(K)] for q in self.QS}
        self.ms = {e: 0 for e in self.CE}
        self.dcount = {q: 0 for q in self.QS}
        self.seen = {}
        self.ops = []
        self.bufs = []
        self.n_instr = 0
        self.uid = 0

    def buf(self, name):
        b = Buf(name)
        self.bufs.append(b)
        return b

    def add(self, eng, fn, reads=(), writes=(), dma=False):
        op = Op()
        op.eng = eng
        op.fn = fn
        op.is_dma = dma
        op.needs_inc = False
        op.ms = 0
        op.dma_idx = -1
        deps = {}
        for b in reads:
            if b.w is not None:
                deps[id(b.w)] = b.w
        for b in writes:
            if b.w is not None:
                deps[id(b.w)] = b.w
            for o in b.r:
                deps[id(o)] = o
        for b in reads:
            if dma:
                b.r.append(op)
            else:
                b.r = [o for o in b.r if o.is_dma or o.eng != eng]
                b.r.append(op)
        for b in writes:
            b.w = op
            b.r = []
        dl = []
        for d in deps.values():
            if d is op:
                continue
            if (not d.is_dma) and (not dma) and d.eng == eng:
                if eng == "pe" or not SAME_ENGINE_SYNC:
                    continue
            dl.append(d)
            if not d.is_dma:
                d.needs_inc = True
        op.deps = dl
        if dma:
            op.dma_idx = self.dcount[eng]
            self.dcount[eng] += 1
        self.ops.append(op)
        return op

    def _wait(self, eng, key, sem, val):
        k = (eng, key)
        if self.seen.get(k, 0) >= val:
            return
        self.seen[k] = val
        self.engs[eng].wait_ge(sem, val)

    def flush(self):
        K = self.K
        last = {}
        for op in self.ops:
            if not op.is_dma:
                last[op.eng] = op
        for op in last.values():
            op.needs_inc = True
        for op in self.ops:
            e = self.engs[op.eng]
            if op.is_dma:
                i = op.dma_idx
                q = op.eng
                if i >= K:
                    self._wait(q, ("d", q, i % K), self.dsem[q][i % K], 16 * (i // K))
            for d in op.deps:
                if d.is_dma:
                    j = d.dma_idx
                    self._wait(op.eng, ("d", d.eng, j % K), self.dsem[d.eng][j % K], 16 * (j // K + 1))
                else:
                    self._wait(op.eng, ("c", d.eng), self.sem[d.eng], d.ms)
            ins = op.fn(e)
            self.n_instr += 1
            if op.is_dma:
                ins.then_inc(self.dsem[op.eng][op.dma_idx % K], 16)
            elif op.needs_inc:
                self.ms[op.eng] += 1
                op.ms = self.ms[op.eng]
                ins.then_inc(self.sem[op.eng], 1)
        for eng in ("pe", "act", "dve", "pool", "sp"):
            for c in self.CE:
                if self.ms[c] > 0:
                    self._wait(eng, ("c", c), self.sem[c], self.ms[c])
            for q in self.QS:
                n = self.dcount[q]
                for j in range(K):
                    cnt = (n - j + K - 1) // K if n > j else 0
                    if cnt > 0:
                        self._wait(eng, ("d", q, j), self.dsem[q][j], 16 * cnt)
        for b in self.bufs:
            b.w = None
            b.r = []
        self.bufs = [b for b in self.bufs if not b.name.startswith("~")]
        self.ops = []

    def mm(self, out, lhsT, rhs, start=True, stop=True, **kw):
        return self.add("pe", lambda e: e.matmul(out.ap, lhsT.ap, rhs.ap, start=start, stop=stop, **kw),
                        reads=[lhsT.b, rhs.b], writes=[out.b])

    def transpose(self, out, in_, ident):
        return self.add("pe", lambda e: e.transpose(out.ap, in_.ap, ident.ap), reads=[in_.b, ident.b], writes=[out.b])

    def act(self, out, in_, func, bias=None, scale=None, accum_out=None, extra_reads=()):
        kw = {}
        reads = [in_.b] + list(extra_reads)
        writes = [out.b]
        if bias is not None:
            if isinstance(bias, View):
                kw["bias"] = bias.ap
                reads.append(bias.b)
            else:
                kw["bias"] = bias
        if scale is not None:
            if isinstance(scale, View):
                kw["scale"] = scale.ap
                reads.append(scale.b)
            else:
                kw["scale"] = scale
        if accum_out is not None:
            kw["accum_out"] = accum_out.ap
            writes.append(accum_out.b)
        return self.add("act", lambda e: e.activation(out=out.ap, in_=in_.ap, func=func, **kw), reads=reads, writes=writes)

    def tt(self, eng, out, in0, in1, op):
        return self.add(eng, lambda e: e.tensor_tensor(out=out.ap, in0=in0.ap, in1=in1.ap, op=op),
                        reads=[in0.b, in1.b], writes=[out.b])

    def ts(self, eng, out, in0, s1, op0, s2=None, op1=None):
        reads = [in0.b]
        a1 = s1
        a2 = s2
        if isinstance(s1, View):
            a1 = s1.ap
            reads.append(s1.b)
        if isinstance(s2, View):
            a2 = s2.ap
            reads.append(s2.b)
        if op1 is None:
            return self.add(eng, lambda e: e.tensor_scalar(out=out.ap, in0=in0.ap, scalar1=a1, scalar2=None, op0=op0),
                            reads=reads, writes=[out.b])
        return self.add(eng, lambda e: e.tensor_scalar(out=out.ap, in0=in0.ap, scalar1=a1, scalar2=a2, op0=op0, op1=op1),
                        reads=reads, writes=[out.b])

    def stt(self, out, in0, scalar, in1, op0, op1):
        reads = [in0.b, in1.b]
        sc = scalar
        if isinstance(scalar, View):
            sc = scalar.ap
            reads.append(scalar.b)
        return self.add("dve", lambda e: e.scalar_tensor_tensor(out=out.ap, in0=in0.ap, scalar=sc, in1=in1.ap, op0=op0, op1=op1),
                        reads=reads, writes=[out.b])

    def copy(self, eng, out, in_):
        if eng == "act":
            return self.add("act", lambda e: e.copy(out=out.ap, in_=in_.ap), reads=[in_.b], writes=[out.b])
        return self.add(eng, lambda e: e.tensor_copy(out=out.ap, in_=in_.ap), reads=[in_.b], writes=[out.b])

    def recip(self, out, in_):
        return self.add("dve", lambda e: e.reciprocal(out=out.ap, in_=in_.ap), reads=[in_.b], writes=[out.b])

    def memset(self, eng, out, val):
        return self.add(eng, lambda e: e.memset(out.ap, val), reads=[], writes=[out.b])

    def dma(self, q, out, in_, reads=(), writes=(), **kw):
        return self.add(q, lambda e: e.dma_start(out=out, in_=in_, **kw), reads=list(reads), writes=list(writes), dma=True)


def _const_tables(S):
    bf = ml_dtypes.bfloat16
    ident = np.eye(128, dtype=np.float32)
    rot = np.zeros((128, 128), np.float32)
    for m in range(64):
        rot[m + 64, m] = -1.0
    for m in range(64, 128):
        rot[m - 64, m] = 1.0
    i = np.arange(128)[:, None]
    j = np.arange(128)[None, :]
    m0 = (i >= j).astype(np.float32)
    m1 = (i <= j).astype(np.float32)
    ones = np.ones((128, 128), np.float32)
    s = np.arange(64)[:, None]
    t = np.arange(64)[None, :]
    su = (s < t).astype(np.float32)
    iu = (s <= t).astype(np.float32)
    sl = (s > t).astype(np.float32)
    il = (s >= t).astype(np.float32)
    mf = np.tile(np.concatenate([su, iu, su, iu], axis=1), (2, 1))
    mb = np.tile(np.concatenate([sl, il, sl, il], axis=1), (2, 1))
    lf = np.tile(np.concatenate([sl, sl], axis=1), (2, 1))
    lb = np.tile(np.concatenate([su, su], axis=1), (2, 1))
    i64 = np.tile(np.concatenate([np.eye(64, dtype=np.float32)] * 2, axis=1), (2, 1))
    cb = np.concatenate([ident, rot, m0, m1, ones, mf, mb, lf, lb, i64], axis=1).astype(bf)
    half = 64
    inv_freq = (1.0 / (np.float32(10000.0) ** (np.arange(half, dtype=np.float32) * np.float32(2.0) / np.float32(128)))).astype(np.float32)
    ang = (np.arange(S, dtype=np.float32)[:, None] * inv_freq[None, :]).astype(np.float32)
    cos = np.cos(ang).astype(np.float32).T
    sin = np.sin(ang).astype(np.float32).T
    cs = np.concatenate([np.concatenate([cos, cos], 0), np.concatenate([sin, sin], 0)], axis=1)
    s2 = np.arange(128)[:, None]
    t2 = np.arange(128)[None, :]
    same = (s2 // 64) == (t2 // 64)
    p_i = (same & (s2 <= t2)).astype(np.float32)
    s_i = (same & (s2 >= t2)).astype(np.float32)
    p_s = (same & (s2 < t2)).astype(np.float32)
    s_s = (same & (s2 > t2)).astype(np.float32)
    ind = np.zeros((128, 2), np.float32)
    ind[:64, 0] = 1.0
    ind[64:, 1] = 1.0
    cf = np.concatenate([cs, p_i, s_i, p_s, s_s, ind], axis=1).astype(np.float32)
    return cb, cf


CB_IDENT, CB_ROT, CB_M0, CB_M1, CB_ONES, CB_MF, CB_MB, CB_LF, CB_LB, CB_I64 = 0, 128, 256, 384, 512, 640, 896, 1152, 1280, 1408
CB_W = 1536


def _valid_tables(S, valid_len):
    cols = []
    for (_, d) in GROUPS:
        L = S // d
        nblk = L // 128 + 1
        for r in range(d):
            n = np.arange(nblk * 128) - 64
            pos = n * d + r
            ok = (n >= 0) & (n < L) & (pos < valid_len)
            cols.append(ok.reshape(nblk, 128).T.astype(np.float32))
    return np.concatenate(cols, axis=1).astype(ml_dtypes.bfloat16)


def _tokmask(S, valid_len):
    return np.broadcast_to((np.arange(S) < valid_len).astype(np.float32)[None, :], (128, S)).astype(ml_dtypes.bfloat16).copy()


class Builder:
    def __init__(self, S, n_layers=4, debug=None, debug_out=(), rw_stop=99):
        self.debug_out = set(debug_out)
        self.rw_stop = rw_stop
        self.S = S
        self.n_layers = n_layers
        self.debug = debug or {}
        self.nc = bass.Bass("TRN2", target_bir_lowering=False)
        self.es = contextlib.ExitStack()

    def dram(self, name, shape, dtype, kind="Internal"):
        if name in self.debug_out:
            kind = "ExternalOutput"
        return self.nc.dram_tensor(name, list(shape), dtype, kind=kind).ap()

    def build(self):
        nc = self.nc
        S = self.S
        with self.es as es:
            self.P = Prog(nc, es)
            self.declare_io()
            self.persistent(es)
            self.prologue()
            x_cur = self.x_in
            bufs = [self.xa, self.xb]
            for layer in range(self.n_layers):
                import os as _os
                if int(_os.environ.get("ATTSTOP", "9")) == 0:
                    break
                x_next = self.y_out if layer == self.n_layers - 1 else bufs[layer % 2]
                j = layer // 2
                self.phase_norm(x_cur, layer)
                if layer % 2 == 0:
                    import os as _os
                    _as = int(_os.environ.get("ATTSTOP", "9"))
                    if _as >= 2:
                        self.phase_att_inproj(j)
                    if _as >= 3:
                        self.phase_att_core()
                    if _as >= 4:
                        self.phase_out(x_cur, x_next, self.wb_att_out[j], layer)
                else:
                    self.phase_rwkv(j)
                    self.phase_out(x_cur, x_next, self.wb_rwkv_out[j], layer)
                x_cur = x_next
        return nc

    def declare_io(self):
        S = self.S
        d = self.dram
        self.x_in = d("x", [S, D], F32, "ExternalInput")
        self.y_out = d("y", [S, D], F32, "ExternalOutput")
        self.in_norm_pre = d("norm_pre", [4, D], F32, "ExternalInput")
        self.in_norm_post = d("norm_post", [4, D], F32, "ExternalInput")
        self.in_att_w_in = d("att_w_in", [2, D, ATT_COLS], F32, "ExternalInput")
        self.in_att_w_out = d("att_w_out", [2, D, D], F32, "ExternalInput")
        self.in_rw = {}
        for name, shape in (("rwkv_mu_prev", [2, 6, D]), ("rwkv_mu_next", [2, 6, D]), ("rwkv_w_in", [2, D, 4 * D]),
                            ("rwkv_w0", [2, 2, D]), ("rwkv_w1", [2, 2, D, 96]), ("rwkv_w2", [2, 2, 96, D]),
                            ("rwkv_a0", [2, 2, D]), ("rwkv_a1", [2, 2, D, 96]), ("rwkv_a2", [2, 2, 96, D]),
                            ("rwkv_v0", [1, D]), ("rwkv_v1", [1, D, 64]), ("rwkv_v2", [1, 64, D]),
                            ("rwkv_k_k", [2, D]), ("rwkv_k_a", [2, D]), ("rwkv_r_k", [2, 2, D]),
                            ("rwkv_gn_g", [2, D]), ("rwkv_gn_b", [2, D]), ("rwkv_w_out", [2, D, D])):
            self.in_rw[name] = d(name, shape, F32, "ExternalInput")
        self.in_cb = d("const_bf", [128, CB_W], BF16, "ExternalInput")
        self.cf_w = 2 * S + 4 * 128 + 2
        self.in_cf = d("const_f32", [128, self.cf_w], F32, "ExternalInput")
        self.nvalid = sum(dd * ((S // dd) // 128 + 1) for (_, dd) in GROUPS)
        self.in_valid = d("valid", [128, self.nvalid], BF16, "ExternalInput")
        self.in_tokmask = d("tokmask", [128, S], BF16, "ExternalInput")
        self.xa = d("xa", [S, D], F32)
        self.xb = d("xb", [S, D], F32)
        self.hT = d("hT", [NCH, 128, S + 2], BF16)
        self.wb_att_in = [d("wb_att_in%d" % j, [D, ATT_COLS], BF16) for j in range(2)]
        self.wb_att_out = [d("wb_att_out%d" % j, [D, D], BF16) for j in range(2)]
        self.wb_rwkv_in = [d("wb_rwkv_in%d" % j, [D, 4 * D], BF16) for j in range(2)]
        self.wb_rwkv_out = [d("wb_rwkv_out%d" % j, [D, D], BF16) for j in range(2)]
        self.qT = []
        self.kT = []
        self.vv = []
        for g, (_, dd) in enumerate(GROUPS):
            L = S // dd
            self.qT.append(d("qT%d" % g, [NH_ATT, 128, dd, L], BF16))
            self.kT.append(d("kT%d" % g, [NH_ATT, 128, dd, L + 128], BF16))
            self.vv.append(d("vv%d" % g, [dd, L + 128, D], BF16))
        self.zT = d("zT", [D, S], BF16)
        self.yT = d("yT", [D, S], BF16)
        self.declare_rwkv()
        for name, (shape, dt) in self.debug.items():
            setattr(self, "dbg_" + name, d("dbg_" + name, shape, dt, "ExternalOutput"))

    def persistent(self, es):
        P = self.P
        self.cb = T(P, es, "cb", [128, CB_W], BF16)
        P.dma("sp", self.cb.t[:], self.in_cb[:, :], writes=[self.cb.b])
        self.zeros = T(P, es, "zeros", [128, 2048], BF16)
        P.memset("pool", self.zeros[:], 0.0)
        P.flush()

    def cbv(self, off, w=128, rows=128):
        return self.cb[0:rows, off:off + w]

    def prologue(self):
        P = self.P
        S = self.S

        def cast(dst, src, rows, cols):
            cw = min(cols, 2048)
            nseg = cols // cw
            rstep = max(1, 4096 // nseg)
            for r0 in range(0, rows, rstep):
                r1 = min(rows, r0 + rstep)
                if nseg == 1:
                    o = dst[r0:r1, :]
                    i = src[r0:r1, :]
                else:
                    o = dst[r0:r1, :].rearrange("r (s c) -> r s c", c=cw)
                    i = src[r0:r1, :].rearrange("r (s c) -> r s c", c=cw)
                P.dma("pool", o, i)

        nl = self.n_layers
        for j in range(2):
            if nl > 2 * j:
                cast(self.wb_att_in[j], self.in_att_w_in[j], D, ATT_COLS)
                cast(self.wb_att_out[j], self.in_att_w_out[j], D, D)
            if nl > 2 * j + 1:
                cast(self.wb_rwkv_in[j], self.in_rw["rwkv_w_in"][j], D, 4 * D)
                cast(self.wb_rwkv_out[j], self.in_rw["rwkv_w_out"][j], D, D)
        if nl > 1:
            cast(self.wb_w1, self.in_rw["rwkv_w1"].rearrange("j c d r -> (j c d) r"), 4 * D, 96)
            cast(self.wb_a1, self.in_rw["rwkv_a1"].rearrange("j c d r -> (j c d) r"), 4 * D, 96)
            cast(self.wb_w2, self.in_rw["rwkv_w2"].rearrange("j c r d -> (j c r) d"), 4 * 96, D)
            cast(self.wb_a2, self.in_rw["rwkv_a2"].rearrange("j c r d -> (j c r) d"), 4 * 96, D)
            cast(self.wb_v1, self.in_rw["rwkv_v1"][0], D, 64)
            cast(self.wb_v2, self.in_rw["rwkv_v2"][0], 64, D)
        for g, (_, dd) in enumerate(GROUPS):
            L = S // dd
            for h in range(NH_ATT):
                for side in (0, L + 64):
                    P.dma("sp", self.kT[g][h][:, :, side:side + 64],
                          self.zeros.t[:, 0:dd * 64].rearrange("p (r n) -> p r n", n=64), reads=[self.zeros.b])
            for r in range(dd):
                for side in (0, L + 64):
                    P.dma("sp", self.vv[g][r, side:side + 64, :], self.zeros.t[0:64, :], reads=[self.zeros.b])
        P.flush()

    def load_bcast(self, es, name, src_row):
        P = self.P
        t = T(P, es, name, [128, D], F32)
        P.dma("sp", t.t[:], src_row.partition_broadcast(128), writes=[t.b])
        return t

    def rstd_from_ss(self, ss, tmp, rstd, eps, n):
        P = self.P
        P.act(tmp, ss, AF.Sqrt, bias=self.eps_t[eps], scale=1.0 / n)
        P.recip(rstd, tmp)

    def phase_norm(self, x_src, layer):
        P = self.P
        S = self.S
        with contextlib.ExitStack() as es:
            g = self.load_bcast(es, "g_pre", self.in_norm_pre[layer:layer + 1, :])
            self.eps_t = {}
            epst = T(P, es, "epst", [128, 2], F32)
            P.memset("pool", epst[:, 0:1], RMS_EPS)
            self.eps_t[RMS_EPS] = epst[:, 0:1]
            NS = 3
            xt = [T(P, es, "xt%d" % i, [128, D], F32) for i in range(NS)]
            junk = T(P, es, "junk", [128, D], BF16)
            hb = [T(P, es, "hb%d" % i, [128, D], BF16) for i in range(2)]
            st = [T(P, es, "st%d" % i, [128, 3], F32) for i in range(4)]
            tp = [T(P, es, "tp%d" % i, [128, 1024], BF16, psum=True) for i in range(4)]
            hs = [T(P, es, "hs%d" % i, [128, NCH, 512], BF16) for i in range(2)]
            ident = self.cbv(CB_IDENT)
            nblk = S // 128
            for i in range(min(2, nblk)):
                P.dma("sp", xt[i % NS].t[:], x_src[i * 128:(i + 1) * 128, :], writes=[xt[i % NS].b])
            k = 0
            for i in range(nblk):
                if i + 2 < nblk:
                    P.dma("sp", xt[(i + 2) % NS].t[:], x_src[(i + 2) * 128:(i + 3) * 128, :], writes=[xt[(i + 2) % NS].b])
                x = xt[i % NS]
                s = st[i % 4]
                P.act(junk[:], x[:], AF.Square, accum_out=s[:, 0:1])
                self.rstd_from_ss(s[:, 0:1], s[:, 1:2], s[:, 2:3], RMS_EPS, D)
                h = hb[i % 2]
                P.stt(h[:], x[:], s[:, 2:3], g[:], ALU.mult, ALU.mult)
                u = i // 4
                sub = i % 4
                hst = hs[u % 2]
                for half in range(2):
                    pt = tp[k % 4]
                    k += 1
                    for c in range(8):
                        cc = half * 8 + c
                        P.transpose(pt[:, c * 128:(c + 1) * 128], h[:, cc * 128:(cc + 1) * 128], ident)
                    src = pt.v(pt.t[:, :].rearrange("p (c t) -> p c t", t=128))
                    dst = hst[:, half * 8:(half + 1) * 8, sub * 128:(sub + 1) * 128]
                    if half == 0:
                        P.copy("act", dst, src)
                    else:
                        P.copy("dve", dst, src)
                if sub == 3 or i == nblk - 1:
                    t0 = u * 512
                    n = (sub + 1) * 128
                    P.dma("sp", self.hT[:, :, 1 + t0:1 + t0 + n].rearrange("c p t -> p c t"), hst.t[:, :, 0:n], reads=[hst.b])
            zc = T(P, es, "zc", [128, NCH, 2], BF16)
            P.memset("pool", zc[:], 0.0)
            P.dma("sp", self.hT[:, :, 0:1].rearrange("c p t -> p c t"), zc.t[:, :, 0:1], reads=[zc.b], allow_slow_non_contiguous=True)
            P.dma("sp", self.hT[:, :, S + 1:S + 2].rearrange("c p t -> p c t"), zc.t[:, :, 1:2], reads=[zc.b], allow_slow_non_contiguous=True)
            P.flush()

    def phase_att_inproj(self, j):
        P = self.P
        S = self.S
        ST = min(2048, S)
        W = self.wb_att_in[j]
        with contextlib.ExitStack() as es:
            hts = T(P, es, "hts", [128, NCH, ST], BF16)
            cs = T(P, es, "cs", [128, 2, ST], F32)
            NW = 3
            wt = [T(P, es, "wt%d" % i, [128, NCH, 512], BF16) for i in range(NW)]
            ps = [T(P, es, "ps%d" % i, [128, 512], F32, psum=True) for i in range(3)]
            pr = [T(P, es, "pr%d" % i, [128, 512], F32, psum=True) for i in range(2)]
            tb = [T(P, es, "tb%d" % i, [128, 512], BF16) for i in range(3)]
            t1 = [T(P, es, "t1_%d" % i, [128, 512], F32) for i in range(3)]
            t2 = [T(P, es, "t2_%d" % i, [128, 512], F32) for i in range(3)]
            qs = [T(P, es, "qs%d" % i, [128, ST], BF16) for i in range(3)]
            vs = [T(P, es, "vs%d" % i, [128, ST // 128, 512], BF16) for i in range(2)]
            rot = self.cbv(CB_ROT)
            NCG = ATT_COLS // 512
            Wv = W.rearrange("(c p) n -> p c n", p=128)
            cnt = {"ps": 0, "pr": 0, "tb": 0, "qs": 0, "vs": 0, "ev": 0}

            def load_w(cg):
                P.dma("sp", wt[cg % NW].t[:], Wv[:, :, cg * 512:(cg + 1) * 512], writes=[wt[cg % NW].b])

            for st_i in range(S // ST):
                t0 = st_i * ST
                P.dma("sp", hts.t[:], self.hT[:, :, 1 + t0:1 + t0 + ST].rearrange("c p t -> p c t"), writes=[hts.b])
                P.dma("sp", cs.t[:, 0, :], self.in_cf[:, t0:t0 + ST], writes=[cs.b])
                P.dma("sp", cs.t[:, 1, :], self.in_cf[:, S + t0:S + t0 + ST], writes=[cs.b])
                load_w(0)
                load_w(1)
                for cg in range(NCG):
                    if cg + 2 < NCG:
                        load_w(cg + 2)
                    w = wt[cg % NW]
                    if cg < 36:
                        g = cg // 12
                        typ = (cg % 12) // 4
                        h0 = (cg % 4) * 4
                    else:
                        g = -1
                        typ = 3
                        h0 = (cg - 36) * 4
                    if typ == 2:
                        dd = GROUPS[g][1]
                        vst = vs[cnt["vs"] % 2]
                        cnt["vs"] += 1
                        for tbi in range(ST // 128):
                            p = ps[cnt["ps"] % 3]
                            cnt["ps"] += 1
                            for c in range(NCH):
                                P.mm(p[:], hts[:, c, tbi * 128:(tbi + 1) * 128], w[:, c, :], start=(c == 0), stop=(c == NCH - 1))
                            if cnt["ev"] % 2 == 0:
                                P.copy("act", vst[:, tbi, :], p[:])
                            else:
                                P.copy("dve", vst[:, tbi, :], p[:])
                            cnt["ev"] += 1
                        npb = 128 // dd
                        for r in range(dd):
                            n0 = 64 + t0 // dd
                            dst = self.vv[g][r, n0:n0 + ST // dd, h0 * 128:h0 * 128 + 512].rearrange("(b n) c -> n b c", n=npb)
                            src = vst.t[r::dd, :, :] if dd > 1 else vst.t[:, :, :]
                            P.dma("sp", dst, src, reads=[vst.b])
                    else:
                        for jh in range(4):
                            h = h0 + jh
                            qst = qs[cnt["qs"] % 3]
                            cnt["qs"] += 1
                            dd = GROUPS[g][1] if typ < 2 else 1
                            for sub in range(ST // 512):
                                p = ps[cnt["ps"] % 3]
                                cnt["ps"] += 1
                                for c in range(NCH):
                                    P.mm(p[:], w[:, c, jh * 128:(jh + 1) * 128], hts[:, c, sub * 512:(sub + 1) * 512],
                                         start=(c == 0), stop=(c == NCH - 1))
                                if typ == 3:
                                    P.act(qst[:, sub * 512:(sub + 1) * 512], p[:], AF.Silu)
                                    continue
                                k = cnt["tb"] % 3
                                cnt["tb"] += 1
                                tbf = tb[k]
                                P.copy("act", tbf[:], p[:])
                                rp = pr[cnt["pr"] % 2]
                                cnt["pr"] += 1
                                P.mm(rp[:], rot, tbf[:])
                                P.tt("dve", t2[k][:], rp[:], cs[:, 1, sub * 512:(sub + 1) * 512], ALU.mult)
                                P.tt("dve", t1[k][:], p[:], cs[:, 0, sub * 512:(sub + 1) * 512], ALU.mult)
                                nsub = 512 // dd
                                if dd == 1:
                                    dst = qst[:, sub * 512:(sub + 1) * 512]
                                    P.tt("pool", dst, t1[k][:], t2[k][:], ALU.add)
                                else:
                                    dst = qst.v(qst.t[:, :].rearrange("p (r n) -> p r n", r=dd)[:, :, sub * nsub:(sub + 1) * nsub])
                                    a = t1[k].v(t1[k].t[:, :].rearrange("p (n r) -> p r n", r=dd))
                                    b = t2[k].v(t2[k].t[:, :].rearrange("p (n r) -> p r n", r=dd))
                                    P.tt("pool", dst, a, b, ALU.add)
                            if typ == 3:
                                P.dma("sp", self.zT[h * 128:(h + 1) * 128, t0:t0 + ST], qst.t[:, :], reads=[qst.b])
                            elif typ == 0:
                                dst = self.qT[g][h][:, :, t0 // dd:(t0 + ST) // dd]
                                P.dma("sp", dst, qst.t[:, :].rearrange("p (r n) -> p r n", r=dd), reads=[qst.b])
                            else:
                                dst = self.kT[g][h][:, :, 64 + t0 // dd:64 + (t0 + ST) // dd]
                                P.dma("sp", dst, qst.t[:, :].rearrange("p (r n) -> p r n", r=dd), reads=[qst.b])
            P.flush()

    def phase_att_core(self):
        P = self.P
        S = self.S
        scale = 1.0 / math.sqrt(DH)
        with contextlib.ExitStack() as es:
            vt = T(P, es, "valid", [128, self.nvalid], BF16)
            P.dma("sp", vt.t[:], self.in_valid[:, :], writes=[vt.b])
            accn = [T(P, es, "accn%d" % i, [128, S], F32) for i in range(2)]
            accd = [T(P, es, "accd%d" % i, [128, S], F32) for i in range(2)]
            NSL = 6
            qsb = [T(P, es, "qsb%d" % i, [128, 512], BF16) for i in range(NSL)]
            ksb = [T(P, es, "ksb%d" % i, [128, 640], BF16) for i in range(NSL)]
            vsb = [T(P, es, "vsb%d" % i, [128, 5, 128], BF16) for i in range(NSL)]
            pss = [T(P, es, "pss%d" % i, [128, 512], F32, psum=True) for i in range(4)]
            psn = [T(P, es, "psn%d" % i, [128, 512], F32, psum=True) for i in range(2)]
            psd = [T(P, es, "psd%d" % i, [128, 512], F32, psum=True) for i in range(2)]
            pe = [T(P, es, "pe%d" % i, [128, 256], BF16) for i in range(6)]
            pm = [T(P, es, "pm%d" % i, [128, 256], BF16) for i in range(6)]
            rc = [T(P, es, "rc%d" % i, [128, 512], F32) for i in range(2)]
            ob = [T(P, es, "ob%d" % i, [128, 512], F32) for i in range(2)]
            zb = [T(P, es, "zb%d" % i, [128, 512], BF16) for i in range(2)]
            yb = [T(P, es, "yb%d" % i, [128, 512], BF16) for i in range(2)]
            ones = self.cbv(CB_ONES)
            mask = self.cb[:, CB_M0:CB_M0 + 256]
            tiles = []
            voff = 0
            for g, (_, dd) in enumerate(GROUPS):
                L = S // dd
                QT = min(512, L)
                nblk_r = L // 128 + 1
                for r in range(dd):
                    for qt in range(L // QT):
                        tiles.append((g, dd, L, QT, r, qt, voff + r * nblk_r))
                voff += dd * nblk_r
            cnt = {"sl": 0, "ps": 0, "pe": 0, "fin": 0}

            def load_tile(h, ti, slot):
                g, dd, L, QT, r, qt, vo = tiles[ti]
                nb = QT // 128 + 1
                P.dma("sp", qsb[slot].t[:, 0:QT], self.qT[g][h][:, r, qt * QT:(qt + 1) * QT], writes=[qsb[slot].b])
                P.dma("sp", ksb[slot].t[:, 0:QT + 128], self.kT[g][h][:, r, qt * QT:(qt + 1) * QT + 128], writes=[ksb[slot].b])
                P.dma("sp", vsb[slot].t[:, 0:nb, :],
                      self.vv[g][r, qt * QT:qt * QT + QT + 128, h * 128:(h + 1) * 128].rearrange("(b p) c -> p b c", p=128),
                      writes=[vsb[slot].b])

            seq = [(h, ti) for h in range(NH_ATT) for ti in range(len(tiles))]
            PF = 2
            qbs = []
            for i, (h, ti) in enumerate(seq):
                for qb in range(tiles[ti][3] // 128):
                    qbs.append((i, h, ti, qb))
            st_ = {}
            for i in range(min(PF, len(seq))):
                load_tile(seq[i][0], seq[i][1], i % NSL)

            def stageS(k):
                i, h, ti, qb = qbs[k]
                g, dd, L, QT, r, qt, vo = tiles[ti]
                nqb = QT // 128
                if qb == 0 and i + PF < len(seq):
                    load_tile(seq[i + PF][0], seq[i + PF][1], (i + PF) % NSL)
                slot = i % NSL
                sc = pss[k % len(pss)]
                qv = qsb[slot][:, qb * 128:(qb + 1) * 128]
                for kb in range(2):
                    P.mm(sc[:, kb * 128:(kb + 1) * 128], ksb[slot][:, (qb + kb) * 128:(qb + kb + 1) * 128], qv)
                e = pe[k % len(pe)]
                m = pm[k % len(pm)]
                P.act(e[:], sc[:, 0:256], AF.Exp, scale=scale)
                P.tt("pool" if k % 3 == 2 else "dve", m[:], e[:], self.cb[:, CB_M0:CB_M0 + 256], ALU.mult)
                st_[k] = m

            def stageV(k):
                i, h, ti, qb = qbs[k]
                g, dd, L, QT, r, qt, vo = tiles[ti]
                nqb = QT // 128
                slot = i % NSL
                m = st_.pop(k)
                an = accn[h % 2]
                ad = accd[h % 2]
                pn = psn[i % 2]
                pd = psd[i % 2]
                for kb in range(2):
                    P.mm(pn[:, qb * 128:(qb + 1) * 128], vsb[slot][:, qb + kb, :], m[:, kb * 128:(kb + 1) * 128],
                         start=(kb == 0), stop=(kb == 1))
                for kb in range(2):
                    blk = vo + qt * nqb + qb + kb
                    P.mm(pd[:, qb * 128:(qb + 1) * 128], View(vt.b, vt.t[:, blk:blk + 1].broadcast_to([128, 128])), m[:, kb * 128:(kb + 1) * 128],
                         start=(kb == 0), stop=(kb == 1))
                if qb != nqb - 1:
                    return
                base = qt * QT * dd + r
                if dd == 1:
                    dn = an[:, base:base + QT]
                    dden = ad[:, base:base + QT]
                else:
                    dn = an.v(an.t[:, qt * QT * dd:(qt + 1) * QT * dd].rearrange("p (n r) -> p r n", r=dd)[:, r, :])
                    dden = ad.v(ad.t[:, qt * QT * dd:(qt + 1) * QT * dd].rearrange("p (n r) -> p r n", r=dd)[:, r, :])
                if g == 0:
                    P.copy("dve", dn, pn[:, 0:QT])
                    P.copy("act", dden, pd[:, 0:QT])
                else:
                    P.tt("dve", dn, pn[:, 0:QT], dn, ALU.add)
                    P.tt("dve", dden, pd[:, 0:QT], dden, ALU.add)
                if ti == len(tiles) - 1:
                    for c0 in range(0, S, 512):
                        kf = cnt["fin"] % 2
                        cnt["fin"] += 1
                        w = min(512, S - c0)
                        P.dma("act", zb[kf].t[:, 0:w], self.zT[h * 128:(h + 1) * 128, c0:c0 + w], writes=[zb[kf].b])
                        P.ts("dve", rc[kf][:, 0:w], ad[:, c0:c0 + w], 1e-30, ALU.max)
                        P.recip(rc[kf][:, 0:w], rc[kf][:, 0:w])
                        P.tt("pool", ob[kf][:, 0:w], an[:, c0:c0 + w], rc[kf][:, 0:w], ALU.mult)
                        P.tt("pool", yb[kf][:, 0:w], ob[kf][:, 0:w], zb[kf][:, 0:w], ALU.mult)
                        P.dma("act", self.yT[h * 128:(h + 1) * 128, c0:c0 + w], yb[kf].t[:, 0:w], reads=[yb[kf].b])

            LAG = 2
            for k in range(len(qbs) + LAG):
                if k < len(qbs):
                    stageS(k)
                if k - LAG >= 0:
                    stageV(k - LAG)
            P.flush()

    def phase_out(self, x_src, x_dst, Wb, layer):
        P = self.P
        S = self.S
        with contextlib.ExitStack() as es:
            g = self.load_bcast(es, "g_post", self.in_norm_post[layer:layer + 1, :])
            epst = T(P, es, "epst", [128, 2], F32)
            P.memset("pool", epst[:, 0:1], RMS_EPS)
            self.eps_t = {RMS_EPS: epst[:, 0:1]}
            w = T(P, es, "wout", [128, NCH, D], BF16)
            P.dma("sp", w.t[:], Wb.rearrange("(c p) n -> p c n", p=128), writes=[w.b])
            ys = [T(P, es, "ys%d" % i, [128, NCH, 512], BF16) for i in range(2)]
            xt = [T(P, es, "xo%d" % i, [128, D], F32) for i in range(3)]
            ot = [T(P, es, "ot%d" % i, [128, D], F32) for i in range(2)]
            junk = T(P, es, "junk", [128, D], BF16)
            st = [T(P, es, "st%d" % i, [128, 3], F32) for i in range(4)]
            po = [T(P, es, "po%d" % i, [128, D], F32, psum=True) for i in range(2)]
            yTv = self.yT.rearrange("(c p) t -> p c t", p=128)
            nu = (S + 511) // 512

            def load_u(u):
                t0 = u * 512
                n = min(512, S - t0)
                P.dma("sp", ys[u % 2].t[:, :, 0:n], yTv[:, :, t0:t0 + n], writes=[ys[u % 2].b])

            load_u(0)
            nblk = S // 128
            for i in range(min(2, nblk)):
                P.dma("sp", xt[i % 3].t[:], x_src[i * 128:(i + 1) * 128, :], writes=[xt[i % 3].b])
            for i in range(nblk):
                u = i // 4
                sub = i % 4
                if sub == 0 and u + 1 < nu:
                    load_u(u + 1)
                if i + 2 < nblk:
                    P.dma("sp", xt[(i + 2) % 3].t[:], x_src[(i + 2) * 128:(i + 3) * 128, :], writes=[xt[(i + 2) % 3].b])
                y = ys[u % 2]
                p = po[i % 2]
                for nb in range(4):
                    for c in range(NCH):
                        P.mm(p[:, nb * 512:(nb + 1) * 512], y[:, c, sub * 128:(sub + 1) * 128], w[:, c, nb * 512:(nb + 1) * 512],
                             start=(c == 0), stop=(c == NCH - 1))
                s = st[i % 4]
                P.act(junk[:], p[:], AF.Square, accum_out=s[:, 0:1])
                self.rstd_from_ss(s[:, 0:1], s[:, 1:2], s[:, 2:3], RMS_EPS, D)
                o = ot[i % 2]
                P.stt(o[:], p[:], s[:, 2:3], g[:], ALU.mult, ALU.mult)
                P.tt("pool", o[:], o[:], xt[i % 3][:], ALU.add)
                P.dma("act", x_dst[i * 128:(i + 1) * 128, :], o.t[:], reads=[o.b])
            P.flush()

    def declare_rwkv(self):
        S = self.S
        d = self.dram
        self.rw_r = d("rw_r", [S, D], BF16)
        self.rw_k = d("rw_k", [S, D], BF16)
        self.rw_sz = d("rw_sz", [S, D], BF16)
        self.rw_v = [d("rw_v%d" % j, [S, D], BF16) for j in range(2)]
        self.rw_sw = [d("rw_sw%d" % c, [S, D], F32) for c in range(2)]
        self.rw_a = [d("rw_a%d" % c, [S, D], BF16) for c in range(2)]
        self.rw_gate = d("rw_gate", [S, D], BF16)
        self.wb_w1 = d("wb_w1", [4 * D, 96], BF16)
        self.wb_a1 = d("wb_a1", [4 * D, 96], BF16)
        self.wb_w2 = d("wb_w2", [4 * 96, D], BF16)
        self.wb_a2 = d("wb_a2", [4 * 96, D], BF16)
        self.wb_v1 = d("wb_v1", [D, 64], BF16)
        self.wb_v2 = d("wb_v2", [64, D], BF16)
        self.ft = [d("ft%d" % c, [4, 16, 128, S], BF16) for c in range(2)]
        self.ah = [d("ah%d" % c, [S, D], BF16) for c in range(2)]
        self.kh = [d("kh%d" % c, [S, D], BF16) for c in range(2)]
        self.vp = d("vp", [S, D], BF16)
        self.bon = d("bon", [S, NHR], F32)
        self.gam = [d("gam%d" % c, [128, 16, S // 64], F32) for c in range(2)]
        self.am = [d("am%d" % c, [S // 128, 128, 16, 512], BF16) for c in range(2)]
        self.ttm = [d("ttm%d" % c, [S // 128, 128, 32, 64], BF16) for c in range(2)]
        self.yd = [d("yd%d" % c, [S, D], F32) for c in range(2)]

    def phase_rwkv(self, j):
        self.vp_src = self.vp if j == 1 else self.rw_v[j]
        for i, f in enumerate((lambda: self.phase_rwkv_proj(j), lambda: self.phase_rwkv_prep(j), self.phase_rwkv_chain,
                               self.phase_rwkv_scan, lambda: self.phase_rwkv_post(j))):
            if i < self.rw_stop:
                f()

    def phase_rwkv_proj(self, j):
        P = self.P
        S = self.S
        TT = min(512, S)
        NTB = TT // 128
        vres = (j == 1)
        Wv = self.wb_rwkv_in[j].rearrange("(c p) n -> p c n", p=128)
        R = self.in_rw
        with contextlib.ExitStack() as es:
            mup = T(P, es, "mup", [128, 6, NCH], F32)
            mun = T(P, es, "mun", [128, 6, NCH], F32)
            P.dma("sp", mup.t[:], R["rwkv_mu_prev"][j].rearrange("g (c p) -> p g c", p=128), writes=[mup.b], allow_slow_non_contiguous=True)
            P.dma("sp", mun.t[:], R["rwkv_mu_next"][j].rearrange("g (c p) -> p g c", p=128), writes=[mun.b], allow_slow_non_contiguous=True)
            w1 = []
            a1 = []
            w2 = []
            a2 = []
            for c in range(2):
                o = (j * 2 + c)
                t_ = T(P, es, "w1_%d" % c, [128, NCH, 96], BF16)
                P.dma("sp", t_.t[:], self.wb_w1[o * D:(o + 1) * D, :].rearrange("(k p) r -> p k r", p=128), writes=[t_.b])
                w1.append(t_)
                t_ = T(P, es, "a1_%d" % c, [128, NCH, 96], BF16)
                P.dma("sp", t_.t[:], self.wb_a1[o * D:(o + 1) * D, :].rearrange("(k p) r -> p k r", p=128), writes=[t_.b])
                a1.append(t_)
                t_ = T(P, es, "w2_%d" % c, [98, D], BF16)
                P.dma("sp", t_.t[0:96, :], self.wb_w2[o * 96:(o + 1) * 96, :], writes=[t_.b])
                w2.append(t_)
                t_ = T(P, es, "a2_%d" % c, [98, D], BF16)
                P.dma("sp", t_.t[0:96, :], self.wb_a2[o * 96:(o + 1) * 96, :], writes=[t_.b])
                a2.append(t_)
            if vres:
                v1 = T(P, es, "v1", [128, NCH, 64], BF16)
                P.dma("sp", v1.t[:], self.wb_v1.rearrange("(k p) r -> p k r", p=128), writes=[v1.b])
                v2 = T(P, es, "v2", [66, D], BF16)
                P.dma("sp", v2.t[0:64, :], self.wb_v2[:, :], writes=[v2.b])
            with contextlib.ExitStack() as es2:
                bf_ = T(P, es2, "bias_f", [1, D], F32)
                bh = T(P, es2, "bias_h", [1, D], BF16)
                b32 = T(P, es2, "bias_32", [1, D], F32)
                bl = T(P, es2, "bias_l", [1, D], BF16)
                rows = [(R["rwkv_w0"][j, 0:1, :], w2[0], 96), (R["rwkv_w0"][j, 1:2, :], w2[1], 96),
                        (R["rwkv_a0"][j, 0:1, :], a2[0], 96), (R["rwkv_a0"][j, 1:2, :], a2[1], 96)]
                if vres:
                    rows.append((R["rwkv_v0"][0:1, :], v2, 64))
                for src, dst, r0 in rows:
                    P.dma("sp", bf_.t[:], src, writes=[bf_.b])
                    P.copy("dve", bh[:], bf_[:])
                    P.copy("dve", b32[:], bh[:])
                    P.tt("dve", b32[:], bf_[:], b32[:], ALU.subtract)
                    P.copy("dve", bl[:], b32[:])
                    P.dma("sp", dst.t[r0:r0 + 1, :], bh.t[:], reads=[bh.b], writes=[dst.b])
                    P.dma("sp", dst.t[r0 + 1:r0 + 2, :], bl.t[:], reads=[bl.b], writes=[dst.b])
                P.flush()
            hts = [T(P, es, "rhts%d" % i, [128, NCH, TT + 2], BF16) for i in range(1)]
            lsf = [T(P, es, "lsf%d" % i, [128, 512], F32) for i in range(5)]
            lsb = [T(P, es, "lsb%d" % i, [128, 512], BF16) for i in range(6)]
            dp = T(P, es, "dp", [128, NCH, TT], BF16)
            tmk = [T(P, es, "tmk%d" % i, [128, TT], BF16) for i in range(2)]
            dn = T(P, es, "dn", [128, NCH, TT], BF16)
            xs = [T(P, es, "xs%d" % i, [128, NCH, TT], BF16) for i in range(2)]
            tmp = [T(P, es, "xtmp%d" % i, [128, TT], F32) for i in range(2)]
            wt = [T(P, es, "rwt%d" % i, [128, NCH, 512], BF16) for i in range(3)]
            ps = [T(P, es, "rps%d" % i, [128, 512], F32, psum=True) for i in range(4)]
            pl = [T(P, es, "rpl%d" % i, [128, 512], F32, psum=True) for i in range(2)]
            stg = [T(P, es, "rstg%d" % i, [128, NTB, 512], BF16) for i in range(2)]
            l1 = [T(P, es, "l1_%d" % i, [98, TT], BF16) for i in range(2)]
            for t_ in l1:
                P.memset("pool", t_[96:98, :], 1.0)
            l1v = None
            if vres:
                l1v = T(P, es, "l1v", [66, TT], BF16)
                P.memset("pool", l1v[64:66, :], 1.0)
            cnt = {"ps": 0, "pl": 0, "stg": 0, "stf": 0, "wt": 0, "ev": 0, "xs": 0, "l1": 0}
            ntile = S // TT

            def load_h(ti):
                t0 = ti * TT
                P.dma("sp", hts[0].t[:], self.hT[:, :, t0:t0 + TT + 2].rearrange("c p t -> p c t"), writes=[hts[0].b])

            def evac(dst, src, func=None):
                if func is not None:
                    P.act(dst, src, func)
                else:
                    P.copy("act", dst, src)
                cnt["ev"] += 1

            def lora(x, wA, wB, krows, func1, dst_dram, fp32_out, l1t, t0):
                p1 = pl[cnt["pl"] % 2]
                cnt["pl"] += 1
                for c in range(NCH):
                    P.mm(p1[0:krows, 0:TT], wA[:, c, 0:krows], x[:, c, :], start=(c == 0), stop=(c == NCH - 1))
                if func1 is None:
                    P.copy("act", l1t[0:krows, :], p1[0:krows, 0:TT])
                else:
                    P.act(l1t[0:krows, :], p1[0:krows, 0:TT], func1)
                for tb in range(NTB):
                    for cg in range(4):
                        p2 = ps[cnt["ps"] % 4]
                        cnt["ps"] += 1
                        P.mm(p2[:], l1t[0:krows + 2, tb * 128:(tb + 1) * 128], wB[0:krows + 2, cg * 512:(cg + 1) * 512])
                        if fp32_out:
                            sf = lsf[cnt["stf"] % len(lsf)]
                        else:
                            sf = lsb[cnt["stf"] % len(lsb)]
                        cnt["stf"] += 1
                        P.act(sf[:], p2[:], AF.Sigmoid)
                        P.dma("act", dst_dram[t0 + tb * 128:t0 + (tb + 1) * 128, cg * 512:(cg + 1) * 512], sf.t[:], reads=[sf.b])

            wseq = [(gi, cg) for _ in range(ntile) for gi in range(4) for cg in range(4)]

            def issue_w(k):
                if k < len(wseq):
                    gi_, cg_ = wseq[k]
                    col0_ = gi_ * D + cg_ * 512
                    P.dma("sp", wt[k % 3].t[:], Wv[:, :, col0_:col0_ + 512], writes=[wt[k % 3].b])

            load_h(0)
            issue_w(0)
            issue_w(1)
            for ti in range(ntile):
                t0 = ti * TT
                h = hts[0]
                P.tt("pool", dp[:], h[:, :, 0:TT], h[:, :, 1:TT + 1], ALU.subtract)
                mk_ = tmk[ti % 2]
                P.dma("sp", mk_.t[:], self.in_tokmask[:, t0:t0 + TT], writes=[mk_.b])
                P.tt("pool", dp[:], dp[:], View(mk_.b, mk_.t[:, :].unsqueeze(1).to_broadcast([128, NCH, TT])), ALU.mult)
                P.tt("pool", dn[:], h[:, :, 2:TT + 2], h[:, :, 1:TT + 1], ALU.subtract)
                for g in (0, 2, 3, 5, 1, 4):
                    x = xs[cnt["xs"] % 2]
                    cnt["xs"] += 1
                    for c in range(NCH):
                        tm = tmp[c % 2]
                        P.stt(tm[:], dp[:, c, :], mup[:, g, c:c + 1], h[:, c, 1:TT + 1], ALU.mult, ALU.add)
                        P.stt(x[:, c, :], dn[:, c, :], mun[:, g, c:c + 1], tm[:], ALU.mult, ALU.add)
                    if g in (0, 2, 3, 5):
                        gi = {0: 0, 2: 1, 3: 2, 5: 3}[g]
                        dst = {0: self.rw_r, 2: self.rw_k, 3: self.rw_v[j], 5: self.rw_sz}[g]
                        for cg in range(4):
                            w = wt[cnt["wt"] % 3]
                            issue_w(cnt["wt"] + 2)
                            cnt["wt"] += 1
                            sg = stg[cnt["stg"] % 2]
                            cnt["stg"] += 1
                            for tb in range(NTB):
                                p = ps[cnt["ps"] % 4]
                                cnt["ps"] += 1
                                for c in range(NCH):
                                    P.mm(p[:], x[:, c, tb * 128:(tb + 1) * 128], w[:, c, :], start=(c == 0), stop=(c == NCH - 1))
                                evac(sg[:, tb, :], p[:], AF.Silu if g == 5 else None)
                            P.dma("sp", dst[t0:t0 + TT, cg * 512:(cg + 1) * 512].rearrange("(b p) c -> p b c", p=128), sg.t[:], reads=[sg.b])
                        if g == 3 and vres:
                            lora(x, v1, v2, 64, None, self.rw_gate, False, l1v, t0)
                    elif g == 1:
                        for c in range(2):
                            lt = l1[cnt["l1"] % 2]
                            cnt["l1"] += 1
                            lora(x, w1[c], w2[c], 96, AF.Tanh, self.rw_sw[c], True, lt, t0)
                    else:
                        for c in range(2):
                            lt = l1[cnt["l1"] % 2]
                            cnt["l1"] += 1
                            lora(x, a1[c], a2[c], 96, None, self.rw_a[c], False, lt, t0)
                if ti + 1 < ntile:
                    load_h(ti + 1)
            P.flush()
    def phase_rwkv_prep(self, j):
        P = self.P
        S = self.S
        vres = (j == 1)
        R = self.in_rw
        CDEC = math.exp(-0.5)
        nblk = S // 128
        self.vp_src = self.vp if vres else self.rw_v[j]
        with contextlib.ExitStack() as es:
            kkB = self.load_bcast(es, "kkB", R["rwkv_k_k"][j:j + 1, :])
            kaB = self.load_bcast(es, "kaB", R["rwkv_k_a"][j:j + 1, :])
            rkB = []
            for c in range(2):
                t_ = T(P, es, "rkB%d" % c, [128, D], BF16)
                P.dma("pool", t_.t[:], R["rwkv_r_k"][j, c:c + 1, :].partition_broadcast(128), writes=[t_.b])
                rkB.append(t_)
            tri = T(P, es, "tri", [128, 4, 128], F32)
            P.dma("sp", tri.t[:], self.in_cf[:, 2 * S:2 * S + 512].rearrange("p (a b) -> p a b", b=128), writes=[tri.b])
            ind = T(P, es, "ind", [128, 2], F32)
            P.dma("sp", ind.t[:], self.in_cf[:, 2 * S + 512:2 * S + 514], writes=[ind.b])
            ident = self.cbv(CB_IDENT)
            inb = [T(P, es, "inb%d" % i, [128, 5, D], BF16) for i in range(2)]
            swt = [T(P, es, "swt%d" % c, [128, D], F32) for c in range(2)]
            if vres:
                gv = T(P, es, "gv", [128, 2, D], BF16)
            kkf = T(P, es, "kkf", [128, D], F32)
            sq = T(P, es, "sq", [128, D], F32)
            kk = T(P, es, "kk", [128, D], BF16)
            kkac = T(P, es, "kkac", [128, D], BF16)
            kmk = T(P, es, "kmk", [128, D], BF16)
            vpt = T(P, es, "vpt", [128, D], BF16)
            tb_ = T(P, es, "tb_", [128, D], BF16)
            kd = T(P, es, "kd", [128, D], BF16)
            kka = T(P, es, "kka", [128, D], BF16)
            ub = kkf
            sm = [T(P, es, "sm%d" % i, [128, 4, NHR], F32) for i in range(2)]
            Eb = [T(P, es, "Eb%d" % i, [128, 4, 512], BF16) for i in range(2)]
            prod = [T(P, es, "prod%d" % i, [128, D], BF16) for i in range(6)]
            fts = T(P, es, "fts", [128, 4, 16, 128], BF16)
            gall = [T(P, es, "gall%d" % i, [128, 16, 2], F32) for i in range(4)]
            pc = [T(P, es, "pc%d" % i, [128, 512], F32, psum=True) for i in range(5)]
            ptr = [T(P, es, "ptr%d" % i, [128, 1024], BF16, psum=True) for i in range(2)]
            pg = T(P, es, "pg", [128, 16, 2], F32, psum=True)
            cnt = {"pc": 0, "ptr": 0, "E": 0, "alt": 0}

            def load_in(bi):
                t0 = bi * 128
                ib = inb[bi % 2]
                for q, src in enumerate((self.rw_r, self.rw_k, self.rw_v[j], self.rw_a[0], self.rw_a[1])):
                    P.dma("sp", ib.t[:, q, :], src[t0:t0 + 128, :], writes=[ib.b])

            def alt():
                cnt["alt"] += 1
                return "dve" if cnt["alt"] % 2 == 0 else "pool"

            load_in(0)
            for bi in range(nblk):
                t0 = bi * 128
                if bi + 1 < nblk:
                    load_in(bi + 1)
                ib = inb[bi % 2]
                r_, k_, v_ = ib[:, 0, :], ib[:, 1, :], ib[:, 2, :]
                for c in range(2):
                    P.dma("sp", swt[c].t[:], self.rw_sw[c][t0:t0 + 128, :], writes=[swt[c].b])
                if vres:
                    P.dma("sp", gv.t[:, 0, :], self.rw_gate[t0:t0 + 128, :], writes=[gv.b])
                    P.dma("sp", gv.t[:, 1, :], self.rw_v[0][t0:t0 + 128, :], writes=[gv.b])
                    P.tt("pool", tb_[:], gv[:, 1, :], v_, ALU.subtract)
                    P.tt("pool", tb_[:], tb_[:], gv[:, 0, :], ALU.mult)
                    P.tt("pool", vpt[:], v_, tb_[:], ALU.add)
                    P.dma("sp", self.vp[t0:t0 + 128, :], vpt.t[:], reads=[vpt.b])
                s_ = sm[bi % 2]
                P.tt("pool", kkf[:], k_, kkB[:], ALU.mult)
                P.tt("pool", sq[:], kkf[:], kkf[:], ALU.mult)
                P.add("dve", lambda e, o=s_.t[:, 0, :], i=sq.t[:, :].rearrange("p (h n) -> p h n", n=NR): e.tensor_reduce(out=o, in_=i, axis=AX.X, op=ALU.add),
                      reads=[sq.b], writes=[s_.b])
                P.act(s_[:, 1, :], s_[:, 0, :], AF.Sqrt)
                P.ts("dve", s_[:, 1, :], s_[:, 1, :], 1e-12, ALU.max)
                P.recip(s_[:, 2, :], s_[:, 1, :])
                P.tt("pool", kk.v(kk.t[:, :].rearrange("p (h n) -> p h n", n=NR)), kkf.v(kkf.t[:, :].rearrange("p (h n) -> p h n", n=NR)),
                     s_.v(s_.t[:, 2, :].unsqueeze(2).to_broadcast([128, NHR, NR])), ALU.mult)
                P.tt("pool", kkac[:], k_, kaB[:], ALU.mult)
                P.tt("pool", kmk[:], k_, kkac[:], ALU.subtract)
                for c in range(2):
                    a_ = ib[:, 3 + c, :]
                    P.tt(alt(), tb_[:], a_, kkac[:], ALU.mult)
                    P.tt(alt(), kd[:], tb_[:], kmk[:], ALU.add)
                    P.tt(alt(), kka[:], kk[:], a_, ALU.mult)
                    if c == 0:
                        P.tt(alt(), ub[:], kd[:], rkB[0][:], ALU.mult)
                    else:
                        P.tt(alt(), sq[:], kd[:], rkB[1][:], ALU.mult)
                        P.tt(alt(), ub[:], ub[:], sq[:], ALU.add)
                        P.tt(alt(), ub[:], ub[:], r_, ALU.mult)
                        P.add("dve", lambda e, o=s_.t[:, 3, :], i=ub.t[:, :].rearrange("p (h n) -> p h n", n=NR): e.tensor_reduce(out=o, in_=i, axis=AX.X, op=ALU.add),
                              reads=[ub.b], writes=[s_.b])
                        P.dma("sp", self.bon[t0:t0 + 128, :], s_.t[:, 3, :], reads=[s_.b])
                    mats = (0, 2, 3) if c == 0 else (1, 3, 2)
                    sw = swt[c]
                    for cg in range(4):
                        cs_ = slice(cg * 512, (cg + 1) * 512)
                        pcs = []
                        for mi in mats:
                            p = pc[cnt["pc"] % 5]
                            cnt["pc"] += 1
                            P.mm(p[:], tri[:, mi, :], sw[:, cs_])
                            pcs.append(p)
                        E = Eb[cnt["E"] % 2]
                        cnt["E"] += 1
                        P.act(E[:, 0, :], pcs[0][:], AF.Exp, scale=-CDEC)
                        P.act(E[:, 1, :], pcs[1][:], AF.Exp, scale=-CDEC)
                        P.act(E[:, 2, :], pcs[0][:], AF.Exp, scale=CDEC)
                        P.act(E[:, 3, :], pcs[2][:], AF.Exp, scale=-CDEC)
                        P.tt(alt(), prod[0][:, cs_], kk[:, cs_], E[:, 1, :], ALU.mult)
                        P.tt(alt(), prod[1][:, cs_], ib[:, 0, cs_], E[:, 0, :], ALU.mult)
                        P.tt(alt(), prod[2][:, cs_], kka[:, cs_], E[:, 2, :], ALU.mult)
                        P.tt(alt(), prod[3][:, cs_], kd[:, cs_], E[:, 2, :], ALU.mult)
                        P.tt(alt(), prod[4][:, cs_], kka[:, cs_], E[:, 3, :], ALU.mult)
                        P.tt(alt(), prod[5][:, cs_], kd[:, cs_], E[:, 3, :], ALU.mult)
                    for p_ in range(16):
                        P.mm(pg[:, p_, :], sw[:, p_ * 128:(p_ + 1) * 128], ind[:, :])
                    gt_ = gall[(bi * 2 + c) % 4]
                    P.act(gt_[:], pg[:], AF.Exp, scale=-CDEC)
                    P.dma("act", self.gam[c][:, :, bi * 2:bi * 2 + 2], gt_.t[:], reads=[gt_.b])
                    for q in range(4):
                        for half in range(2):
                            pt = ptr[cnt["ptr"] % 2]
                            cnt["ptr"] += 1
                            for pp in range(8):
                                p_ = half * 8 + pp
                                P.transpose(pt[:, pp * 128:(pp + 1) * 128], prod[q][:, p_ * 128:(p_ + 1) * 128], ident)
                            src = pt.v(pt.t[:, :].rearrange("p (c t) -> p c t", t=128))
                            dst = fts[:, q, half * 8:(half + 1) * 8, :]
                            if cnt["ptr"] % 2 == 0:
                                P.copy("act", dst, src)
                            else:
                                P.copy("dve", dst, src)
                    P.dma("sp", self.ft[c][:, :, :, t0:t0 + 128].rearrange("q p c t -> c q p t"), fts.t[:], reads=[fts.b])
                    P.dma("sp", self.ah[c][t0:t0 + 128, :], prod[4].t[:], reads=[prod[4].b])
                    P.dma("sp", self.kh[c][t0:t0 + 128, :], prod[5].t[:], reads=[prod[5].b])
            P.flush()
    def phase_rwkv_chain(self):
        P = self.P
        S = self.S
        nblk = S // 128
        with contextlib.ExitStack() as es:
            ftl = [T(P, es, "ftl%d" % i, [128, 4, 16, 128], BF16) for i in range(2)]
            amt = [T(P, es, "amt%d" % i, [128, 16, 512], BF16) for i in range(2)]
            Q0 = [T(P, es, "Q0_%d" % i, [128, 32, 64], BF16) for i in range(2)]
            G0 = [T(P, es, "G0_%d" % i, [128, 32, 64], BF16) for i in range(2)]
            N0 = [T(P, es, "N0_%d" % i, [128, 32, 64], BF16) for i in range(2)]
            Pst = [T(P, es, "Pst%d" % i, [128, 32, 64], BF16) for i in range(2)]
            Qst = [T(P, es, "Qst%d" % i, [128, 32, 64], BF16) for i in range(2)]
            Pbd = [T(P, es, "Pbd%d" % i, [128, 32, 128], BF16) for i in range(2)]
            Qbd = [T(P, es, "Qbd%d" % i, [128, 32, 128], BF16) for i in range(2)]
            for t_ in Pbd + Qbd:
                P.memset("pool", t_[:], 0.0)
            Gm = [T(P, es, "Gm%d" % i, [128, 32, 64], BF16) for i in range(2)]
            pb = [T(P, es, "pb%d" % i, [128, 512], F32, psum=True) for i in range(8)]
            cnt = {"pa": 0, "b": 0, "ev": 0}
            seq = [(c, bi) for bi in range(nblk) for c in range(2)]

            def load(i):
                c, bi = seq[i]
                P.dma("sp", ftl[i % 2].t[:], self.ft[c][:, :, :, bi * 128:(bi + 1) * 128].rearrange("q p c t -> c q p t"), writes=[ftl[i % 2].b])

            import os as _os
            def evcopy(dst, src):
                cnt["ev"] += 1
                ev_ = _os.environ.get('EVENG')
                P.copy("dve", dst, src)

            load(0)
            for i, (c, bi) in enumerate(seq):
                if i + 1 < len(seq):
                    load(i + 1)
                f = ftl[i % 2]
                am_ = amt[i % 2]
                q0 = Q0[i % 2]
                g0 = G0[i % 2]
                mk = CB_MF if c == 0 else CB_MB
                lk = CB_LF if c == 0 else CB_LB
                for p2 in range(8):
                    base = (cnt["pa"] % 4) * 2
                    cnt["pa"] += 1
                    for hh in range(2):
                        ps_ = pb[base + hh]
                        kr = slice(hh * 64, (hh + 1) * 64)
                        for pl in range(2):
                            p = p2 * 2 + pl
                            for cc in range(2):
                                tk = slice(cc * 64, (cc + 1) * 64)
                                P.mm(ps_[tk, pl * 256:pl * 256 + 128], f[kr, 2, p, tk], f[kr, 0:2, p, tk])
                                P.mm(ps_[tk, pl * 256 + 128:pl * 256 + 256], f[kr, 3, p, tk], f[kr, 0:2, p, tk])
                        P.tt("dve", am_[:, p2 * 2:p2 * 2 + 2, hh * 256:(hh + 1) * 256], ps_.v(ps_.t[:, :].rearrange("p (a m) -> p a m", a=2)),
                             View(self.cb.b, self.cb.t[:, mk:mk + 256].unsqueeze(1).to_broadcast([128, 2, 256])), ALU.mult)
                q0v = q0.t[:, :, :].rearrange("p (a h) n -> p a h n", h=2)
                import os as _os
                CHS = int(_os.environ.get("CHSTOP", "9"))
                if CHS < 2:
                    continue
                for grp in range(2):
                    base = (cnt["pa"] % 4) * 2
                    cnt["pa"] += 1
                    for hh in range(2):
                        ps_ = pb[base + hh]
                        kr = slice(hh * 64, (hh + 1) * 64)
                        for pl in range(8):
                            p = grp * 8 + pl
                            for cc in range(2):
                                tk = slice(cc * 64, (cc + 1) * 64)
                                P.mm(ps_[tk, pl * 64:(pl + 1) * 64], f[kr, 0, p, tk], f[kr, 2, p, tk])
                        P.tt("dve", View(q0.b, q0v[:, grp * 8:(grp + 1) * 8, hh, :]), ps_.v(ps_.t[:, :].rearrange("p (a m) -> p a m", m=64)),
                             View(self.cb.b, self.cb.t[:, lk:lk + 64].unsqueeze(1).to_broadcast([128, 8, 64])), ALU.mult)
                P.dma("act", self.am[c][bi], am_.t[:], reads=[am_.b])
                if CHS < 3:
                    continue
                nview = am_.t[:, :, :].rearrange("p a (h q n) -> p a h q n", h=2, q=4)[:, :, :, 0, :]
                P.tt("dve", g0.v(g0.t[:, :, :].rearrange("p (a h) n -> p a h n", h=2)),
                     View(self.cb.b, self.cb.t[:, CB_I64:CB_I64 + 64].unsqueeze(1).unsqueeze(1).to_broadcast([128, 16, 2, 64])),
                     View(am_.b, nview), ALU.subtract)

                nst = N0[i % 2]
                nv4 = am_.t[:, :, :].rearrange("p a (h q n) -> p a h q n", h=2, q=4)[:, :, :, 0, :]
                P.copy("act", nst.v(nst.t[:, :, :].rearrange("p (a h) n -> p a h n", h=2)), View(am_.b, nv4))
                for cc in range(2):
                    rows = slice(cc * 64, (cc + 1) * 64)
                    eng_ = "act" if cc == 0 else "dve"
                    P.copy(eng_, Pbd[1][rows, :, cc * 64:(cc + 1) * 64], nst[rows, :, :])
                    P.copy(eng_, Qbd[1][rows, :, cc * 64:(cc + 1) * 64], q0[rows, :, :])
                for bg in range(2):
                    for st in range(6):
                        pst = nst if st == 0 else Pst[(st - 1) % 2]
                        qst = q0 if st == 0 else Qst[(st - 1) % 2]
                        pbd = Pbd[(st - 1) % 2]
                        qbd = Qbd[(st - 1) % 2]
                        gsrc = g0 if st == 1 else Gm[(st - 2) % 2]
                        for bt in (bg * 2, bg * 2 + 1):
                            base = (bt % 2) * 3
                            pP, pQ, pG = pb[base], pb[base + 1], pb[base + 2]
                            bs = slice(bt * 8, bt * 8 + 8)
                            for hl in range(8):
                                hd = bt * 8 + hl
                                cs_ = slice(hl * 64, (hl + 1) * 64)
                                if st <= 3:
                                    P.mm(pP[:, cs_], qbd[:, hd, :], pst[:, hd, :])
                                if st <= 4:
                                    P.mm(pQ[:, cs_], pbd[:, hd, :], qst[:, hd, :])
                                if st >= 1:
                                    P.mm(pG[:, cs_], qbd[:, hd, :], gsrc[:, hd, :])
                            v3 = lambda t_, r_=slice(0, 128): View(t_.b, t_.t[r_, :].rearrange("p (h m) -> p h m", m=64))
                            if st <= 3:
                                P.copy("act", Pst[st % 2][:, bs, :], v3(pP))
                                P.copy("act", Pbd[st % 2][0:64, bs, 0:64], v3(pP, slice(0, 64)))
                                P.copy("dve", Pbd[st % 2][64:128, bs, 64:128], v3(pP, slice(64, 128)))
                            if st <= 4:
                                if st <= 3:
                                    P.copy("act", Qst[st % 2][:, bs, :], v3(pQ))
                                P.copy("act", Qbd[st % 2][0:64, bs, 0:64], v3(pQ, slice(0, 64)))
                                P.copy("dve", Qbd[st % 2][64:128, bs, 64:128], v3(pQ, slice(64, 128)))
                            if st >= 1:
                                P.tt("dve", Gm[(st - 1) % 2][:, bs, :], v3(pG), gsrc[:, bs, :], ALU.add)
                P.dma("act", self.ttm[c][bi], Gm[0].t[:], reads=[Gm[0].b])
            P.flush()

    def phase_rwkv_scan(self):
        P = self.P
        S = self.S
        nblk = S // 128
        with contextlib.ExitStack() as es:
            gl = [T(P, es, "gl%d" % c, [128, 16, S // 64], F32) for c in range(2)]
            for c in range(2):
                P.dma("sp", gl[c].t[:], self.gam[c][:, :, :], writes=[gl[c].b])
            ST = [T(P, es, "ST%d" % c, [128, 16, 64], F32) for c in range(2)]
            STb = [[T(P, es, "STb%d_%d" % (c, i), [128, 16, 2, 64], BF16) for i in range(2)] for c in range(2)]
            for c in range(2):
                P.memset("pool", ST[c][:], 0.0)
                P.memset("pool", STb[c][0][:], 0.0)
                P.memset("pool", STb[c][1][:], 0.0)
            fK = [[T(P, es, "fK%d_%d" % (c, i), [128, 2, 16, 128], BF16) for i in range(2)] for c in range(2)]
            amt = [T(P, es, "samt%d" % c, [128, 16, 512], BF16) for c in range(2)]
            ttl = [[T(P, es, "ttl%d_%d" % (c, i), [128, 32, 64], BF16) for i in range(2)] for c in range(2)]
            akv = [[T(P, es, "akv%d_%d" % (c, i), [128, 3, D], BF16) for i in range(2)] for c in range(2)]
            yst = [T(P, es, "yst%d" % c, [128, D], F32) for c in range(2)]
            NB = 4
            Bt = [T(P, es, "Bt%d" % i, [128, 256], BF16) for i in range(NB)]
            Ut = [T(P, es, "Ut%d" % i, [128, 256], BF16) for i in range(NB)]
            tmpS = [T(P, es, "tmpS%d" % i, [128, 2, 64], F32) for i in range(NB)]
            pBU = [T(P, es, "pBU%d" % i, [128, 512], F32, psum=True) for i in range(4)]
            pMY = [T(P, es, "pMY%d" % i, [128, 512], F32, psum=True) for i in range(4)]

            def load(c, step):
                b = step if c == 0 else nblk - 1 - step
                t0 = b * 128
                sl = step % 2
                P.dma("sp", fK[c][sl].t[:], self.ft[c][0:2, :, :, t0:t0 + 128].rearrange("q p c t -> c q p t"), writes=[fK[c][sl].b])
                P.dma("sp", ttl[c][sl].t[:], self.ttm[c][b], writes=[ttl[c][sl].b])
                P.dma("sp", akv[c][sl].t[:, 0, :], self.ah[c][t0:t0 + 128, :], writes=[akv[c][sl].b])
                P.dma("sp", akv[c][sl].t[:, 1, :], self.kh[c][t0:t0 + 128, :], writes=[akv[c][sl].b])
                P.dma("sp", akv[c][sl].t[:, 2, :], self.vp_src[t0:t0 + 128, :], writes=[akv[c][sl].b])

            for c in range(2):
                load(c, 0)
            kctr = [0]
            for step in range(nblk):
                for c in range(2):
                    if step + 1 < nblk:
                        load(c, step + 1)
                    b = step if c == 0 else nblk - 1 - step
                    P.dma("sp", amt[c].t[:], self.am[c][b], writes=[amt[c].b])
                for ci_ in range(2):
                    combos = []
                    for e8 in range(8):
                        for c in range(2):
                            b = step if c == 0 else nblk - 1 - step
                            cc = ci_ if c == 0 else 1 - ci_
                            combos.append((c, b, cc, b * 2 + cc, e8))
                    nco = len(combos)
                    ctx = {}

                    def stageA(k):
                        c, b, cc, gch, e8 = combos[k]
                        sl = step % 2
                        R_ = slice(cc * 64, (cc + 1) * 64)
                        kk_ = kctr[0]
                        kctr[0] += 1
                        pbu = pBU[kk_ % 4]
                        pmy = pMY[kk_ % 4]
                        bt_ = Bt[kk_ % NB]
                        ut_ = Ut[kk_ % NB]
                        ctx[k] = (pbu, pmy, bt_, ut_, kk_)
                        stb = STb[c][(step * 2 + ci_) % 2]
                        for pl in range(2):
                            p = e8 * 2 + pl
                            P.mm(pbu[R_, pl * 128:(pl + 1) * 128], fK[c][sl][:, 0, p, R_], stb[:, p, :, :], start=True, stop=False)
                            for hh in range(2):
                                hd = p * 2 + hh
                                i = pl * 2 + hh
                                P.mm(pbu[R_, i * 64:(i + 1) * 64], amt[c][R_, p, hh * 256 + 128:hh * 256 + 192],
                                     akv[c][sl][R_, 2, hd * 64:(hd + 1) * 64], start=False, stop=(hh == 1))
                        P.copy("dve", bt_[R_, :], pbu[R_, 0:256])

                    def stageA2(k):
                        c, b, cc, gch, e8 = combos[k]
                        sl = step % 2
                        R_ = slice(cc * 64, (cc + 1) * 64)
                        pbu, pmy, bt_, ut_, kk_ = ctx[k]
                        for i in range(4):
                            hd = e8 * 4 + i
                            P.mm(pbu[R_, 256 + i * 64:256 + (i + 1) * 64], ttl[c][sl][R_, hd, :], bt_[R_, i * 64:(i + 1) * 64])
                        P.ts("dve", ut_[R_, :], pbu[R_, 256:512], -1.0, ALU.mult)

                    def stageC(k):
                        c, b, cc, gch, e8 = combos[k]
                        sl = step % 2
                        R_ = slice(cc * 64, (cc + 1) * 64)
                        pbu, pmy, bt_, ut_, kk_ = ctx[k]
                        stb = STb[c][(step * 2 + ci_) % 2]
                        stn = STb[c][(step * 2 + ci_ + 1) % 2]
                        a_ = akv[c][sl]
                        for i in range(4):
                            hd = e8 * 4 + i
                            p, hh = hd // 2, hd % 2
                            kr = slice(hh * 64, (hh + 1) * 64)
                            hc = slice(hd * 64, (hd + 1) * 64)
                            o = pmy[kr, (i // 2) * 64:(i // 2 + 1) * 64]
                            P.mm(o, a_[R_, 1, hc], a_[R_, 2, hc], start=True, stop=False)
                            P.mm(o, a_[R_, 0, hc], ut_[R_, i * 64:(i + 1) * 64], start=False, stop=True)
                        for pl in range(2):
                            p = e8 * 2 + pl
                            P.mm(pmy[R_, 256 + pl * 128:256 + (pl + 1) * 128], fK[c][sl][:, 1, p, R_], stb[:, p, :, :], start=True, stop=False)
                            for hh in range(2):
                                hd = p * 2 + hh
                                i = pl * 2 + hh
                                hc = slice(hd * 64, (hd + 1) * 64)
                                o = pmy[R_, 256 + i * 64:256 + (i + 1) * 64]
                                P.mm(o, amt[c][R_, p, hh * 256 + 64:hh * 256 + 128], ut_[R_, i * 64:(i + 1) * 64], start=False, stop=False)
                                P.mm(o, amt[c][R_, p, hh * 256 + 192:hh * 256 + 256], a_[R_, 2, hc], start=False, stop=(hh == 1))
                        P.copy("dve", yst[c][R_, e8 * 256:(e8 + 1) * 256], pmy[R_, 256:512])
                        ps2 = slice(e8 * 2, e8 * 2 + 2)
                        tm = tmpS[kk_ % NB]
                        P.tt("dve", tm[:], ST[c][:, ps2, :], View(gl[c].b, gl[c].t[:, ps2, gch:gch + 1].broadcast_to([128, 2, 64])), ALU.mult)
                        P.tt("dve", ST[c][:, ps2, :], tm[:], pmy.v(pmy.t[:, 0:128].rearrange("p (a v) -> p a v", v=64)), ALU.add)
                        for hh in range(2):
                            kr = slice(hh * 64, (hh + 1) * 64)
                            P.copy("pool", stn[kr, ps2, hh, :], ST[c][kr, ps2, :])

                    for k in range(nco + 2):
                        if k < nco:
                            stageA(k)
                        if 0 <= k - 1 < nco:
                            stageA2(k - 1)
                        if 0 <= k - 2 < nco:
                            stageC(k - 2)
                for c in range(2):
                    b = step if c == 0 else nblk - 1 - step
                    P.dma("act", self.yd[c][b * 128:(b + 1) * 128, :], yst[c].t[:], reads=[yst[c].b])
            P.flush()

    def phase_rwkv_post(self, j):
        P = self.P
        S = self.S
        R = self.in_rw
        nblk = S // 128
        with contextlib.ExitStack() as es:
            gG = self.load_bcast(es, "gnG", R["rwkv_gn_g"][j:j + 1, :])
            gB = self.load_bcast(es, "gnB", R["rwkv_gn_b"][j:j + 1, :])
            yin = [T(P, es, "yin%d" % i, [128, 2, D], F32) for i in range(2)]
            vz = [T(P, es, "vz%d" % i, [128, 2, D], BF16) for i in range(2)]
            bn = [T(P, es, "bn%d" % i, [128, NHR], F32) for i in range(2)]
            y = T(P, es, "ypost", [128, D], F32)
            sq = T(P, es, "sqpost", [128, D], F32)
            yo = [T(P, es, "yo%d" % i, [128, D], BF16) for i in range(2)]
            sm = [T(P, es, "smp%d" % i, [128, 6, NHR], F32) for i in range(2)]
            tp = [T(P, es, "tpp%d" % i, [128, 1024], BF16, psum=True) for i in range(4)]
            hs = [T(P, es, "hsp%d" % i, [128, NCH, 512], BF16) for i in range(2)]
            ident = self.cbv(CB_IDENT)

            def load(bi):
                t0 = bi * 128
                sl = bi % 2
                P.dma("sp", yin[sl].t[:, 0, :], self.yd[0][t0:t0 + 128, :], writes=[yin[sl].b])
                P.dma("sp", yin[sl].t[:, 1, :], self.yd[1][t0:t0 + 128, :], writes=[yin[sl].b])
                P.dma("sp", vz[sl].t[:, 0, :], self.vp_src[t0:t0 + 128, :], writes=[vz[sl].b])
                P.dma("sp", vz[sl].t[:, 1, :], self.rw_sz[t0:t0 + 128, :], writes=[vz[sl].b])
                P.dma("sp", bn[sl].t[:], self.bon[t0:t0 + 128, :], writes=[bn[sl].b])

            def v3(t_, ap=None):
                a = t_.t[:, :] if ap is None else ap
                return View(t_.b, a.rearrange("p (h n) -> p h n", n=NR))

            def bc(t_, ap):
                return View(t_.b, ap.unsqueeze(2).to_broadcast([128, NHR, NR]))

            load(0)
            k = 0
            for bi in range(nblk):
                if bi + 1 < nblk:
                    load(bi + 1)
                sl = bi % 2
                s_ = sm[sl]
                P.tt("pool", y[:], yin[sl][:, 0, :], yin[sl][:, 1, :], ALU.add)
                P.tt("pool", sq[:], y[:], y[:], ALU.mult)
                P.add("dve", lambda e, o=s_.t[:, 0, :], i=y.t[:, :].rearrange("p (h n) -> p h n", n=NR): e.tensor_reduce(out=o, in_=i, axis=AX.X, op=ALU.add),
                      reads=[y.b], writes=[s_.b])
                P.add("dve", lambda e, o=s_.t[:, 1, :], i=sq.t[:, :].rearrange("p (h n) -> p h n", n=NR): e.tensor_reduce(out=o, in_=i, axis=AX.X, op=ALU.add),
                      reads=[sq.b], writes=[s_.b])
                P.ts("dve", s_[:, 2, :], s_[:, 0, :], 1.0 / NR, ALU.mult)
                P.tt("dve", s_[:, 3, :], s_[:, 2, :], s_[:, 2, :], ALU.mult)
                P.stt(s_[:, 4, :], s_[:, 1, :], 1.0 / NR, s_[:, 3, :], ALU.mult, ALU.subtract)
                P.ts("dve", s_[:, 4, :], s_[:, 4, :], GN_EPS, ALU.add)
                P.act(s_[:, 5, :], s_[:, 4, :], AF.Sqrt)
                P.recip(s_[:, 4, :], s_[:, 5, :])
                P.tt("pool", v3(y), v3(y), bc(s_, s_.t[:, 2, :]), ALU.subtract)
                P.tt("pool", v3(y), v3(y), bc(s_, s_.t[:, 4, :]), ALU.mult)
                P.tt("dve", y[:], y[:], gG[:], ALU.mult)
                P.tt("dve", y[:], y[:], gB[:], ALU.add)
                P.tt("pool", v3(sq), View(vz[sl].b, vz[sl].t[:, 0, :].rearrange("p (h n) -> p h n", n=NR)), bc(bn[sl], bn[sl].t[:, :]), ALU.mult)
                P.tt("pool", y[:], y[:], sq[:], ALU.add)
                o = yo[bi % 2]
                P.tt("dve", o[:], y[:], vz[sl][:, 1, :], ALU.mult)
                u = bi // 4
                sub = bi % 4
                hst = hs[u % 2]
                for half in range(2):
                    pt = tp[k % 4]
                    k += 1
                    for c in range(8):
                        cc = half * 8 + c
                        P.transpose(pt[:, c * 128:(c + 1) * 128], o[:, cc * 128:(cc + 1) * 128], ident)
                    src = pt.v(pt.t[:, :].rearrange("p (c t) -> p c t", t=128))
                    dst = hst[:, half * 8:(half + 1) * 8, sub * 128:(sub + 1) * 128]
                    P.copy("act", dst, src)
                if sub == 3 or bi == nblk - 1:
                    t0 = u * 512
                    n = (sub + 1) * 128
                    P.dma("sp", self.yT.rearrange("(c p) t -> p c t", p=128)[:, :, t0:t0 + n], hst.t[:, :, 0:n], reads=[hst.b])
            P.flush()


S_FULL = 8192
_CACHE = {}


def kernel(**inputs):
    S = S_FULL
    x_prompt = np.asarray(inputs["x_prompt"], dtype=np.float32)
    x_sample = np.asarray(inputs["x_sample"], dtype=np.float32)
    b = Builder(S)
    nc = b.build()
    cb, cf = _const_tables(S)
    xs = []
    valid_lens = []
    xs.append(np.ascontiguousarray(x_prompt[0]))
    valid_lens.append(S)
    for i in range(4):
        xp = np.zeros((S, D), np.float32)
        xp[:4096] = x_sample[i]
        xs.append(xp)
        valid_lens.append(4096)
    for i in range(3):
        xs.append(np.zeros((S, D), np.float32))
        valid_lens.append(S)
    shared = {}
    for k, v in inputs.items():
        if k in ("x_prompt", "x_sample"):
            continue
        a = np.ascontiguousarray(np.asarray(v, dtype=np.float32))
        if k == "rwkv_r_k":
            a = a.reshape(2, 2, D)
        shared[k] = a
    in_maps = []
    for c in range(8):
        m = dict(shared)
        m["x"] = xs[c]
        m["const_bf"] = cb
        m["const_f32"] = cf
        m["valid"] = _valid_tables(S, valid_lens[c])
        m["tokmask"] = _tokmask(S, valid_lens[c])
        in_maps.append(m)
    res = run_bass_kernel_spmd(nc, in_maps, core_ids=list(range(8)))
    y_prompt = np.asarray(res.results[0]["y"], dtype=np.float32)[None]
    y_sample = np.stack([np.asarray(res.results[1 + i]["y"], dtype=np.float32)[:4096] for i in range(4)], axis=0)
    return (y_prompt, y_sample)
```

```python
import contextlib
import math
import numpy as np
import ml_dtypes
import concourse.bass as bass
import concourse.mybir as mybir
from concourse.bass_utils import run_bass_kernel_spmd

F32 = mybir.dt.float32
BF16 = mybir.dt.bfloat16
AF = mybir.ActivationFunctionType
ALU = mybir.AluOpType
AX = mybir.AxisListType

D = 2048
NCH = 16
NH_ATT = 16
DH = 128
GROUPS = ((128, 1), (512, 4), (2048, 16))
ATT_COLS = 20480
RMS_EPS = 1e-6
GN_EPS = 64e-5
NHR = 32
NR = 64
CH = 64
SAME_ENGINE_SYNC = True


class Buf:
    __slots__ = ("name", "w", "r")

    def __init__(self, name):
        self.name = name
        self.w = None
        self.r = []


class View:
    __slots__ = ("b", "ap")

    def __init__(self, b, ap):
        self.b = b
        self.ap = ap


class T:
    def __init__(self, prog, es, name, shape, dtype, psum=False):
        nc = prog.nc
        prog.uid += 1
        name = "%s_u%d" % (name, prog.uid)
        self.t = es.enter_context(nc.psum_tensor(name, shape, dtype) if psum else nc.sbuf_tensor(name, shape, dtype))
        self.b = Buf(name)
        prog.bufs.append(self.b)

    def __getitem__(self, idx):
        return View(self.b, self.t[idx])

    def v(self, ap):
        return View(self.b, ap)


class Op:
    __slots__ = ("eng", "fn", "deps", "is_dma", "needs_inc", "ms", "dma_idx")


class Prog:
    CE = ("pe", "act", "dve", "pool")
    QS = ("sp", "act", "pool")

    def __init__(self, nc, es, K=8):
        self.nc = nc
        self.engs = {"pe": nc.tensor, "act": nc.scalar, "dve": nc.vector, "pool": nc.gpsimd, "sp": nc.sync}
        self.sem = {e: es.enter_context(nc.semaphore("sem_" + e)) for e in self.CE}
        self.K = K
        self.dsem = {q: [es.enter_context(nc.semaphore("dsem_%s%d" % (q, i))) for i in range(K)] for q in self.QS}
        self.ms = {e: 0 for e in self.CE}
        self.dcount = {q: 0 for q in self.QS}
        self.seen = {}
        self.ops = []
        self.bufs = []
        self.n_instr = 0
        self.uid = 0

    def buf(self, name):
        b = Buf(name)
        self.bufs.append(b)
        return b

    def add(self, eng, fn, reads=(), writes=(), dma=False):
        op = Op()
        op.eng = eng
        op.fn = fn
        op.is_dma = dma
        op.needs_inc = False
        op.ms = 0
        op.dma_idx = -1
        deps = {}
        for b in reads:
            if b.w is not None:
                deps[id(b.w)] = b.w
        for b in writes:
            if b.w is not None:
                deps[id(b.w)] = b.w
            for o in b.r:
                deps[id(o)] = o
        for b in reads:
            if dma:
                b.r.append(op)
            else:
                b.r = [o for o in b.r if o.is_dma or o.eng != eng]
                b.r.append(op)
        for b in writes:
            b.w = op
            b.r = []
        dl = []
        for d in deps.values():
            if d is op:
                continue
            if (not d.is_dma) and (not dma) and d.eng == eng:
                if eng == "pe" or not SAME_ENGINE_SYNC:
                    continue
            dl.append(d)
            if not d.is_dma:
                d.needs_inc = True
        op.deps = dl
        if dma:
            op.dma_idx = self.dcount[eng]
            self.dcount[eng] += 1
        self.ops.append(op)
        return op

    def _wait(self, eng, key, sem, val):
        k = (eng, key)
        if self.seen.get(k, 0) >= val:
            return
        self.seen[k] = val
        self.engs[eng].wait_ge(sem, val)

    def flush(self):
        K = self.K
        last = {}
        for op in self.ops:
            if not op.is_dma:
                last[op.eng] = op
        for op in last.values():
            op.needs_inc = True
        for op in self.ops:
            e = self.engs[op.eng]
            if op.is_dma:
                i = op.dma_idx
                q = op.eng
                if i >= K:
                    self._wait(q, ("d", q, i % K), self.dsem[q][i % K], 16 * (i // K))
            for d in op.deps:
                if d.is_dma:
                    j = d.dma_idx
                    self._wait(op.eng, ("d", d.eng, j % K), self.dsem[d.eng][j % K], 16 * (j // K + 1))
                else:
                    self._wait(op.eng, ("c", d.eng), self.sem[d.eng], d.ms)
            ins = op.fn(e)
            self.n_instr += 1
            if op.is_dma:
                ins.then_inc(self.dsem[op.eng][op.dma_idx % K], 16)
            elif op.needs_inc:
                self.ms[op.eng] += 1
                op.ms = self.ms[op.eng]
                ins.then_inc(self.sem[op.eng], 1)
        for eng in ("pe", "act", "dve", "pool", "sp"):
            for c in self.CE:
                if self.ms[c] > 0:
                    self._wait(eng, ("c", c), self.sem[c], self.ms[c])
            for q in self.QS:
                n = self.dcount[q]
                for j in range(K):
                    cnt = (n - j + K - 1) // K if n > j else 0
                    if cnt > 0:
                        self._wait(eng, ("d", q, j), self.dsem[q][j], 16 * cnt)
        for b in self.bufs:
            b.w = None
            b.r = []
        self.bufs = [b for b in self.bufs if not b.name.startswith("~")]
        self.ops = []

    def mm(self, out, lhsT, rhs, start=True, stop=True, **kw):
        return self.add("pe", lambda e: e.matmul(out.ap, lhsT.ap, rhs.ap, start=start, stop=stop, **kw),
                        reads=[lhsT.b, rhs.b], writes=[out.b])

    def transpose(self, out, in_, ident):
        return self.add("pe", lambda e: e.transpose(out.ap, in_.ap, ident.ap), reads=[in_.b, ident.b], writes=[out.b])

    def act(self, out, in_, func, bias=None, scale=None, accum_out=None, extra_reads=()):
        kw = {}
        reads = [in_.b] + list(extra_reads)
        writes = [out.b]
        if bias is not None:
            if isinstance(bias, View):
                kw["bias"] = bias.ap
                reads.append(bias.b)
            else:
                kw["bias"] = bias
        if scale is not None:
            if isinstance(scale, View):
                kw["scale"] = scale.ap
                reads.append(scale.b)
            else:
                kw["scale"] = scale
        if accum_out is not None:
            kw["accum_out"] = accum_out.ap
            writes.append(accum_out.b)
        return self.add("act", lambda e: e.activation(out=out.ap, in_=in_.ap, func=func, **kw), reads=reads, writes=writes)

    def tt(self, eng, out, in0, in1, op):
        return self.add(eng, lambda e: e.tensor_tensor(out=out.ap, in0=in0.ap, in1=in1.ap, op=op),
                        reads=[in0.b, in1.b], writes=[out.b])

    def ts(self, eng, out, in0, s1, op0, s2=None, op1=None):
        reads = [in0.b]
        a1 = s1
        a2 = s2
        if isinstance(s1, View):
            a1 = s1.ap
            reads.append(s1.b)
        if isinstance(s2, View):
            a2 = s2.ap
            reads.append(s2.b)
        if op1 is None:
            return self.add(eng, lambda e: e.tensor_scalar(out=out.ap, in0=in0.ap, scalar1=a1, scalar2=None, op0=op0),
                            reads=reads, writes=[out.b])
        return self.add(eng, lambda e: e.tensor_scalar(out=out.ap, in0=in0.ap, scalar1=a1, scalar2=a2, op0=op0, op1=op1),
                        reads=reads, writes=[out.b])

    def stt(self, out, in0, scalar, in1, op0, op1):
        reads = [in0.b, in1.b]
        sc = scalar
        if isinstance(scalar, View):
            sc = scalar.ap
            reads.append(scalar.b)
        return self.add("dve", lambda e: e.scalar_tensor_tensor(out=out.ap, in0=in0.ap, scalar=sc, in1=in1.ap, op0=op0, op1=op1),
                        reads=reads, writes=[out.b])

    def copy(self, eng, out, in_):
        if eng == "act":
            return self.add("act", lambda e: e.copy(out=out.ap, in_=in_.ap), reads=[in_.b], writes=[out.b])
        return self.add(eng, lambda e: e.tensor_copy(out=out.ap, in_=in_.ap), reads=[in_.b], writes=[out.b])

    def recip(self, out, in_):
        return self.add("dve", lambda e: e.reciprocal(out=out.ap, in_=in_.ap), reads=[in_.b], writes=[out.b])

    def memset(self, eng, out, val):
        return self.add(eng, lambda e: e.memset(out.ap, val), reads=[], writes=[out.b])

    def dma(self, q, out, in_, reads=(), writes=(), **kw):
        return self.add(q, lambda e: e.dma_start(out=out, in_=in_, **kw), reads=list(reads), writes=list(writes), dma=True)


def _const_tables(S):
    bf = ml_dtypes.bfloat16
    ident = np.eye(128, dtype=np.float32)
    rot = np.zeros((128, 128), np.float32)
    for m in range(64):
        rot[m + 64, m] = -1.0
    for m in range(64, 128):
        rot[m - 64, m] = 1.0
    i = np.arange(128)[:, None]
    j = np.arange(128)[None, :]
    m0 = (i >= j).astype(np.float32)
    m1 = (i <= j).astype(np.float32)
    ones = np.ones((128, 128), np.float32)
    s = np.arange(64)[:, None]
    t = np.arange(64)[None, :]
    su = (s < t).astype(np.float32)
    iu = (s <= t).astype(np.float32)
    sl = (s > t).astype(np.float32)
    il = (s >= t).astype(np.float32)
    mf = np.tile(np.concatenate([su, iu, su, iu], axis=1), (2, 1))
    mb = np.tile(np.concatenate([sl, il, sl, il], axis=1), (2, 1))
    lf = np.tile(np.concatenate([sl, sl], axis=1), (2, 1))
    lb = np.tile(np.concatenate([su, su], axis=1), (2, 1))
    i64 = np.tile(np.concatenate([np.eye(64, dtype=np.float32)] * 2, axis=1), (2, 1))
    cb = np.concatenate([ident, rot, m0, m1, ones, mf, mb, lf, lb, i64], axis=1).astype(bf)
    half = 64
    inv_freq = (1.0 / (np.float32(10000.0) ** (np.arange(half, dtype=np.float32) * np.float32(2.0) / np.float32(128)))).astype(np.float32)
    ang = (np.arange(S, dtype=np.float32)[:, None] * inv_freq[None, :]).astype(np.float32)
    cos = np.cos(ang).astype(np.float32).T
    sin = np.sin(ang).astype(np.float32).T
    cs = np.concatenate([np.concatenate([cos, cos], 0), np.concatenate([sin, sin], 0)], axis=1)
    s2 = np.arange(128)[:, None]
    t2 = np.arange(128)[None, :]
    same = (s2 // 64) == (t2 // 64)
    p_i = (same & (s2 <= t2)).astype(np.float32)
    s_i = (same & (s2 >= t2)).astype(np.float32)
    p_s = (same & (s2 < t2)).astype(np.float32)
    s_s = (same & (s2 > t2)).astype(np.float32)
    ind = np.zeros((128, 2), np.float32)
    ind[:64, 0] = 1.0
    ind[64:, 1] = 1.0
    cf = np.concatenate([cs, p_i, s_i, p_s, s_s, ind], axis=1).astype(np.float32)
    return cb, cf


CB_IDENT, CB_ROT, CB_M0, CB_M1, CB_ONES, CB_MF, CB_MB, CB_LF, CB_LB, CB_I64 = 0, 128, 256, 384, 512, 640, 896, 1152, 1280, 1408
CB_W = 1536


def _valid_tables(S, valid_len):
    cols = []
    for (_, d) in GROUPS:
        L = S // d
        nblk = L // 128 + 1
        for r in range(d):
            n = np.arange(nblk * 128) - 64
            pos = n * d + r
            ok = (n >= 0) & (n < L) & (pos < valid_len)
            cols.append(ok.reshape(nblk, 128).T.astype(np.float32))
    return np.concatenate(cols, axis=1).astype(ml_dtypes.bfloat16)


def _tokmask(S, valid_len):
    return np.broadcast_to((np.arange(S) < valid_len).astype(np.float32)[None, :], (128, S)).astype(ml_dtypes.bfloat16).copy()


class Builder:
    def __init__(self, S, n_layers=4, debug=None, debug_out=(), rw_stop=99):
        self.debug_out = set(debug_out)
        self.rw_stop = rw_stop
        self.S = S
        self.n_layers = n_layers
        self.debug = debug or {}
        self.nc = bass.Bass("TRN2", target_bir_lowering=False)
        self.es = contextlib.ExitStack()

    def dram(self, name, shape, dtype, kind="Internal"):
        if name in self.debug_out:
            kind = "ExternalOutput"
        return self.nc.dram_tensor(name, list(shape), dtype, kind=kind).ap()

    def build(self):
        nc = self.nc
        S = self.S
        with self.es as es:
            self.P = Prog(nc, es)
            self.declare_io()
            self.persistent(es)
            self.prologue()
            x_cur = self.x_in
            bufs = [self.xa, self.xb]
            for layer in range(self.n_layers):
                import os as _os
                if int(_os.environ.get("ATTSTOP", "9")) == 0:
                    break
                x_next = self.y_out if layer == self.n_layers - 1 else bufs[layer % 2]
                j = layer // 2
                self.phase_norm(x_cur, layer)
                if layer % 2 == 0:
                    import os as _os
                    _as = int(_os.environ.get("ATTSTOP", "9"))
                    if _as >= 2:
                        self.phase_att_inproj(j)
                    if _as >= 3:
                        self.phase_att_core()
                    if _as >= 4:
                        self.phase_out(x_cur, x_next, self.wb_att_out[j], layer)
                else:
                    self.phase_rwkv(j)
                    self.phase_out(x_cur, x_next, self.wb_rwkv_out[j], layer)
                x_cur = x_next
        return nc

    def declare_io(self):
        S = self.S
        d = self.dram
        self.x_in = d("x", [S, D], F32, "ExternalInput")
        self.y_out = d("y", [S, D], F32, "ExternalOutput")
        self.in_norm_pre = d("norm_pre", [4, D], F32, "ExternalInput")
        self.in_norm_post = d("norm_post", [4, D], F32, "ExternalInput")
        self.in_att_w_in = d("att_w_in", [2, D, ATT_COLS], F32, "ExternalInput")
        self.in_att_w_out = d("att_w_out", [2, D, D], F32, "ExternalInput")
        self.in_rw = {}
        for name, shape in (("rwkv_mu_prev", [2, 6, D]), ("rwkv_mu_next", [2, 6, D]), ("rwkv_w_in", [2, D, 4 * D]),
                            ("rwkv_w0", [2, 2, D]), ("rwkv_w1", [2, 2, D, 96]), ("rwkv_w2", [2, 2, 96, D]),
                            ("rwkv_a0", [2, 2, D]), ("rwkv_a1", [2, 2, D, 96]), ("rwkv_a2", [2, 2, 96, D]),
                            ("rwkv_v0", [1, D]), ("rwkv_v1", [1, D, 64]), ("rwkv_v2", [1, 64, D]),
                            ("rwkv_k_k", [2, D]), ("rwkv_k_a", [2, D]), ("rwkv_r_k", [2, 2, D]),
                            ("rwkv_gn_g", [2, D]), ("rwkv_gn_b", [2, D]), ("rwkv_w_out", [2, D, D])):
            self.in_rw[name] = d(name, shape, F32, "ExternalInput")
        self.in_cb = d("const_bf", [128, CB_W], BF16, "ExternalInput")
        self.cf_w = 2 * S + 4 * 128 + 2
        self.in_cf = d("const_f32", [128, self.cf_w], F32, "ExternalInput")
        self.nvalid = sum(dd * ((S // dd) // 128 + 1) for (_, dd) in GROUPS)
        self.in_valid = d("valid", [128, self.nvalid], BF16, "ExternalInput")
        self.in_tokmask = d("tokmask", [128, S], BF16, "ExternalInput")
        self.xa = d("xa", [S, D], F32)
        self.xb = d("xb", [S, D], F32)
        self.hT = d("hT", [NCH, 128, S + 2], BF16)
        self.wb_att_in = [d("wb_att_in%d" % j, [D, ATT_COLS], BF16) for j in range(2)]
        self.wb_att_out = [d("wb_att_out%d" % j, [D, D], BF16) for j in range(2)]
        self.wb_rwkv_in = [d("wb_rwkv_in%d" % j, [D, 4 * D], BF16) for j in range(2)]
        self.wb_rwkv_out = [d("wb_rwkv_out%d" % j, [D, D], BF16) for j in range(2)]
        self.qT = []
        self.kT = []
        self.vv = []
        for g, (_, dd) in enumerate(GROUPS):
            L = S // dd
            self.qT.append(d("qT%d" % g, [NH_ATT, 128, dd, L], BF16))
            self.kT.append(d("kT%d" % g, [NH_ATT, 128, dd, L + 128], BF16))
            self.vv.append(d("vv%d" % g, [dd, L + 128, D], BF16))
        self.zT = d("zT", [D, S], BF16)
        self.yT = d("yT", [D, S], BF16)
        self.declare_rwkv()
        for name, (shape, dt) in self.debug.items():
            setattr(self, "dbg_" + name, d("dbg_" + name, shape, dt, "ExternalOutput"))

    def persistent(self, es):
        P = self.P
        self.cb = T(P, es, "cb", [128, CB_W], BF16)
        P.dma("sp", self.cb.t[:], self.in_cb[:, :], writes=[self.cb.b])
        self.zeros = T(P, es, "zeros", [128, 2048], BF16)
        P.memset("pool", self.zeros[:], 0.0)
        P.flush()

    def cbv(self, off, w=128, rows=128):
        return self.cb[0:rows, off:off + w]

    def prologue(self):
        P = self.P
        S = self.S

        def cast(dst, src, rows, cols):
            cw = min(cols, 2048)
            nseg = cols // cw
            rstep = max(1, 4096 // nseg)
            for r0 in range(0, rows, rstep):
                r1 = min(rows, r0 + rstep)
                if nseg == 1:
                    o = dst[r0:r1, :]
                    i = src[r0:r1, :]
                else:
                    o = dst[r0:r1, :].rearrange("r (s c) -> r s c", c=cw)
                    i = src[r0:r1, :].rearrange("r (s c) -> r s c", c=cw)
                P.dma("pool", o, i)

        nl = self.n_layers
        for j in range(2):
            if nl > 2 * j:
                cast(self.wb_att_in[j], self.in_att_w_in[j], D, ATT_COLS)
                cast(self.wb_att_out[j], self.in_att_w_out[j], D, D)
            if nl > 2 * j + 1:
                cast(self.wb_rwkv_in[j], self.in_rw["rwkv_w_in"][j], D, 4 * D)
                cast(self.wb_rwkv_out[j], self.in_rw["rwkv_w_out"][j], D, D)
        if nl > 1:
            cast(self.wb_w1, self.in_rw["rwkv_w1"].rearrange("j c d r -> (j c d) r"), 4 * D, 96)
            cast(self.wb_a1, self.in_rw["rwkv_a1"].rearrange("j c d r -> (j c d) r"), 4 * D, 96)
            cast(self.wb_w2, self.in_rw["rwkv_w2"].rearrange("j c r d -> (j c r) d"), 4 * 96, D)
            cast(self.wb_a2, self.in_rw["rwkv_a2"].rearrange("j c r d -> (j c r) d"), 4 * 96, D)
            cast(self.wb_v1, self.in_rw["rwkv_v1"][0], D, 64)
            cast(self.wb_v2, self.in_rw["rwkv_v2"][0], 64, D)
        for g, (_, dd) in enumerate(GROUPS):
            L = S // dd
            for h in range(NH_ATT):
                for side in (0, L + 64):
                    P.dma("sp", self.kT[g][h][:, :, side:side + 64],
                          self.zeros.t[:, 0:dd * 64].rearrange("p (r n) -> p r n", n=64), reads=[self.zeros.b])
            for r in range(dd):
                for side in (0, L + 64):
                    P.dma("sp", self.vv[g][r, side:side + 64, :], self.zeros.t[0:64, :], reads=[self.zeros.b])
        P.flush()

    def load_bcast(self, es, name, src_row):
        P = self.P
        t = T(P, es, name, [128, D], F32)
        P.dma("sp", t.t[:], src_row.partition_broadcast(128), writes=[t.b])
        return t

    def rstd_from_ss(self, ss, tmp, rstd, eps, n):
        P = self.P
        P.act(tmp, ss, AF.Sqrt, bias=self.eps_t[eps], scale=1.0 / n)
        P.recip(rstd, tmp)

    def phase_norm(self, x_src, layer):
        P = self.P
        S = self.S
        with contextlib.ExitStack() as es:
            g = self.load_bcast(es, "g_pre", self.in_norm_pre[layer:layer + 1, :])
            self.eps_t = {}
            epst = T(P, es, "epst", [128, 2], F32)
            P.memset("pool", epst[:, 0:1], RMS_EPS)
            self.eps_t[RMS_EPS] = epst[:, 0:1]
            NS = 3
            xt = [T(P, es, "xt%d" % i, [128, D], F32) for i in range(NS)]
            junk = T(P, es, "junk", [128, D], BF16)
            hb = [T(P, es, "hb%d" % i, [128, D], BF16) for i in range(2)]
            st = [T(P, es, "st%d" % i, [128, 3], F32) for i in range(4)]
            tp = [T(P, es, "tp%d" % i, [128, 1024], BF16, psum=True) for i in range(4)]
            hs = [T(P, es, "hs%d" % i, [128, NCH, 512], BF16) for i in range(2)]
            ident = self.cbv(CB_IDENT)
            nblk = S // 128
            for i in range(min(2, nblk)):
                P.dma("sp", xt[i % NS].t[:], x_src[i * 128:(i + 1) * 128, :], writes=[xt[i % NS].b])
            k = 0
            for i in range(nblk):
                if i + 2 < nblk:
                    P.dma("sp", xt[(i + 2) % NS].t[:], x_src[(i + 2) * 128:(i + 3) * 128, :], writes=[xt[(i + 2) % NS].b])
                x = xt[i % NS]
                s = st[i % 4]
                P.act(junk[:], x[:], AF.Square, accum_out=s[:, 0:1])
                self.rstd_from_ss(s[:, 0:1], s[:, 1:2], s[:, 2:3], RMS_EPS, D)
                h = hb[i % 2]
                P.stt(h[:], x[:], s[:, 2:3], g[:], ALU.mult, ALU.mult)
                u = i // 4
                sub = i % 4
                hst = hs[u % 2]
                for half in range(2):
                    pt = tp[k % 4]
                    k += 1
                    for c in range(8):
                        cc = half * 8 + c
                        P.transpose(pt[:, c * 128:(c + 1) * 128], h[:, cc * 128:(cc + 1) * 128], ident)
                    src = pt.v(pt.t[:, :].rearrange("p (c t) -> p c t", t=128))
                    dst = hst[:, half * 8:(half + 1) * 8, sub * 128:(sub + 1) * 128]
                    if half == 0:
                        P.copy("act", dst, src)
                    else:
                        P.copy("dve", dst, src)
                if sub == 3 or i == nblk - 1:
                    t0 = u * 512
                    n = (sub + 1) * 128
                    P.dma("sp", self.hT[:, :, 1 + t0:1 + t0 + n].rearrange("c p t -> p c t"), hst.t[:, :, 0:n], reads=[hst.b])
            zc = T(P, es, "zc", [128, NCH, 2], BF16)
            P.memset("pool", zc[:], 0.0)
            P.dma("sp", self.hT[:, :, 0:1].rearrange("c p t -> p c t"), zc.t[:, :, 0:1], reads=[zc.b], allow_slow_non_contiguous=True)
            P.dma("sp", self.hT[:, :, S + 1:S + 2].rearrange("c p t -> p c t"), zc.t[:, :, 1:2], reads=[zc.b], allow_slow_non_contiguous=True)
            P.flush()

    def phase_att_inproj(self, j):
        P = self.P
        S = self.S
        ST = min(2048, S)
        W = self.wb_att_in[j]
        with contextlib.ExitStack() as es:
            hts = T(P, es, "hts", [128, NCH, ST], BF16)
            cs = T(P, es, "cs", [128, 2, ST], F32)
            NW = 3
            wt = [T(P, es, "wt%d" % i, [128, NCH, 512], BF16) for i in range(NW)]
            ps = [T(P, es, "ps%d" % i, [128, 512], F32, psum=True) for i in range(3)]
            pr = [T(P, es, "pr%d" % i, [128, 512], F32, psum=True) for i in range(2)]
            tb = [T(P, es, "tb%d" % i, [128, 512], BF16) for i in range(3)]
            t1 = [T(P, es, "t1_%d" % i, [128, 512], F32) for i in range(3)]
            t2 = [T(P, es, "t2_%d" % i, [128, 512], F32) for i in range(3)]
            qs = [T(P, es, "qs%d" % i, [128, ST], BF16) for i in range(3)]
            vs = [T(P, es, "vs%d" % i, [128, ST // 128, 512], BF16) for i in range(2)]
            rot = self.cbv(CB_ROT)
            NCG = ATT_COLS // 512
            Wv = W.rearrange("(c p) n -> p c n", p=128)
            cnt = {"ps": 0, "pr": 0, "tb": 0, "qs": 0, "vs": 0, "ev": 0}

            def load_w(cg):
                P.dma("sp", wt[cg % NW].t[:], Wv[:, :, cg * 512:(cg + 1) * 512], writes=[wt[cg % NW].b])

            for st_i in range(S // ST):
                t0 = st_i * ST
                P.dma("sp", hts.t[:], self.hT[:, :, 1 + t0:1 + t0 + ST].rearrange("c p t -> p c t"), writes=[hts.b])
                P.dma("sp", cs.t[:, 0, :], self.in_cf[:, t0:t0 + ST], writes=[cs.b])
                P.dma("sp", cs.t[:, 1, :], self.in_cf[:, S + t0:S + t0 + ST], writes=[cs.b])
                load_w(0)
                load_w(1)
                for cg in range(NCG):
                    if cg + 2 < NCG:
                        load_w(cg + 2)
                    w = wt[cg % NW]
                    if cg < 36:
                        g = cg // 12
                        typ = (cg % 12) // 4
                        h0 = (cg % 4) * 4
                    else:
                        g = -1
                        typ = 3
                        h0 = (cg - 36) * 4
                    if typ == 2:
                        dd = GROUPS[g][1]
                        vst = vs[cnt["vs"] % 2]
                        cnt["vs"] += 1
                        for tbi in range(ST // 128):
                            p = ps[cnt["ps"] % 3]
                            cnt["ps"] += 1
                            for c in range(NCH):
                                P.mm(p[:], hts[:, c, tbi * 128:(tbi + 1) * 128], w[:, c, :], start=(c == 0), stop=(c == NCH - 1))
                            if cnt["ev"] % 2 == 0:
                                P.copy("act", vst[:, tbi, :], p[:])
                            else:
                                P.copy("dve", vst[:, tbi, :], p[:])
                            cnt["ev"] += 1
                        npb = 128 // dd
                        for r in range(dd):
                            n0 = 64 + t0 // dd
                            dst = self.vv[g][r, n0:n0 + ST // dd, h0 * 128:h0 * 128 + 512].rearrange("(b n) c -> n b c", n=npb)
                            src = vst.t[r::dd, :, :] if dd > 1 else vst.t[:, :, :]
                            P.dma("sp", dst, src, reads=[vst.b])
                    else:
                        for jh in range(4):
                            h = h0 + jh
                            qst = qs[cnt["qs"] % 3]
                            cnt["qs"] += 1
                            dd = GROUPS[g][1] if typ < 2 else 1
                            for sub in range(ST // 512):
                                p = ps[cnt["ps"] % 3]
                                cnt["ps"] += 1
                                for c in range(NCH):
                                    P.mm(p[:], w[:, c, jh * 128:(jh + 1) * 128], hts[:, c, sub * 512:(sub + 1) * 512],
                                         start=(c == 0), stop=(c == NCH - 1))
                                if typ == 3:
                                    P.act(qst[:, sub * 512:(sub + 1) * 512], p[:], AF.Silu)
                                    continue
                                k = cnt["tb"] % 3
                                cnt["tb"] += 1
                                tbf = tb[k]
                                P.copy("act", tbf[:], p[:])
                                rp = pr[cnt["pr"] % 2]
                                cnt["pr"] += 1
                                P.mm(rp[:], rot, tbf[:])
                                P.tt("dve", t2[k][:], rp[:], cs[:, 1, sub * 512:(sub + 1) * 512], ALU.mult)
                                P.tt("dve", t1[k][:], p[:], cs[:, 0, sub * 512:(sub + 1) * 512], ALU.mult)
                                nsub = 512 // dd
                                if dd == 1:
                                    dst = qst[:, sub * 512:(sub + 1) * 512]
                                    P.tt("pool", dst, t1[k][:], t2[k][:], ALU.add)
                                else:
                                    dst = qst.v(qst.t[:, :].rearrange("p (r n) -> p r n", r=dd)[:, :, sub * nsub:(sub + 1) * nsub])
                                    a = t1[k].v(t1[k].t[:, :].rearrange("p (n r) -> p r n", r=dd))
                                    b = t2[k].v(t2[k].t[:, :].rearrange("p (n r) -> p r n", r=dd))
                                    P.tt("pool", dst, a, b, ALU.add)
                            if typ == 3:
                                P.dma("sp", self.zT[h * 128:(h + 1) * 128, t0:t0 + ST], qst.t[:, :], reads=[qst.b])
                            elif typ == 0:
                                dst = self.qT[g][h][:, :, t0 // dd:(t0 + ST) // dd]
                                P.dma("sp", dst, qst.t[:, :].rearrange("p (r n) -> p r n", r=dd), reads=[qst.b])
                            else:
                                dst = self.kT[g][h][:, :, 64 + t0 // dd:64 + (t0 + ST) // dd]
                                P.dma("sp", dst, qst.t[:, :].rearrange("p (r n) -> p r n", r=dd), reads=[qst.b])
            P.flush()

    def phase_att_core(self):
        P = self.P
        S = self.S
        scale = 1.0 / math.sqrt(DH)
        with contextlib.ExitStack() as es:
            vt = T(P, es, "valid", [128, self.nvalid], BF16)
            P.dma("sp", vt.t[:], self.in_valid[:, :], writes=[vt.b])
            accn = [T(P, es, "accn%d" % i, [128, S], F32) for i in range(2)]
            accd = [T(P, es, "accd%d" % i, [128, S], F32) for i in range(2)]
            NSL = 6
            qsb = [T(P, es, "qsb%d" % i, [128, 512], BF16) for i in range(NSL)]
            ksb = [T(P, es, "ksb%d" % i, [128, 640], BF16) for i in range(NSL)]
            vsb = [T(P, es, "vsb%d" % i, [128, 5, 128], BF16) for i in range(NSL)]
            pss = [T(P, es, "pss%d" % i, [128, 512], F32, psum=True) for i in range(4)]
            psn = [T(P, es, "psn%d" % i, [128, 512], F32, psum=True) for i in range(2)]
            psd = [T(P, es, "psd%d" % i, [128, 512], F32, psum=True) for i in range(2)]
            pe = [T(P, es, "pe%d" % i, [128, 256], BF16) for i in range(6)]
            pm = [T(P, es, "pm%d" % i, [128, 256], BF16) for i in range(6)]
            rc = [T(P, es, "rc%d" % i, [128, 512], F32) for i in range(2)]
            ob = [T(P, es, "ob%d" % i, [128, 512], F32) for i in range(2)]
            zb = [T(P, es, "zb%d" % i, [128, 512], BF16) for i in range(2)]
            yb = [T(P, es, "yb%d" % i, [128, 512], BF16) for i in range(2)]
            ones = self.cbv(CB_ONES)
            mask = self.cb[:, CB_M0:CB_M0 + 256]
            tiles = []
            voff = 0
            for g, (_, dd) in enumerate(GROUPS):
                L = S // dd
                QT = min(512, L)
                nblk_r = L // 128 + 1
                for r in range(dd):
                    for qt in range(L // QT):
                        tiles.append((g, dd, L, QT, r, qt, voff + r * nblk_r))
                voff += dd * nblk_r
            cnt = {"sl": 0, "ps": 0, "pe": 0, "fin": 0}

            def load_tile(h, ti, slot):
                g, dd, L, QT, r, qt, vo = tiles[ti]
                nb = QT // 128 + 1
                P.dma("sp", qsb[slot].t[:, 0:QT], self.qT[g][h][:, r, qt * QT:(qt + 1) * QT], writes=[qsb[slot].b])
                P.dma("sp", ksb[slot].t[:, 0:QT + 128], self.kT[g][h][:, r, qt * QT:(qt + 1) * QT + 128], writes=[ksb[slot].b])
                P.dma("sp", vsb[slot].t[:, 0:nb, :],
                      self.vv[g][r, qt * QT:qt * QT + QT + 128, h * 128:(h + 1) * 128].rearrange("(b p) c -> p b c", p=128),
                      writes=[vsb[slot].b])

            seq = [(h, ti) for h in range(NH_ATT) for ti in range(len(tiles))]
            PF = 2
            qbs = []
            for i, (h, ti) in enumerate(seq):
                for qb in range(tiles[ti][3] // 128):
                    qbs.append((i, h, ti, qb))
            st_ = {}
            for i in range(min(PF, len(seq))):
                load_tile(seq[i][0], seq[i][1], i % NSL)

            def stageS(k):
                i, h, ti, qb = qbs[k]
                g, dd, L, QT, r, qt, vo = tiles[ti]
                nqb = QT // 128
                if qb == 0 and i + PF < len(seq):
                    load_tile(seq[i + PF][0], seq[i + PF][1], (i + PF) % NSL)
                slot = i % NSL
                sc = pss[k % len(pss)]
                qv = qsb[slot][:, qb * 128:(qb + 1) * 128]
                for kb in range(2):
                    P.mm(sc[:, kb * 128:(kb + 1) * 128], ksb[slot][:, (qb + kb) * 128:(qb + kb + 1) * 128], qv)
                e = pe[k % len(pe)]
                m = pm[k % len(pm)]
                P.act(e[:], sc[:, 0:256], AF.Exp, scale=scale)
                P.tt("pool" if k % 3 == 2 else "dve", m[:], e[:], self.cb[:, CB_M0:CB_M0 + 256], ALU.mult)
                st_[k] = m

            def stageV(k):
                i, h, ti, qb = qbs[k]
                g, dd, L, QT, r, qt, vo = tiles[ti]
                nqb = QT // 128
                slot = i % NSL
                m = st_.pop(k)
                an = accn[h % 2]
                ad = accd[h % 2]
                pn = psn[i % 2]
                pd = psd[i % 2]
                for kb in range(2):
                    P.mm(pn[:, qb * 128:(qb + 1) * 128], vsb[slot][:, qb + kb, :], m[:, kb * 128:(kb + 1) * 128],
                         start=(kb == 0), stop=(kb == 1))
                for kb in range(2):
                    blk = vo + qt * nqb + qb + kb
                    P.mm(pd[:, qb * 128:(qb + 1) * 128], View(vt.b, vt.t[:, blk:blk + 1].broadcast_to([128, 128])), m[:, kb * 128:(kb + 1) * 128],
                         start=(kb == 0), stop=(kb == 1))
                if qb != nqb - 1:
                    return
                base = qt * QT * dd + r
                if dd == 1:
                    dn = an[:, base:base + QT]
                    dden = ad[:, base:base + QT]
                else:
                    dn = an.v(an.t[:, qt * QT * dd:(qt + 1) * QT * dd].rearrange("p (n r) -> p r n", r=dd)[:, r, :])
                    dden = ad.v(ad.t[:, qt * QT * dd:(qt + 1) * QT * dd].rearrange("p (n r) -> p r n", r=dd)[:, r, :])
                if g == 0:
                    P.copy("dve", dn, pn[:, 0:QT])
                    P.copy("act", dden, pd[:, 0:QT])
                else:
                    P.tt("dve", dn, pn[:, 0:QT], dn, ALU.add)
                    P.tt("dve", dden, pd[:, 0:QT], dden, ALU.add)
                if ti == len(tiles) - 1:
                    for c0 in range(0, S, 512):
                        kf = cnt["fin"] % 2
                        cnt["fin"] += 1
                        w = min(512, S - c0)
                        P.dma("act", zb[kf].t[:, 0:w], self.zT[h * 128:(h + 1) * 128, c0:c0 + w], writes=[zb[kf].b])
                        P.ts("dve", rc[kf][:, 0:w], ad[:, c0:c0 + w], 1e-30, ALU.max)
                        P.recip(rc[kf][:, 0:w], rc[kf][:, 0:w])
                        P.tt("pool", ob[kf][:, 0:w], an[:, c0:c0 + w], rc[kf][:, 0:w], ALU.mult)
                        P.tt("pool", yb[kf][:, 0:w], ob[kf][:, 0:w], zb[kf][:, 0:w], ALU.mult)
                        P.dma("act", self.yT[h * 128:(h + 1) * 128, c0:c0 + w], yb[kf].t[:, 0:w], reads=[yb[kf].b])

            LAG = 2
            for k in range(len(qbs) + LAG):
                if k < len(qbs):
                    stageS(k)
                if k - LAG >= 0:
                    stageV(k - LAG)
            P.flush()

    def phase_out(self, x_src, x_dst, Wb, layer):
        P = self.P
        S = self.S
        with contextlib.ExitStack() as es:
            g = self.load_bcast(es, "g_post", self.in_norm_post[layer:layer + 1, :])
            epst = T(P, es, "epst", [128, 2], F32)
            P.memset("pool", epst[:, 0:1], RMS_EPS)
            self.eps_t = {RMS_EPS: epst[:, 0:1]}
            w = T(P, es, "wout", [128, NCH, D], BF16)
            P.dma("sp", w.t[:], Wb.rearrange("(c p) n -> p c n", p=128), writes=[w.b])
            ys = [T(P, es, "ys%d" % i, [128, NCH, 512], BF16) for i in range(2)]
            xt = [T(P, es, "xo%d" % i, [128, D], F32) for i in range(3)]
            ot = [T(P, es, "ot%d" % i, [128, D], F32) for i in range(2)]
            junk = T(P, es, "junk", [128, D], BF16)
            st = [T(P, es, "st%d" % i, [128, 3], F32) for i in range(4)]
            po = [T(P, es, "po%d" % i, [128, D], F32, psum=True) for i in range(2)]
            yTv = self.yT.rearrange("(c p) t -> p c t", p=128)
            nu = (S + 511) // 512

            def load_u(u):
                t0 = u * 512
                n = min(512, S - t0)
                P.dma("sp", ys[u % 2].t[:, :, 0:n], yTv[:, :, t0:t0 + n], writes=[ys[u % 2].b])

            load_u(0)
            nblk = S // 128
            for i in range(min(2, nblk)):
                P.dma("sp", xt[i % 3].t[:], x_src[i * 128:(i + 1) * 128, :], writes=[xt[i % 3].b])
            for i in range(nblk):
                u = i // 4
                sub = i % 4
                if sub == 0 and u + 1 < nu:
                    load_u(u + 1)
                if i + 2 < nblk:
                    P.dma("sp", xt[(i + 2) % 3].t[:], x_src[(i + 2) * 128:(i + 3) * 128, :], writes=[xt[(i + 2) % 3].b])
                y = ys[u % 2]
                p = po[i % 2]
                for nb in range(4):
                    for c in range(NCH):
                        P.mm(p[:, nb * 512:(nb + 1) * 512], y[:, c, sub * 128:(sub + 1) * 128], w[:, c, nb * 512:(nb + 1) * 512],
                             start=(c == 0), stop=(c == NCH - 1))
                s = st[i % 4]
                P.act(junk[:], p[:], AF.Square, accum_out=s[:, 0:1])
                self.rstd_from_ss(s[:, 0:1], s[:, 1:2], s[:, 2:3], RMS_EPS, D)
                o = ot[i % 2]
                P.stt(o[:], p[:], s[:, 2:3], g[:], ALU.mult, ALU.mult)
                P.tt("pool", o[:], o[:], xt[i % 3][:], ALU.add)
                P.dma("act", x_dst[i * 128:(i + 1) * 128, :], o.t[:], reads=[o.b])
            P.flush()

    def declare_rwkv(self):
        S = self.S
        d = self.dram
        self.rw_r = d("rw_r", [S, D], BF16)
        self.rw_k = d("rw_k", [S, D], BF16)
        self.rw_sz = d("rw_sz", [S, D], BF16)
        self.rw_v = [d("rw_v%d" % j, [S, D], BF16) for j in range(2)]
        self.rw_sw = [d("rw_sw%d" % c, [S, D], F32) for c in range(2)]
        self.rw_a = [d("rw_a%d" % c, [S, D], BF16) for c in range(2)]
        self.rw_gate = d("rw_gate", [S, D], BF16)
        self.wb_w1 = d("wb_w1", [4 * D, 96], BF16)
        self.wb_a1 = d("wb_a1", [4 * D, 96], BF16)
        self.wb_w2 = d("wb_w2", [4 * 96, D], BF16)
        self.wb_a2 = d("wb_a2", [4 * 96, D], BF16)
        self.wb_v1 = d("wb_v1", [D, 64], BF16)
        self.wb_v2 = d("wb_v2", [64, D], BF16)
        self.ft = [d("ft%d" % c, [4, 16, 128, S], BF16) for c in range(2)]
        self.ah = [d("ah%d" % c, [S, D], BF16) for c in range(2)]
        self.kh = [d("kh%d" % c, [S, D], BF16) for c in range(2)]
        self.vp = d("vp", [S, D], BF16)
        self.bon = d("bon", [S, NHR], F32)
        self.gam = [d("gam%d" % c, [128, 16, S // 64], F32) for c in range(2)]
        self.am = [d("am%d" % c, [S // 128, 128, 16, 512], BF16) for c in range(2)]
        self.ttm = [d("ttm%d" % c, [S // 128, 128, 32, 64], BF16) for c in range(2)]
        self.yd = [d("yd%d" % c, [S, D], F32) for c in range(2)]

    def phase_rwkv(self, j):
        self.vp_src = self.vp if j == 1 else self.rw_v[j]
        for i, f in enumerate((lambda: self.phase_rwkv_proj(j), lambda: self.phase_rwkv_prep(j), self.phase_rwkv_chain,
                               self.phase_rwkv_scan, lambda: self.phase_rwkv_post(j))):
            if i < self.rw_stop:
                f()

    def phase_rwkv_proj(self, j):
        P = self.P
        S = self.S
        TT = min(512, S)
        NTB = TT // 128
        vres = (j == 1)
        Wv = self.wb_rwkv_in[j].rearrange("(c p) n -> p c n", p=128)
        R = self.in_rw
        with contextlib.ExitStack() as es:
            mup = T(P, es, "mup", [128, 6, NCH], F32)
            mun = T(P, es, "mun", [128, 6, NCH], F32)
            P.dma("sp", mup.t[:], R["rwkv_mu_prev"][j].rearrange("g (c p) -> p g c", p=128), writes=[mup.b], allow_slow_non_contiguous=True)
            P.dma("sp", mun.t[:], R["rwkv_mu_next"][j].rearrange("g (c p) -> p g c", p=128), writes=[mun.b], allow_slow_non_contiguous=True)
            w1 = []
            a1 = []
            w2 = []
            a2 = []
            for c in range(2):
                o = (j * 2 + c)
                t_ = T(P, es, "w1_%d" % c, [128, NCH, 96], BF16)
                P.dma("sp", t_.t[:], self.wb_w1[o * D:(o + 1) * D, :].rearrange("(k p) r -> p k r", p=128), writes=[t_.b])
                w1.append(t_)
                t_ = T(P, es, "a1_%d" % c, [128, NCH, 96], BF16)
                P.dma("sp", t_.t[:], self.wb_a1[o * D:(o + 1) * D, :].rearrange("(k p) r -> p k r", p=128), writes=[t_.b])
                a1.append(t_)
                t_ = T(P, es, "w2_%d" % c, [98, D], BF16)
                P.dma("sp", t_.t[0:96, :], self.wb_w2[o * 96:(o + 1) * 96, :], writes=[t_.b])
                w2.append(t_)
                t_ = T(P, es, "a2_%d" % c, [98, D], BF16)
                P.dma("sp", t_.t[0:96, :], self.wb_a2[o * 96:(o + 1) * 96, :], writes=[t_.b])
                a2.append(t_)
            if vres:
                v1 = T(P, es, "v1", [128, NCH, 64], BF16)
                P.dma("sp", v1.t[:], self.wb_v1.rearrange("(k p) r -> p k r", p=128), writes=[v1.b])
                v2 = T(P, es, "v2", [66, D], BF16)
                P.dma("sp", v2.t[0:64, :], self.wb_v2[:, :], writes=[v2.b])
            with contextlib.ExitStack() as es2:
                bf_ = T(P, es2, "bias_f", [1, D], F32)
                bh = T(P, es2, "bias_h", [1, D], BF16)
                b32 = T(P, es2, "bias_32", [1, D], F32)
                bl = T(P, es2, "bias_l", [1, D], BF16)
                rows = [(R["rwkv_w0"][j, 0:1, :], w2[0], 96), (R["rwkv_w0"][j, 1:2, :], w2[1], 96),
                        (R["rwkv_a0"][j, 0:1, :], a2[0], 96), (R["rwkv_a0"][j, 1:2, :], a2[1], 96)]
                if vres:
                    rows.append((R["rwkv_v0"][0:1, :], v2, 64))
                for src, dst, r0 in rows:
                    P.dma("sp", bf_.t[:], src, writes=[bf_.b])
                    P.copy("dve", bh[:], bf_[:])
                    P.copy("dve", b32[:], bh[:])
                    P.tt("dve", b32[:], bf_[:], b32[:], ALU.subtract)
                    P.copy("dve", bl[:], b32[:])
                    P.dma("sp", dst.t[r0:r0 + 1, :], bh.t[:], reads=[bh.b], writes=[dst.b])
                    P.dma("sp", dst.t[r0 + 1:r0 + 2, :], bl.t[:], reads=[bl.b], writes=[dst.b])
                P.flush()
            hts = [T(P, es, "rhts%d" % i, [128, NCH, TT + 2], BF16) for i in range(1)]
            lsf = [T(P, es, "lsf%d" % i, [128, 512], F32) for i in range(5)]
            lsb = [T(P, es, "lsb%d" % i, [128, 512], BF16) for i in range(6)]
            dp = T(P, es, "dp", [128, NCH, TT], BF16)
            tmk = [T(P, es, "tmk%d" % i, [128, TT], BF16) for i in range(2)]
            dn = T(P, es, "dn", [128, NCH, TT], BF16)
            xs = [T(P, es, "xs%d" % i, [128, NCH, TT], BF16) for i in range(2)]
            tmp = [T(P, es, "xtmp%d" % i, [128, TT], F32) for i in range(2)]
            wt = [T(P, es, "rwt%d" % i, [128, NCH, 512], BF16) for i in range(3)]
            ps = [T(P, es, "rps%d" % i, [128, 512], F32, psum=True) for i in range(4)]
            pl = [T(P, es, "rpl%d" % i, [128, 512], F32, psum=True) for i in range(2)]
            stg = [T(P, es, "rstg%d" % i, [128, NTB, 512], BF16) for i in range(2)]
            l1 = [T(P, es, "l1_%d" % i, [98, TT], BF16) for i in range(2)]
            for t_ in l1:
                P.memset("pool", t_[96:98, :], 1.0)
            l1v = None
            if vres:
                l1v = T(P, es, "l1v", [66, TT], BF16)
                P.memset("pool", l1v[64:66, :], 1.0)
            cnt = {"ps": 0, "pl": 0, "stg": 0, "stf": 0, "wt": 0, "ev": 0, "xs": 0, "l1": 0}
            ntile = S // TT

            def load_h(ti):
                t0 = ti * TT
                P.dma("sp", hts[0].t[:], self.hT[:, :, t0:t0 + TT + 2].rearrange("c p t -> p c t"), writes=[hts[0].b])

            def evac(dst, src, func=None):
                if func is not None:
                    P.act(dst, src, func)
                else:
                    P.copy("act", dst, src)
                cnt["ev"] += 1

            def lora(x, wA, wB, krows, func1, dst_dram, fp32_out, l1t, t0):
                p1 = pl[cnt["pl"] % 2]
                cnt["pl"] += 1
                for c in range(NCH):
                    P.mm(p1[0:krows, 0:TT], wA[:, c, 0:krows], x[:, c, :], start=(c == 0), stop=(c == NCH - 1))
                if func1 is None:
                    P.copy("act", l1t[0:krows, :], p1[0:krows, 0:TT])
                else:
                    P.act(l1t[0:krows, :], p1[0:krows, 0:TT], func1)
                for tb in range(NTB):
                    for cg in range(4):
                        p2 = ps[cnt["ps"] % 4]
                        cnt["ps"] += 1
                        P.mm(p2[:], l1t[0:krows + 2, tb * 128:(tb + 1) * 128], wB[0:krows + 2, cg * 512:(cg + 1) * 512])
                        if fp32_out:
                            sf = lsf[cnt["stf"] % len(lsf)]
                        else:
                            sf = lsb[cnt["stf"] % len(lsb)]
                        cnt["stf"] += 1
                        P.act(sf[:], p2[:], AF.Sigmoid)
                        P.dma("act", dst_dram[t0 + tb * 128:t0 + (tb + 1) * 128, cg * 512:(cg + 1) * 512], sf.t[:], reads=[sf.b])

            wseq = [(gi, cg) for _ in range(ntile) for gi in range(4) for cg in range(4)]

            def issue_w(k):
                if k < len(wseq):
                    gi_, cg_ = wseq[k]
                    col0_ = gi_ * D + cg_ * 512
                    P.dma("sp", wt[k % 3].t[:], Wv[:, :, col0_:col0_ + 512], writes=[wt[k % 3].b])

            load_h(0)
            issue_w(0)
            issue_w(1)
            for ti in range(ntile):
                t0 = ti * TT
                h = hts[0]
                P.tt("pool", dp[:], h[:, :, 0:TT], h[:, :, 1:TT + 1], ALU.subtract)
                mk_ = tmk[ti % 2]
                P.dma("sp", mk_.t[:], self.in_tokmask[:, t0:t0 + TT], writes=[mk_.b])
                P.tt("pool", dp[:], dp[:], View(mk_.b, mk_.t[:, :].unsqueeze(1).to_broadcast([128, NCH, TT])), ALU.mult)
                P.tt("pool", dn[:], h[:, :, 2:TT + 2], h[:, :, 1:TT + 1], ALU.subtract)
                for g in (0, 2, 3, 5, 1, 4):
                    x = xs[cnt["xs"] % 2]
                    cnt["xs"] += 1
                    for c in range(NCH):
                        tm = tmp[c % 2]
                        P.stt(tm[:], dp[:, c, :], mup[:, g, c:c + 1], h[:, c, 1:TT + 1], ALU.mult, ALU.add)
                        P.stt(x[:, c, :], dn[:, c, :], mun[:, g, c:c + 1], tm[:], ALU.mult, ALU.add)
                    if g in (0, 2, 3, 5):
                        gi = {0: 0, 2: 1, 3: 2, 5: 3}[g]
                        dst = {0: self.rw_r, 2: self.rw_k, 3: self.rw_v[j], 5: self.rw_sz}[g]
                        for cg in range(4):
                            w = wt[cnt["wt"] % 3]
                            issue_w(cnt["wt"] + 2)
                            cnt["wt"] += 1
                            sg = stg[cnt["stg"] % 2]
                            cnt["stg"] += 1
                            for tb in range(NTB):
                                p = ps[cnt["ps"] % 4]
                                cnt["ps"] += 1
                                for c in range(NCH):
                                    P.mm(p[:], x[:, c, tb * 128:(tb + 1) * 128], w[:, c, :], start=(c == 0), stop=(c == NCH - 1))
                                evac(sg[:, tb, :], p[:], AF.Silu if g == 5 else None)
                            P.dma("sp", dst[t0:t0 + TT, cg * 512:(cg + 1) * 512].rearrange("(b p) c -> p b c", p=128), sg.t[:], reads=[sg.b])
                        if g == 3 and vres:
                            lora(x, v1, v2, 64, None, self.rw_gate, False, l1v, t0)
                    elif g == 1:
                        for c in range(2):
                            lt = l1[cnt["l1"] % 2]
                            cnt["l1"] += 1
                            lora(x, w1[c], w2[c], 96, AF.Tanh, self.rw_sw[c], True, lt, t0)
                    else:
                        for c in range(2):
                            lt = l1[cnt["l1"] % 2]
                            cnt["l1"] += 1
                            lora(x, a1[c], a2[c], 96, None, self.rw_a[c], False, lt, t0)
                if ti + 1 < ntile:
                    load_h(ti + 1)
            P.flush()
    def phase_rwkv_prep(self, j):
        P = self.P
        S = self.S
        vres = (j == 1)
        R = self.in_rw
        CDEC = math.exp(-0.5)
        nblk = S // 128
        self.vp_src = self.vp if vres else self.rw_v[j]
        with contextlib.ExitStack() as es:
            kkB = self.load_bcast(es, "kkB", R["rwkv_k_k"][j:j + 1, :])
            kaB = self.load_bcast(es, "kaB", R["rwkv_k_a"][j:j + 1, :])
            rkB = []
            for c in range(2):
                t_ = T(P, es, "rkB%d" % c, [128, D], BF16)
                P.dma("pool", t_.t[:], R["rwkv_r_k"][j, c:c + 1, :].partition_broadcast(128), writes=[t_.b])
                rkB.append(t_)
            tri = T(P, es, "tri", [128, 4, 128], F32)
            P.dma("sp", tri.t[:], self.in_cf[:, 2 * S:2 * S + 512].rearrange("p (a b) -> p a b", b=128), writes=[tri.b])
            ind = T(P, es, "ind", [128, 2], F32)
            P.dma("sp", ind.t[:], self.in_cf[:, 2 * S + 512:2 * S + 514], writes=[ind.b])
            ident = self.cbv(CB_IDENT)
            inb = [T(P, es, "inb%d" % i, [128, 5, D], BF16) for i in range(2)]
            swt = [T(P, es, "swt%d" % c, [128, D], F32) for c in range(2)]
            if vres:
                gv = T(P, es, "gv", [128, 2, D], BF16)
            kkf = T(P, es, "kkf", [128, D], F32)
            sq = T(P, es, "sq", [128, D], F32)
            kk = T(P, es, "kk", [128, D], BF16)
            kkac = T(P, es, "kkac", [128, D], BF16)
            kmk = T(P, es, "kmk", [128, D], BF16)
            vpt = T(P, es, "vpt", [128, D], BF16)
            tb_ = T(P, es, "tb_", [128, D], BF16)
            kd = T(P, es, "kd", [128, D], BF16)
            kka = T(P, es, "kka", [128, D], BF16)
            ub = kkf
            sm = [T(P, es, "sm%d" % i, [128, 4, NHR], F32) for i in range(2)]
            Eb = [T(P, es, "Eb%d" % i, [128, 4, 512], BF16) for i in range(2)]
            prod = [T(P, es, "prod%d" % i, [128, D], BF16) for i in range(6)]
            fts = T(P, es, "fts", [128, 4, 16, 128], BF16)
            gall = [T(P, es, "gall%d" % i, [128, 16, 2], F32) for i in range(4)]
            pc = [T(P, es, "pc%d" % i, [128, 512], F32, psum=True) for i in range(5)]
            ptr = [T(P, es, "ptr%d" % i, [128, 1024], BF16, psum=True) for i in range(2)]
            pg = T(P, es, "pg", [128, 16, 2], F32, psum=True)
            cnt = {"pc": 0, "ptr": 0, "E": 0, "alt": 0}

            def load_in(bi):
                t0 = bi * 128
                ib = inb[bi % 2]
                for q, src in enumerate((self.rw_r, self.rw_k, self.rw_v[j], self.rw_a[0], self.rw_a[1])):
                    P.dma("sp", ib.t[:, q, :], src[t0:t0 + 128, :], writes=[ib.b])

            def alt():
                cnt["alt"] += 1
                return "dve" if cnt["alt"] % 2 == 0 else "pool"

            load_in(0)
            for bi in range(nblk):
                t0 = bi * 128
                if bi + 1 < nblk:
                    load_in(bi + 1)
                ib = inb[bi % 2]
                r_, k_, v_ = ib[:, 0, :], ib[:, 1, :], ib[:, 2, :]
                for c in range(2):
                    P.dma("sp", swt[c].t[:], self.rw_sw[c][t0:t0 + 128, :], writes=[swt[c].b])
                if vres:
                    P.dma("sp", gv.t[:, 0, :], self.rw_gate[t0:t0 + 128, :], writes=[gv.b])
                    P.dma("sp", gv.t[:, 1, :], self.rw_v[0][t0:t0 + 128, :], writes=[gv.b])
                    P.tt("pool", tb_[:], gv[:, 1, :], v_, ALU.subtract)
                    P.tt("pool", tb_[:], tb_[:], gv[:, 0, :], ALU.mult)
                    P.tt("pool", vpt[:], v_, tb_[:], ALU.add)
                    P.dma("sp", self.vp[t0:t0 + 128, :], vpt.t[:], reads=[vpt.b])
                s_ = sm[bi % 2]
                P.tt("pool", kkf[:], k_, kkB[:], ALU.mult)
                P.tt("pool", sq[:], kkf[:], kkf[:], ALU.mult)
                P.add("dve", lambda e, o=s_.t[:, 0, :], i=sq.t[:, :].rearrange("p (h n) -> p h n", n=NR): e.tensor_reduce(out=o, in_=i, axis=AX.X, op=ALU.add),
                      reads=[sq.b], writes=[s_.b])
                P.act(s_[:, 1, :], s_[:, 0, :], AF.Sqrt)
                P.ts("dve", s_[:, 1, :], s_[:, 1, :], 1e-12, ALU.max)
                P.recip(s_[:, 2, :], s_[:, 1, :])
                P.tt("pool", kk.v(kk.t[:, :].rearrange("p (h n) -> p h n", n=NR)), kkf.v(kkf.t[:, :].rearrange("p (h n) -> p h n", n=NR)),
                     s_.v(s_.t[:, 2, :].unsqueeze(2).to_broadcast([128, NHR, NR])), ALU.mult)
                P.tt("pool", kkac[:], k_, kaB[:], ALU.mult)
                P.tt("pool", kmk[:], k_, kkac[:], ALU.subtract)
                for c in range(2):
                    a_ = ib[:, 3 + c, :]
                    P.tt(alt(), tb_[:], a_, kkac[:], ALU.mult)
                    P.tt(alt(), kd[:], tb_[:], kmk[:], ALU.add)
                    P.tt(alt(), kka[:], kk[:], a_, ALU.mult)
                    if c == 0:
                        P.tt(alt(), ub[:], kd[:], rkB[0][:], ALU.mult)
                    else:
                        P.tt(alt(), sq[:], kd[:], rkB[1][:], ALU.mult)
                        P.tt(alt(), ub[:], ub[:], sq[:], ALU.add)
                        P.tt(alt(), ub[:], ub[:], r_, ALU.mult)
                        P.add("dve", lambda e, o=s_.t[:, 3, :], i=ub.t[:, :].rearrange("p (h n) -> p h n", n=NR): e.tensor_reduce(out=o, in_=i, axis=AX.X, op=ALU.add),
                              reads=[ub.b], writes=[s_.b])
                        P.dma("sp", self.bon[t0:t0 + 128, :], s_.t[:, 3, :], reads=[s_.b])
                    mats = (0, 2, 3) if c == 0 else (1, 3, 2)
                    sw = swt[c]
                    for cg in range(4):
                        cs_ = slice(cg * 512, (cg + 1) * 512)
                        pcs = []
                        for mi in mats:
                            p = pc[cnt["pc"] % 5]
                            cnt["pc"] += 1
                            P.mm(p[:], tri[:, mi, :], sw[:, cs_])
                            pcs.append(p)
                        E = Eb[cnt["E"] % 2]
                        cnt["E"] += 1
                        P.act(E[:, 0, :], pcs[0][:], AF.Exp, scale=-CDEC)
                        P.act(E[:, 1, :], pcs[1][:], AF.Exp, scale=-CDEC)
                        P.act(E[:, 2, :], pcs[0][:], AF.Exp, scale=CDEC)
                        P.act(E[:, 3, :], pcs[2][:], AF.Exp, scale=-CDEC)
                        P.tt(alt(), prod[0][:, cs_], kk[:, cs_], E[:, 1, :], ALU.mult)
                        P.tt(alt(), prod[1][:, cs_], ib[:, 0, cs_], E[:, 0, :], ALU.mult)
                        P.tt(alt(), prod[2][:, cs_], kka[:, cs_], E[:, 2, :], ALU.mult)
                        P.tt(alt(), prod[3][:, cs_], kd[:, cs_], E[:, 2, :], ALU.mult)
                        P.tt(alt(), prod[4][:, cs_], kka[:, cs_], E[:, 3, :], ALU.mult)
                        P.tt(alt(), prod[5][:, cs_], kd[:, cs_], E[:, 3, :], ALU.mult)
                    for p_ in range(16):
                        P.mm(pg[:, p_, :], sw[:, p_ * 128:(p_ + 1) * 128], ind[:, :])
                    gt_ = gall[(bi * 2 + c) % 4]
                    P.act(gt_[:], pg[:], AF.Exp, scale=-CDEC)
                    P.dma("act", self.gam[c][:, :, bi * 2:bi * 2 + 2], gt_.t[:], reads=[gt_.b])
                    for q in range(4):
                        for half in range(2):
                            pt = ptr[cnt["ptr"] % 2]
                            cnt["ptr"] += 1
                            for pp in range(8):
                                p_ = half * 8 + pp
                                P.transpose(pt[:, pp * 128:(pp + 1) * 128], prod[q][:, p_ * 128:(p_ + 1) * 128], ident)
                            src = pt.v(pt.t[:, :].rearrange("p (c t) -> p c t", t=128))
                            dst = fts[:, q, half * 8:(half + 1) * 8, :]
                            if cnt["ptr"] % 2 == 0:
                                P.copy("act", dst, src)
                            else:
                                P.copy("dve", dst, src)
                    P.dma("sp", self.ft[c][:, :, :, t0:t0 + 128].rearrange("q p c t -> c q p t"), fts.t[:], reads=[fts.b])
                    P.dma("sp", self.ah[c][t0:t0 + 128, :], prod[4].t[:], reads=[prod[4].b])
                    P.dma("sp", self.kh[c][t0:t0 + 128, :], prod[5].t[:], reads=[prod[5].b])
            P.flush()
    def phase_rwkv_chain(self):
        P = self.P
        S = self.S
        nblk = S // 128
        with contextlib.ExitStack() as es:
            ftl = [T(P, es, "ftl%d" % i, [128, 4, 16, 128], BF16) for i in range(2)]
            amt = [T(P, es, "amt%d" % i, [128, 16, 512], BF16) for i in range(2)]
            Q0 = [T(P, es, "Q0_%d" % i, [128, 32, 64], BF16) for i in range(2)]
            G0 = [T(P, es, "G0_%d" % i, [128, 32, 64], BF16) for i in range(2)]
            N0 = [T(P, es, "N0_%d" % i, [128, 32, 64], BF16) for i in range(2)]
            Pst = [T(P, es, "Pst%d" % i, [128, 32, 64], BF16) for i in range(2)]
            Qst = [T(P, es, "Qst%d" % i, [128, 32, 64], BF16) for i in range(2)]
            Pbd = [T(P, es, "Pbd%d" % i, [128, 32, 128], BF16) for i in range(2)]
            Qbd = [T(P, es, "Qbd%d" % i, [128, 32, 128], BF16) for i in range(2)]
            for t_ in Pbd + Qbd:
                P.memset("pool", t_[:], 0.0)
            Gm = [T(P, es, "Gm%d" % i, [128, 32, 64], BF16) for i in range(2)]
            pb = [T(P, es, "pb%d" % i, [128, 512], F32, psum=True) for i in range(8)]
            cnt = {"pa": 0, "b": 0, "ev": 0}
            seq = [(c, bi) for bi in range(nblk) for c in range(2)]

            def load(i):
                c, bi = seq[i]
                P.dma("sp", ftl[i % 2].t[:], self.ft[c][:, :, :, bi * 128:(bi + 1) * 128].rearrange("q p c t -> c q p t"), writes=[ftl[i % 2].b])

            import os as _os
            def evcopy(dst, src):
                cnt["ev"] += 1
                ev_ = _os.environ.get('EVENG')
                P.copy("dve", dst, src)

            load(0)
            for i, (c, bi) in enumerate(seq):
                if i + 1 < len(seq):
                    load(i + 1)
                f = ftl[i % 2]
                am_ = amt[i % 2]
                q0 = Q0[i % 2]
                g0 = G0[i % 2]
                mk = CB_MF if c == 0 else CB_MB
                lk = CB_LF if c == 0 else CB_LB
                for p2 in range(8):
                    base = (cnt["pa"] % 4) * 2
                    cnt["pa"] += 1
                    for hh in range(2):
                        ps_ = pb[base + hh]
                        kr = slice(hh * 64, (hh + 1) * 64)
                        for pl in range(2):
                            p = p2 * 2 + pl
                            for cc in range(2):
                                tk = slice(cc * 64, (cc + 1) * 64)
                                P.mm(ps_[tk, pl * 256:pl * 256 + 128], f[kr, 2, p, tk], f[kr, 0:2, p, tk])
                                P.mm(ps_[tk, pl * 256 + 128:pl * 256 + 256], f[kr, 3, p, tk], f[kr, 0:2, p, tk])
                        P.tt("dve", am_[:, p2 * 2:p2 * 2 + 2, hh * 256:(hh + 1) * 256], ps_.v(ps_.t[:, :].rearrange("p (a m) -> p a m", a=2)),
                             View(self.cb.b, self.cb.t[:, mk:mk + 256].unsqueeze(1).to_broadcast([128, 2, 256])), ALU.mult)
                q0v = q0.t[:, :, :].rearrange("p (a h) n -> p a h n", h=2)
                import os as _os
                CHS = int(_os.environ.get("CHSTOP", "9"))
                if CHS < 2:
                    continue
                for grp in range(2):
                    base = (cnt["pa"] % 4) * 2
                    cnt["pa"] += 1
                    for hh in range(2):
                        ps_ = pb[base + hh]
                        kr = slice(hh * 64, (hh + 1) * 64)
                        for pl in range(8):
                            p = grp * 8 + pl
                            for cc in range(2):
                                tk = slice(cc * 64, (cc + 1) * 64)
                                P.mm(ps_[tk, pl * 64:(pl + 1) * 64], f[kr, 0, p, tk], f[kr, 2, p, tk])
                        P.tt("dve", View(q0.b, q0v[:, grp * 8:(grp + 1) * 8, hh, :]), ps_.v(ps_.t[:, :].rearrange("p (a m) -> p a m", m=64)),
                             View(self.cb.b, self.cb.t[:, lk:lk + 64].unsqueeze(1).to_broadcast([128, 8, 64])), ALU.mult)
                P.dma("act", self.am[c][bi], am_.t[:], reads=[am_.b])
                if CHS < 3:
                    continue
                nview = am_.t[:, :, :].rearrange("p a (h q n) -> p a h q n", h=2, q=4)[:, :, :, 0, :]
                P.tt("dve", g0.v(g0.t[:, :, :].rearrange("p (a h) n -> p a h n", h=2)),
                     View(self.cb.b, self.cb.t[:, CB_I64:CB_I64 + 64].unsqueeze(1).unsqueeze(1).to_broadcast([128, 16, 2, 64])),
                     View(am_.b, nview), ALU.subtract)

                nst = N0[i % 2]
                nv4 = am_.t[:, :, :].rearrange("p a (h q n) -> p a h q n", h=2, q=4)[:, :, :, 0, :]
                P.copy("act", nst.v(nst.t[:, :, :].rearrange("p (a h) n -> p a h n", h=2)), View(am_.b, nv4))
                for cc in range(2):
                    rows = slice(cc * 64, (cc + 1) * 64)
                    eng_ = "act" if cc == 0 else "dve"
                    P.copy(eng_, Pbd[1][rows, :, cc * 64:(cc + 1) * 64], nst[rows, :, :])
                    P.copy(eng_, Qbd[1][rows, :, cc * 64:(cc + 1) * 64], q0[rows, :, :])
                for bg in range(2):
                    for st in range(6):
                        pst = nst if st == 0 else Pst[(st - 1) % 2]
                        qst = q0 if st == 0 else Qst[(st - 1) % 2]
                        pbd = Pbd[(st - 1) % 2]
                        qbd = Qbd[(st - 1) % 2]
                        gsrc = g0 if st == 1 else Gm[(st - 2) % 2]
                        for bt in (bg * 2, bg * 2 + 1):
                            base = (bt % 2) * 3
                            pP, pQ, pG = pb[base], pb[base + 1], pb[base + 2]
                            bs = slice(bt * 8, bt * 8 + 8)
                            for hl in range(8):
                                hd = bt * 8 + hl
                                cs_ = slice(hl * 64, (hl + 1) * 64)
                                if st <= 3:
                                    P.mm(pP[:, cs_], qbd[:, hd, :], pst[:, hd, :])
                                if st <= 4:
                                    P.mm(pQ[:, cs_], pbd[:, hd, :], qst[:, hd, :])
                                if st >= 1:
                                    P.mm(pG[:, cs_], qbd[:, hd, :], gsrc[:, hd, :])
                            v3 = lambda t_, r_=slice(0, 128): View(t_.b, t_.t[r_, :].rearrange("p (h m) -> p h m", m=64))
                            if st <= 3:
                                P.copy("act", Pst[st % 2][:, bs, :], v3(pP))
                                P.copy("act", Pbd[st % 2][0:64, bs, 0:64], v3(pP, slice(0, 64)))
                                P.copy("dve", Pbd[st % 2][64:128, bs, 64:128], v3(pP, slice(64, 128)))
                            if st <= 4:
                                if st <= 3:
                                    P.copy("act", Qst[st % 2][:, bs, :], v3(pQ))
                                P.copy("act", Qbd[st % 2][0:64, bs, 0:64], v3(pQ, slice(0, 64)))
                                P.copy("dve", Qbd[st % 2][64:128, bs, 64:128], v3(pQ, slice(64, 128)))
                            if st >= 1:
                                P.tt("dve", Gm[(st - 1) % 2][:, bs, :], v3(pG), gsrc[:, bs, :], ALU.add)
                P.dma("act", self.ttm[c][bi], Gm[0].t[:], reads=[Gm[0].b])
            P.flush()

    def phase_rwkv_scan(self):
        P = self.P
        S = self.S
        nblk = S // 128
        with contextlib.ExitStack() as es:
            gl = [[T(P, es, "gl%d_%d" % (c, i), [128, 16, 2], F32) for i in range(2)] for c in range(2)]
            ST = [T(P, es, "ST%d" % c, [128, 16, 64], F32) for c in range(2)]
            STb = [[T(P, es, "STb%d_%d" % (c, i), [128, 16, 2, 64], BF16) for i in range(2)] for c in range(2)]
            for c in range(2):
                P.memset("pool", ST[c][:], 0.0)
                P.memset("pool", STb[c][0][:], 0.0)
                P.memset("pool", STb[c][1][:], 0.0)
            fK = [[T(P, es, "fK%d_%d" % (c, i), [128, 2, 16, 128], BF16) for i in range(2)] for c in range(2)]
            amt = [[T(P, es, "samt%d_%d" % (c, i), [128, 16, 2, 192], BF16) for i in range(2)] for c in range(2)]
            ttl = [[T(P, es, "ttl%d_%d" % (c, i), [128, 32, 64], BF16) for i in range(2)] for c in range(2)]
            akv = [[T(P, es, "akv%d_%d" % (c, i), [128, 3, D], BF16) for i in range(2)] for c in range(2)]
            yst = [T(P, es, "yst%d" % c, [128, D], F32) for c in range(2)]
            NB = 4
            Bt = [T(P, es, "Bt%d" % i, [128, 256], BF16) for i in range(NB)]
            Ut = [T(P, es, "Ut%d" % i, [128, 256], BF16) for i in range(NB)]
            tmpS = [T(P, es, "tmpS%d" % i, [128, 2, 64], F32) for i in range(NB)]
            pBU = [T(P, es, "pBU%d" % i, [128, 512], F32, psum=True) for i in range(4)]
            pMY = [T(P, es, "pMY%d" % i, [128, 512], F32, psum=True) for i in range(4)]

            def load(c, step):
                b = step if c == 0 else nblk - 1 - step
                t0 = b * 128
                sl = step % 2
                P.dma("sp", fK[c][sl].t[:], self.ft[c][0:2, :, :, t0:t0 + 128].rearrange("q p c t -> c q p t"), writes=[fK[c][sl].b])
                P.dma("sp", ttl[c][sl].t[:], self.ttm[c][b], writes=[ttl[c][sl].b])
                P.dma("sp", akv[c][sl].t[:, 0, :], self.ah[c][t0:t0 + 128, :], writes=[akv[c][sl].b])
                P.dma("sp", akv[c][sl].t[:, 1, :], self.kh[c][t0:t0 + 128, :], writes=[akv[c][sl].b])
                P.dma("sp", akv[c][sl].t[:, 2, :], self.vp_src[t0:t0 + 128, :], writes=[akv[c][sl].b])
                P.dma("sp", gl[c][sl].t[:], self.gam[c][:, :, b * 2:b * 2 + 2], writes=[gl[c][sl].b])
                P.dma("sp", amt[c][sl].t[:, :, :, :].rearrange("p a h (q n) -> p a h q n", q=3),
                      self.am[c][b].rearrange("p a (h q n) -> p a h q n", h=2, q=4)[:, :, :, 1:4, :], writes=[amt[c][sl].b])

            for c in range(2):
                load(c, 0)
            kctr = [0]
            for step in range(nblk):
                for c in range(2):
                    if step + 1 < nblk:
                        load(c, step + 1)
                for ci_ in range(2):
                    combos = []
                    for e8 in range(8):
                        for c in range(2):
                            b = step if c == 0 else nblk - 1 - step
                            cc = ci_ if c == 0 else 1 - ci_
                            combos.append((c, b, cc, b * 2 + cc, e8))
                    nco = len(combos)
                    ctx = {}

                    def stageA(k):
                        c, b, cc, gch, e8 = combos[k]
                        sl = step % 2
                        R_ = slice(cc * 64, (cc + 1) * 64)
                        kk_ = kctr[0]
                        kctr[0] += 1
                        pbu = pBU[kk_ % 4]
                        pmy = pMY[kk_ % 4]
                        bt_ = Bt[kk_ % NB]
                        ut_ = Ut[kk_ % NB]
                        ctx[k] = (pbu, pmy, bt_, ut_, kk_)
                        stb = STb[c][(step * 2 + ci_) % 2]
                        for pl in range(2):
                            p = e8 * 2 + pl
                            P.mm(pbu[R_, pl * 128:(pl + 1) * 128], fK[c][sl][:, 0, p, R_], stb[:, p, :, :], start=True, stop=False)
                            for hh in range(2):
                                hd = p * 2 + hh
                                i = pl * 2 + hh
                                P.mm(pbu[R_, i * 64:(i + 1) * 64], amt[c][sl][R_, p, hh, 64:128],
                                     akv[c][sl][R_, 2, hd * 64:(hd + 1) * 64], start=False, stop=(hh == 1))
                        P.copy("dve", bt_[R_, :], pbu[R_, 0:256])

                    def stageA2(k):
                        c, b, cc, gch, e8 = combos[k]
                        sl = step % 2
                        R_ = slice(cc * 64, (cc + 1) * 64)
                        pbu, pmy, bt_, ut_, kk_ = ctx[k]
                        for i in range(4):
                            hd = e8 * 4 + i
                            P.mm(pbu[R_, 256 + i * 64:256 + (i + 1) * 64], ttl[c][sl][R_, hd, :], bt_[R_, i * 64:(i + 1) * 64])
                        P.ts("dve", ut_[R_, :], pbu[R_, 256:512], -1.0, ALU.mult)

                    def stageC(k):
                        c, b, cc, gch, e8 = combos[k]
                        sl = step % 2
                        R_ = slice(cc * 64, (cc + 1) * 64)
                        pbu, pmy, bt_, ut_, kk_ = ctx[k]
                        stb = STb[c][(step * 2 + ci_) % 2]
                        stn = STb[c][(step * 2 + ci_ + 1) % 2]
                        a_ = akv[c][sl]
                        for i in range(4):
                            hd = e8 * 4 + i
                            p, hh = hd // 2, hd % 2
                            kr = slice(hh * 64, (hh + 1) * 64)
                            hc = slice(hd * 64, (hd + 1) * 64)
                            o = pmy[kr, (i // 2) * 64:(i // 2 + 1) * 64]
                            P.mm(o, a_[R_, 1, hc], a_[R_, 2, hc], start=True, stop=False)
                            P.mm(o, a_[R_, 0, hc], ut_[R_, i * 64:(i + 1) * 64], start=False, stop=True)
                        for pl in range(2):
                            p = e8 * 2 + pl
                            P.mm(pmy[R_, 256 + pl * 128:256 + (pl + 1) * 128], fK[c][sl][:, 1, p, R_], stb[:, p, :, :], start=True, stop=False)
                            for hh in range(2):
                                hd = p * 2 + hh
                                i = pl * 2 + hh
                                hc = slice(hd * 64, (hd + 1) * 64)
                                o = pmy[R_, 256 + i * 64:256 + (i + 1) * 64]
                                P.mm(o, amt[c][sl][R_, p, hh, 0:64], ut_[R_, i * 64:(i + 1) * 64], start=False, stop=False)
                                P.mm(o, amt[c][sl][R_, p, hh, 128:192], a_[R_, 2, hc], start=False, stop=(hh == 1))
                        P.copy("dve", yst[c][R_, e8 * 256:(e8 + 1) * 256], pmy[R_, 256:512])
                        ps2 = slice(e8 * 2, e8 * 2 + 2)
                        tm = tmpS[kk_ % NB]
                        P.tt("dve", tm[:], ST[c][:, ps2, :], View(gl[c][sl].b, gl[c][sl].t[:, ps2, cc:cc + 1].broadcast_to([128, 2, 64])), ALU.mult)
                        P.tt("dve", ST[c][:, ps2, :], tm[:], pmy.v(pmy.t[:, 0:128].rearrange("p (a v) -> p a v", v=64)), ALU.add)
                        for hh in range(2):
                            kr = slice(hh * 64, (hh + 1) * 64)
                            P.copy("pool", stn[kr, ps2, hh, :], ST[c][kr, ps2, :])

                    for k in range(nco + 2):
                        if k < nco:
                            stageA(k)
                        if 0 <= k - 1 < nco:
                            stageA2(k - 1)
                        if 0 <= k - 2 < nco:
                            stageC(k - 2)
                for c in range(2):
                    b = step if c == 0 else nblk - 1 - step
                    P.dma("act", self.yd[c][b * 128:(b + 1) * 128, :], yst[c].t[:], reads=[yst[c].b])
            P.flush()

    def phase_rwkv_post(self, j):
        P = self.P
        S = self.S
        R = self.in_rw
        nblk = S // 128
        with contextlib.ExitStack() as es:
            gG = self.load_bcast(es, "gnG", R["rwkv_gn_g"][j:j + 1, :])
            gB = self.load_bcast(es, "gnB", R["rwkv_gn_b"][j:j + 1, :])
            yin = [T(P, es, "yin%d" % i, [128, 2, D], F32) for i in range(2)]
            vz = [T(P, es, "vz%d" % i, [128, 2, D], BF16) for i in range(2)]
            bn = [T(P, es, "bn%d" % i, [128, NHR], F32) for i in range(2)]
            y = T(P, es, "ypost", [128, D], F32)
            sq = T(P, es, "sqpost", [128, D], F32)
            yo = [T(P, es, "yo%d" % i, [128, D], BF16) for i in range(2)]
            sm = [T(P, es, "smp%d" % i, [128, 6, NHR], F32) for i in range(2)]
            tp = [T(P, es, "tpp%d" % i, [128, 1024], BF16, psum=True) for i in range(4)]
            hs = [T(P, es, "hsp%d" % i, [128, NCH, 512], BF16) for i in range(2)]
            ident = self.cbv(CB_IDENT)

            def load(bi):
                t0 = bi * 128
                sl = bi % 2
                P.dma("sp", yin[sl].t[:, 0, :], self.yd[0][t0:t0 + 128, :], writes=[yin[sl].b])
                P.dma("sp", yin[sl].t[:, 1, :], self.yd[1][t0:t0 + 128, :], writes=[yin[sl].b])
                P.dma("sp", vz[sl].t[:, 0, :], self.vp_src[t0:t0 + 128, :], writes=[vz[sl].b])
                P.dma("sp", vz[sl].t[:, 1, :], self.rw_sz[t0:t0 + 128, :], writes=[vz[sl].b])
                P.dma("sp", bn[sl].t[:], self.bon[t0:t0 + 128, :], writes=[bn[sl].b])

            def v3(t_, ap=None):
                a = t_.t[:, :] if ap is None else ap
                return View(t_.b, a.rearrange("p (h n) -> p h n", n=NR))

            def bc(t_, ap):
                return View(t_.b, ap.unsqueeze(2).to_broadcast([128, NHR, NR]))

            load(0)
            k = 0
            for bi in range(nblk):
                if bi + 1 < nblk:
                    load(bi + 1)
                sl = bi % 2
                s_ = sm[sl]
                P.tt("pool", y[:], yin[sl][:, 0, :], yin[sl][:, 1, :], ALU.add)
                P.tt("pool", sq[:], y[:], y[:], ALU.mult)
                P.add("dve", lambda e, o=s_.t[:, 0, :], i=y.t[:, :].rearrange("p (h n) -> p h n", n=NR): e.tensor_reduce(out=o, in_=i, axis=AX.X, op=ALU.add),
                      reads=[y.b], writes=[s_.b])
                P.add("dve", lambda e, o=s_.t[:, 1, :], i=sq.t[:, :].rearrange("p (h n) -> p h n", n=NR): e.tensor_reduce(out=o, in_=i, axis=AX.X, op=ALU.add),
                      reads=[sq.b], writes=[s_.b])
                P.ts("dve", s_[:, 2, :], s_[:, 0, :], 1.0 / NR, ALU.mult)
                P.tt("dve", s_[:, 3, :], s_[:, 2, :], s_[:, 2, :], ALU.mult)
                P.stt(s_[:, 4, :], s_[:, 1, :], 1.0 / NR, s_[:, 3, :], ALU.mult, ALU.subtract)
                P.ts("dve", s_[:, 4, :], s_[:, 4, :], GN_EPS, ALU.add)
                P.act(s_[:, 5, :], s_[:, 4, :], AF.Sqrt)
                P.recip(s_[:, 4, :], s_[:, 5, :])
                P.tt("pool", v3(y), v3(y), bc(s_, s_.t[:, 2, :]), ALU.subtract)
                P.tt("pool", v3(y), v3(y), bc(s_, s_.t[:, 4, :]), ALU.mult)
                P.tt("dve", y[:], y[:], gG[:], ALU.mult)
                P.tt("dve", y[:], y[:], gB[:], ALU.add)
                P.tt("pool", v3(sq), View(vz[sl].b, vz[sl].t[:, 0, :].rearrange("p (h n) -> p h n", n=NR)), bc(bn[sl], bn[sl].t[:, :]), ALU.mult)
                P.tt("pool", y[:], y[:], sq[:], ALU.add)
                o = yo[bi % 2]
                P.tt("dve", o[:], y[:], vz[sl][:, 1, :], ALU.mult)
                u = bi // 4
                sub = bi % 4
                hst = hs[u % 2]
                for half in range(2):
                    pt = tp[k % 4]
                    k += 1
                    for c in range(8):
                        cc = half * 8 + c
                        P.transpose(pt[:, c * 128:(c + 1) * 128], o[:, cc * 128:(cc + 1) * 128], ident)
                    src = pt.v(pt.t[:, :].rearrange("p (c t) -> p c t", t=128))
                    dst = hst[:, half * 8:(half + 1) * 8, sub * 128:(sub + 1) * 128]
                    P.copy("act", dst, src)
                if sub == 3 or bi == nblk - 1:
                    t0 = u * 512
                    n = (sub + 1) * 128
                    P.dma("sp", self.yT.rearrange("(c p) t -> p c t", p=128)[:, :, t0:t0 + n], hst.t[:, :, 0:n], reads=[hst.b])
            P.flush()


S_FULL = 8192
_CACHE = {}


def kernel(**inputs):
    S = S_FULL
    x_prompt = np.asarray(inputs["x_prompt"], dtype=np.float32)
    x_sample = np.asarray(inputs["x_sample"], dtype=np.float32)
    b = Builder(S)
    nc = b.build()
    cb, cf = _const_tables(S)
    xs = []
    valid_lens = []
    xs.append(np.ascontiguousarray(x_prompt[0]))
    valid_lens.append(S)
    for i in range(4):
        xp = np.zeros((S, D), np.float32)
        xp[:4096] = x_sample[i]
        xs.append(xp)
        valid_lens.append(4096)
    for i in range(3):
        xs.append(np.zeros((S, D), np.float32))
        valid_lens.append(S)
    shared = {}
    for k, v in inputs.items():
        if k in ("x_prompt", "x_sample"):
            continue
        a = np.ascontiguousarray(np.asarray(v, dtype=np.float32))
        if k == "rwkv_r_k":
            a = a.reshape(2, 2, D)
        shared[k] = a
    in_maps = []
    for c in range(8):
        m = dict(shared)
        m["x"] = xs[c]
        m["const_bf"] = cb
        m["const_f32"] = cf
        m["valid"] = _valid_tables(S, valid_lens[c])
        m["tokmask"] = _tokmask(S, valid_lens[c])
        in_maps.append(m)
    res = run_bass_kernel_spmd(nc, in_maps, core_ids=list(range(8)))
    y_prompt = np.asarray(res.results[0]["y"], dtype=np.float32)[None]
    y_sample = np.stack([np.asarray(res.results[1 + i]["y"], dtype=np.float32)[:4096] for i in range(4)], axis=0)
    return (y_prompt, y_sample)
```
